# Optimizing a Trainium2 kernel written in Bass

```python
import math
import jax
import jax.numpy as jnp
from jax import lax
import numpy as np


D_MODEL = 1024
BATCH = 8
SEQ = 2048
DEPTH = 4

GRID_W = 64
NORM_EPS = 1e-6
LN_EPS = 1e-5

SSD_HEADS = 8
SSD_HEAD_DIM = 64
SSD_INNER = SSD_HEADS * SSD_HEAD_DIM
SSD_GROUPS = 2
SSD_STATE = 128
SSD_CONV_K = 5
SSD_CHUNK = 128
SSD_XBC = SSD_INNER + 2 * SSD_GROUPS * SSD_STATE

ATT_Q_HEADS = 8
ATT_KV_HEADS = 2
ATT_HEAD_DIM = 64
ATT_INNER = ATT_Q_HEADS * ATT_HEAD_DIM
ATT_BLOCK = 128
ROPE_THETA = 10000.0

POOL_WINDOWS = (2, 4, 8, 16)
POOL_GROUP = 128
POOL_INNER = POOL_GROUP * len(POOL_WINDOWS)

HGRN_HEADS = 4
HGRN_HEAD_DIM = 128
HGRN_INNER = HGRN_HEADS * HGRN_HEAD_DIM
HGRN_CHUNK = 64

N_EXPERTS = 16
EXPERT_FF = 2048
CAPACITY_FACTOR = 2

DEEPNORM_ALPHA = (2 * DEPTH) ** 0.25
DEEPNORM_BETA = (8 * DEPTH) ** -0.25

N_EVEN = (DEPTH + 1) // 2
N_ODD = DEPTH // 2

AB_WIDTHS = (SSD_INNER, SSD_XBC, 2 * SSD_HEADS, ATT_INNER, ATT_KV_HEADS * ATT_HEAD_DIM, ATT_KV_HEADS * ATT_HEAD_DIM)
CD_WIDTHS = (POOL_INNER, HGRN_INNER, HGRN_INNER, HGRN_INNER, HGRN_INNER, HGRN_INNER)
IN_AB = sum(AB_WIDTHS)
IN_CD = sum(CD_WIDTHS)
OUT_AB = SSD_INNER + ATT_INNER
OUT_CD = POOL_INNER + HGRN_INNER

kernel_name = 'hybrid_ssd_gqa_pool_hgrn2_ecmoe'


def split_cols(t, widths):
    offs = []
    acc = 0
    for w in widths[:-1]:
        acc += w
        offs.append(acc)
    return jnp.split(t, offs, axis=-1)


def rms_norm(x, w):
    xf = x.astype(jnp.float32)
    y = xf * lax.rsqrt(jnp.mean(xf * xf, axis=-1, keepdims=True) + NORM_EPS)
    return (y * w.astype(jnp.float32)).astype(x.dtype)


def layer_norm(x, g, b):
    xf = x.astype(jnp.float32)
    mu = jnp.mean(xf, axis=-1, keepdims=True)
    xc = xf - mu
    var = jnp.mean(xc * xc, axis=-1, keepdims=True)
    y = xc * lax.rsqrt(var + LN_EPS) * g.astype(jnp.float32) + b.astype(jnp.float32)
    return y.astype(x.dtype)


def flip_seq(t):
    return jnp.flip(t, axis=1)


def centred_depthwise_conv(u, w, b):
    k, c = w.shape
    pad = (k - 1) // 2
    y = lax.conv_general_dilated(u, w[:, None, :].astype(u.dtype), window_strides=(1,),
                                 padding=[(pad, k - 1 - pad)],
                                 dimension_numbers=('NWC', 'WIO', 'NWC'),
                                 feature_group_count=c)
    return y + b


def ssd_chunked(x, dt, a_neg, bm, cm):
    bsz, l, h, p = x.shape
    g, n = bm.shape[2], bm.shape[3]
    e = h // g
    q = SSD_CHUNK
    c = l // q
    f32 = jnp.float32
    la = (dt.astype(f32) * a_neg.astype(f32)).reshape(bsz, c, q, g, e)
    xs = (x.astype(f32) * dt.astype(f32)[..., None]).reshape(bsz, c, q, g, e, p)
    bs = bm.astype(f32).reshape(bsz, c, q, g, n)
    cs = cm.astype(f32).reshape(bsz, c, q, g, n)
    acum = jnp.cumsum(la, axis=2)
    lower = jnp.tril(jnp.ones((q, q), dtype=bool))[None, None, :, :, None, None]
    seg = acum[:, :, :, None] - acum[:, :, None, :]
    lmat = jnp.exp(jnp.where(lower, seg, -jnp.inf))
    cb = jnp.einsum('bcign,bcjgn->bcijg', cs, bs)
    y_diag = jnp.einsum('bcijge,bcjgep->bcigep', cb[..., None] * lmat, xs)
    decay_to_end = jnp.exp(acum[:, :, -1:] - acum)
    states = jnp.einsum('bcjgn,bcjge,bcjgep->bcgepn', bs, decay_to_end, xs)
    a_tot = acum[:, :, -1]
    t_inc = jnp.cumsum(a_tot, axis=1)
    t_exc = t_inc - a_tot
    before = jnp.tril(jnp.ones((c, c), dtype=bool), k=-1)[None, :, :, None, None]
    chunk_decay = jnp.exp(jnp.where(before, t_exc[:, :, None] - t_inc[:, None, :], -jnp.inf))
    s_in = jnp.einsum('bzyge,bygepn->bzgepn', chunk_decay, states)
    y_off = jnp.einsum('bcign,bcgepn,bcige->bcigep', cs, s_in, jnp.exp(acum))
    return (y_diag + y_off).reshape(bsz, l, h, p).astype(x.dtype)


def ssd_mixer(z, xbc, dt_raw, conv_w, conv_b, dt_bias, a_log, d_skip, norm_w):
    bsz, l, _ = xbc.shape
    xbc = jax.nn.silu(centred_depthwise_conv(xbc, conv_w, conv_b))
    xs, bm, cm = jnp.split(xbc, [SSD_INNER, SSD_INNER + SSD_GROUPS * SSD_STATE], axis=-1)
    xs = xs.reshape(bsz, l, SSD_HEADS, SSD_HEAD_DIM)
    bm = bm.reshape(bsz, l, SSD_GROUPS, SSD_STATE)
    cm = cm.reshape(bsz, l, SSD_GROUPS, SSD_STATE)
    dt = jax.nn.softplus((dt_raw + dt_bias).astype(jnp.float32))
    a_neg = -jnp.exp(a_log.astype(jnp.float32))
    y_f = ssd_chunked(xs, dt[:, :, 0], a_neg[0], bm, cm)
    y_b = flip_seq(ssd_chunked(flip_seq(xs), flip_seq(dt[:, :, 1]), a_neg[1], flip_seq(bm), flip_seq(cm)))
    y = (y_f + y_b + xs * d_skip[:, None]).reshape(bsz, l, SSD_INNER)
    return rms_norm(y * jax.nn.silu(z), norm_w)


def axial_rope_tables(seq):
    rows = seq // GRID_W
    row = jnp.repeat(jnp.arange(rows), GRID_W).astype(jnp.float32)
    col = jnp.tile(jnp.arange(GRID_W), rows).astype(jnp.float32)
    axis_dims = ATT_HEAD_DIM // 2
    freqs = ROPE_THETA ** (-jnp.arange(0, axis_dims, 2, dtype=jnp.float32) / axis_dims)
    ang = jnp.concatenate([row[:, None] * freqs, col[:, None] * freqs], axis=-1)
    return jnp.cos(ang), jnp.sin(ang)


def apply_rope(x, cos, sin):
    x2 = x.astype(jnp.float32).reshape(x.shape[:-1] + (x.shape[-1] // 2, 2))
    xe, xo = x2[..., 0], x2[..., 1]
    c = cos[None, :, None, :]
    s = sin[None, :, None, :]
    out = jnp.stack([xe * c - xo * s, xe * s + xo * c], axis=-1)
    return out.reshape(x.shape).astype(x.dtype)


def blocked_gqa(q, k, v):
    bsz, l, hq, d = q.shape
    hkv = k.shape[2]
    rep = hq // hkv
    nb = l // ATT_BLOCK
    scale = d ** -0.5
    qb = q.reshape(bsz, nb, ATT_BLOCK, hkv, rep, d).transpose(1, 0, 2, 3, 4, 5)

    def one_block(qblk):
        s = jnp.einsum('bqgrd,bkgd->bgrqk', qblk, k).astype(jnp.float32) * scale
        p = jax.nn.softmax(s, axis=-1).astype(v.dtype)
        return jnp.einsum('bgrqk,bkgd->bqgrd', p, v)

    o = lax.map(one_block, qb)
    return o.transpose(1, 0, 2, 3, 4, 5).reshape(bsz, l, hq * d)


def ab_mixer(x, w_in, conv_w, conv_b, dt_bias, a_log, d_skip, norm_w, q_norm_w, k_norm_w, w_out, cos, sin):
    bsz, l, _ = x.shape
    proj = jnp.einsum('bld,dc->blc', x, w_in)
    z, xbc, dt_raw, q, k, v = split_cols(proj, AB_WIDTHS)
    y_a = ssd_mixer(z, xbc, dt_raw.reshape(bsz, l, 2, SSD_HEADS), conv_w, conv_b, dt_bias, a_log, d_skip, norm_w)
    q = apply_rope(rms_norm(q.reshape(bsz, l, ATT_Q_HEADS, ATT_HEAD_DIM), q_norm_w), cos, sin)
    k = apply_rope(rms_norm(k.reshape(bsz, l, ATT_KV_HEADS, ATT_HEAD_DIM), k_norm_w), cos, sin)
    v = v.reshape(bsz, l, ATT_KV_HEADS, ATT_HEAD_DIM)
    y_b = blocked_gqa(q, k, v)
    return jnp.einsum('blc,cd->bld', jnp.concatenate([y_a, y_b], axis=-1), w_out)


def multiscale_pool(u, pool_w, pool_scale):
    bsz, l, _ = u.shape
    uf = u.astype(jnp.float32)
    csum = jnp.concatenate([jnp.zeros((bsz, 1, POOL_INNER), jnp.float32), jnp.cumsum(uf, axis=1)], axis=1)
    t = jnp.arange(l)
    outs = []
    for gi, w in enumerate(POOL_WINDOWS):
        lo = jnp.clip(t - w // 2, 0, l)
        hi = jnp.clip(t - w // 2 + w, 0, l)
        cg = csum[:, :, gi * POOL_GROUP:(gi + 1) * POOL_GROUP]
        mean = (cg[:, hi] - cg[:, lo]) / (hi - lo).astype(jnp.float32)[None, :, None]
        outs.append(mean - uf[:, :, gi * POOL_GROUP:(gi + 1) * POOL_GROUP])
    pooled = jnp.stack(outs, axis=2).astype(u.dtype)
    mixed = jnp.einsum('blgc,gcd->blgd', pooled, pool_w).reshape(bsz, l, POOL_INNER)
    return mixed * pool_scale


def hgrn2_scan(q, logf, k, v):
    bsz, l, h, dk = q.shape
    dv = v.shape[-1]
    cq = HGRN_CHUNK
    nc = l // cq

    def to_chunks(t):
        return t.reshape(bsz, nc, cq, h, t.shape[-1]).transpose(1, 0, 3, 2, 4)

    lower = jnp.tril(jnp.ones((cq, cq), dtype=bool))[:, :, None]

    def step(state, inp):
        qc, lfc, kc, vc = inp
        bcum = jnp.cumsum(lfc, axis=2)
        diff = bcum[:, :, :, None, :] - bcum[:, :, None, :, :]
        decay = jnp.exp(jnp.where(lower, diff, -jnp.inf))
        scores = jnp.einsum('bhic,bhjc,bhijc->bhij', qc, kc, decay)
        o = jnp.einsum('bhij,bhjv->bhiv', scores, vc) + jnp.einsum('bhic,bhcv->bhiv', qc * jnp.exp(bcum), state)
        last = bcum[:, :, -1]
        k_to_end = kc * jnp.exp(last[:, :, None, :] - bcum)
        state = jnp.exp(last)[..., None] * state + jnp.einsum('bhjc,bhjv->bhcv', k_to_end, vc)
        return state, o

    s0 = jnp.zeros((bsz, h, dk, dv), jnp.float32)
    _, o = lax.scan(step, s0, (to_chunks(q), to_chunks(logf), to_chunks(k), to_chunks(v)))
    return o.transpose(1, 0, 3, 2, 4).reshape(bsz, l, h, dv)


def hgrn2_mixer(q, f_fwd_raw, f_bwd_raw, i_in, g, lb, norm_w):
    bsz, l, _ = q.shape
    heads = lambda t: t.astype(jnp.float32).reshape(bsz, l, HGRN_HEADS, HGRN_HEAD_DIM)
    lbh = lb.reshape(HGRN_HEADS, HGRN_HEAD_DIM)

    def gate(raw):
        raw = heads(raw)
        logf = jnp.logaddexp(jnp.log(lbh), jnp.log1p(-lbh) + jax.nn.log_sigmoid(raw))
        return logf, (1.0 - lbh) * jax.nn.sigmoid(-raw)

    qh, vh = heads(q), heads(i_in)
    lf_f, k_f = gate(f_fwd_raw)
    lf_b, k_b = gate(f_bwd_raw)
    o_f = hgrn2_scan(qh, lf_f, k_f, vh)
    o_b = flip_seq(hgrn2_scan(flip_seq(qh), flip_seq(lf_b), flip_seq(k_b), flip_seq(vh)))
    o = rms_norm(o_f + o_b, norm_w.reshape(HGRN_HEADS, HGRN_HEAD_DIM)).reshape(bsz, l, HGRN_INNER)
    return (o * jax.nn.sigmoid(g.astype(jnp.float32))).astype(q.dtype)


def cd_mixer(x, w_in, pool_w, pool_scale, lb, norm_w, w_out):
    proj = jnp.einsum('bld,dc->blc', x, w_in)
    u_pool, q, f_f, f_b, i_in, g = split_cols(proj, CD_WIDTHS)
    y_c = multiscale_pool(u_pool, pool_w, pool_scale)
    y_d = hgrn2_mixer(q, f_f, f_b, i_in, g, lb, norm_w)
    return jnp.einsum('blc,cd->bld', jnp.concatenate([y_c, y_d], axis=-1), w_out)


def expert_choice_moe(x, router_w, w1, w3, w2):
    bsz, l, d = x.shape
    cap = CAPACITY_FACTOR * l // N_EXPERTS
    logits = jnp.einsum('bld,de->ble', x, router_w).astype(jnp.float32)
    aff = jax.nn.softmax(logits, axis=-1)
    gate, idx = lax.top_k(aff.transpose(0, 2, 1), cap)
    xs = jax.vmap(lambda xb, ib: xb[ib])(x, idx)
    hdn = jax.nn.silu(jnp.einsum('becd,edf->becf', xs, w1)) * jnp.einsum('becd,edf->becf', xs, w3)
    y = jnp.einsum('becf,efd->becd', hdn, w2) * gate[..., None].astype(x.dtype)
    return jax.vmap(lambda yb, ib: jnp.zeros((l, d), yb.dtype).at[ib.reshape(-1)].add(yb.reshape(-1, d)))(y, idx)


def setup_inputs(seed: int = 0) -> dict:
    key = jax.random.key(seed)
    ks = jax.random.split(key, 26)
    f32 = jnp.float32

    def nrm(k, shape, scale):
        return jax.random.normal(k, shape, f32) * scale

    ne, no, nl = N_EVEN, N_ODD, DEPTH
    x = nrm(ks[0], (BATCH, SEQ, D_MODEL), 1.0)
    w_in_ab = nrm(ks[1], (ne, D_MODEL, IN_AB), D_MODEL ** -0.5)
    ssm_conv_w = nrm(ks[2], (ne, SSD_CONV_K, SSD_XBC), SSD_CONV_K ** -0.5)
    ssm_conv_b = nrm(ks[3], (ne, SSD_XBC), 0.02)
    dt0 = jnp.exp(jax.random.uniform(ks[4], (ne, 2, SSD_HEADS), f32, minval=math.log(1e-3), maxval=math.log(1e-1)))
    ssm_dt_bias = dt0 + jnp.log(-jnp.expm1(-dt0))
    ssm_a_log = jnp.log(jax.random.uniform(ks[5], (ne, 2, SSD_HEADS), f32, minval=1.0, maxval=16.0))
    ssm_d = 1.0 + nrm(ks[6], (ne, SSD_HEADS), 0.02)
    ssm_norm_w = 1.0 + nrm(ks[7], (ne, SSD_INNER), 0.02)
    attn_q_norm = 1.0 + nrm(ks[8], (ne, ATT_HEAD_DIM), 0.02)
    attn_k_norm = 1.0 + nrm(ks[9], (ne, ATT_HEAD_DIM), 0.02)
    w_out_ab = nrm(ks[10], (ne, OUT_AB, D_MODEL), OUT_AB ** -0.5 * DEEPNORM_BETA)
    w_in_cd = nrm(ks[11], (no, D_MODEL, IN_CD), D_MODEL ** -0.5)
    pool_w = nrm(ks[12], (no, len(POOL_WINDOWS), POOL_GROUP, POOL_GROUP), POOL_GROUP ** -0.5)
    pool_scale = 1.0 + nrm(ks[13], (no, POOL_INNER), 0.02)
    hgrn_lb_logits = nrm(ks[14], (DEPTH, HGRN_INNER), 0.1)
    hgrn_norm_w = 1.0 + nrm(ks[15], (no, HGRN_INNER), 0.02)
    w_out_cd = nrm(ks[16], (no, OUT_CD, D_MODEL), OUT_CD ** -0.5 * DEEPNORM_BETA)
    router_w = nrm(ks[17], (nl, D_MODEL, N_EXPERTS), D_MODEL ** -0.5)
    moe_w1 = nrm(ks[18], (nl, N_EXPERTS, D_MODEL, EXPERT_FF), D_MODEL ** -0.5)
    moe_w3 = nrm(ks[19], (nl, N_EXPERTS, D_MODEL, EXPERT_FF), D_MODEL ** -0.5)
    moe_w2 = nrm(ks[20], (nl, N_EXPERTS, EXPERT_FF, D_MODEL), EXPERT_FF ** -0.5 * DEEPNORM_BETA)
    ln1_g = 1.0 + nrm(ks[21], (nl, D_MODEL), 0.02)
    ln1_b = nrm(ks[22], (nl, D_MODEL), 0.02)
    ln2_g = 1.0 + nrm(ks[23], (nl, D_MODEL), 0.02)
    ln2_b = nrm(ks[24], (nl, D_MODEL), 0.02)
    return {'x': x, 'w_in_ab': w_in_ab, 'ssm_conv_w': ssm_conv_w, 'ssm_conv_b': ssm_conv_b,
            'ssm_dt_bias': ssm_dt_bias, 'ssm_a_log': ssm_a_log, 'ssm_d': ssm_d, 'ssm_norm_w': ssm_norm_w,
            'attn_q_norm': attn_q_norm, 'attn_k_norm': attn_k_norm, 'w_out_ab': w_out_ab,
            'w_in_cd': w_in_cd, 'pool_w': pool_w, 'pool_scale': pool_scale,
            'hgrn_lb_logits': hgrn_lb_logits, 'hgrn_norm_w': hgrn_norm_w, 'w_out_cd': w_out_cd,
            'router_w': router_w, 'moe_w1': moe_w1, 'moe_w3': moe_w3, 'moe_w2': moe_w2,
            'ln1_g': ln1_g, 'ln1_b': ln1_b, 'ln2_g': ln2_g, 'ln2_b': ln2_b}


def reference(x, w_in_ab, ssm_conv_w, ssm_conv_b, ssm_dt_bias, ssm_a_log, ssm_d, ssm_norm_w,
              attn_q_norm, attn_k_norm, w_out_ab, w_in_cd, pool_w, pool_scale,
              hgrn_lb_logits, hgrn_norm_w, w_out_cd, router_w, moe_w1, moe_w3, moe_w2,
              ln1_g, ln1_b, ln2_g, ln2_b):
    seq = x.shape[1]
    cos, sin = axial_rope_tables(seq)
    lb_all = jnp.cumsum(jax.nn.softmax(hgrn_lb_logits.astype(jnp.float32), axis=0), axis=0)
    lb_all = lb_all - lb_all[0]
    for layer in range(DEPTH):
        j = layer // 2
        if layer % 2 == 0:
            mix = ab_mixer(x, w_in_ab[j], ssm_conv_w[j], ssm_conv_b[j], ssm_dt_bias[j], ssm_a_log[j],
                           ssm_d[j], ssm_norm_w[j], attn_q_norm[j], attn_k_norm[j], w_out_ab[j], cos, sin)
        else:
            mix = cd_mixer(x, w_in_cd[j], pool_w[j], pool_scale[j], lb_all[layer], hgrn_norm_w[j], w_out_cd[j])
        x = layer_norm(DEEPNORM_ALPHA * x + mix, ln1_g[layer], ln1_b[layer])
        ffn = expert_choice_moe(x, router_w[layer], moe_w1[layer], moe_w3[layer], moe_w2[layer])
        x = layer_norm(DEEPNORM_ALPHA * x + ffn, ln2_g[layer], ln2_b[layer])
    return x
```

```python
import numpy as np
import concourse.bass as bass
import concourse.mybir as mybir
from concourse.bass_utils import run_bass_kernel_spmd

F32 = mybir.dt.float32
F32R = mybir.dt.float32r
F16 = mybir.dt.float16
BF16 = mybir.dt.bfloat16
I32 = mybir.dt.int32
AF = mybir.ActivationFunctionType
ALU = mybir.AluOpType
AX = mybir.AxisListType

ENGS = ("pe", "dve", "act", "pool", "sp")
SEM_CAP = 30000
DMA_K = 6


class Acc:
    __slots__ = ("ap", "cells")

    def __init__(self, ap, cells):
        self.ap = ap
        self.cells = cells

    def v(self, fn):
        return Acc(fn(self.ap), self.cells)

    def r(self):
        return Acc(self.ap.bitcast(F32R), self.cells)

    def bc(self, shape):
        return Acc(self.ap.to_broadcast(shape), self.cells)


class TT:
    def __init__(self, prog, name, handle, shape, dtype, blk):
        self.prog = prog
        self.name = name
        self.h = handle
        self.shape = list(shape)
        self.dtype = dtype
        self.blk = blk
        self.fstr = []
        s = 1
        for d in reversed(self.shape[1:]):
            self.fstr.insert(0, s)
            s *= d
        self.fsize = s

    def cells_of(self, idx):
        rngs = [(0, 0)]
        fd = self.shape[1:]
        idx = list(idx) + [slice(None)] * (len(self.shape) - len(idx))
        dims = []
        for i, d in enumerate(fd):
            ix = idx[i + 1]
            if isinstance(ix, int):
                dims.append((ix, ix + 1))
            else:
                a = 0 if ix.start is None else ix.start
                b = d if ix.stop is None else ix.stop
                assert ix.step is None
                dims.append((a, b))
        starts = [0]
        n = len(fd)
        tail = n
        while tail > 0 and dims[tail - 1] == (0, fd[tail - 1]):
            tail -= 1
        if tail == 0:
            return {(self.name, c) for c in range(0, (self.fsize - 1) // self.blk + 1)}
        outer = dims[: tail - 1]
        a, b = dims[tail - 1]
        st = self.fstr[tail - 1]
        starts = [0]
        for (lo, hi), s in zip(outer, self.fstr[: tail - 1]):
            starts = [x + k * s for x in starts for k in range(lo, hi)]
        cells = set()
        for x in starts:
            s0 = x + a * st
            e0 = x + b * st
            for c in range(s0 // self.blk, (e0 - 1) // self.blk + 1):
                cells.add((self.name, c))
        return cells

    def __getitem__(self, idx):
        if not isinstance(idx, tuple):
            idx = (idx,)
        return Acc(self.h[idx], self.cells_of(idx))

    def all(self):
        return self[tuple([slice(None)] * len(self.shape))]


class Op:
    __slots__ = ("eng", "fn", "rc", "wc", "dma", "deps", "signal", "ev", "note")

    def __init__(self, eng, fn, rc, wc, dma=False, note=""):
        self.eng = eng
        self.fn = fn
        self.rc = rc
        self.wc = wc
        self.dma = dma
        self.deps = ()
        self.signal = False
        self.ev = None
        self.note = note


class Prog:
    def __init__(self, nc):
        self.nc = nc
        self.ops = []
        self.stack = []
        self._uid = 0
        self.eobj = {"pe": nc.tensor, "dve": nc.vector, "act": nc.scalar, "pool": nc.gpsimd, "sp": nc.sync}

    def sb(self, name, shape, dtype=F32, blk=None):
        self._uid += 1
        nm = f"{name}_{self._uid}"
        cm = self.nc.sbuf_tensor(nm, list(shape), dtype)
        h = cm.__enter__()
        self.stack.append(cm)
        if blk is None:
            blk = shape[-1]
        return TT(self, nm, h, shape, dtype, blk)

    def ps(self, name, shape, dtype=F32, blk=None):
        self._uid += 1
        nm = f"{name}_{self._uid}"
        cm = self.nc.psum_tensor(nm, list(shape), dtype)
        h = cm.__enter__()
        self.stack.append(cm)
        blk = 2048 // mybir.dt.size(dtype)
        return TT(self, "PS:" + nm, h, shape, dtype, blk)

    def mark(self):
        return len(self.stack)

    def release(self, mark):
        self.barrier()
        while len(self.stack) > mark:
            cm = self.stack.pop()
            cm.__exit__(None, None, None)

    def barrier(self):
        self.ops.append(Op("all", None, set(), set(), note="barrier"))

    def add(self, eng, fn, reads, writes, dma=False, note=""):
        rc = set()
        for a in reads:
            if a is not None and isinstance(a, Acc):
                rc |= a.cells
        wc = set()
        for a in writes:
            if a is not None and isinstance(a, Acc):
                wc |= a.cells
        self.ops.append(Op(eng, fn, rc, wc, dma, note))

    @staticmethod
    def _ap(a):
        return a.ap if isinstance(a, Acc) else a

    def mm(self, out, lhsT, rhs, start=True, stop=True, **kw):
        o, l, r = out.ap, lhsT.ap, rhs.ap
        self.add("pe", lambda e: e.matmul(o, l, r, start=start, stop=stop, **kw), [lhsT, rhs], [out])

    def transpose(self, out, in_, ident):
        o, i, d = out.ap, in_.ap, ident.ap
        self.add("pe", lambda e: e.transpose(o, i, d), [in_, ident], [out])

    def act(self, out, in_, func, bias=None, scale=1.0, accum_out=None, eng="act"):
        o, i = out.ap, in_.ap
        b = self._ap(bias)
        s = self._ap(scale)
        kw = {}
        if b is not None:
            kw["bias"] = b
        if accum_out is not None:
            kw["accum_out"] = accum_out.ap
        self.add(eng, lambda e: e.activation(o, i, func, scale=s, **kw),
                 [in_, bias, scale], [out, accum_out])

    def tt(self, out, in0, in1, op, eng="dve"):
        o, a, b = out.ap, in0.ap, in1.ap
        self.add(eng, lambda e: e.tensor_tensor(o, a, b, op), [in0, in1], [out])

    def ts(self, out, in0, s1, op0, s2=None, op1=None, accum_out=None, eng="dve"):
        o, a = out.ap, in0.ap
        if eng == "pool" and op1 is None and op0 == ALU.mult:
            op1, s2 = ALU.mult, 1.0
        x1 = self._ap(s1)
        x2 = self._ap(s2)
        kw = {}
        if op1 is not None:
            kw["op1"] = op1
        if accum_out is not None:
            kw["accum_out"] = accum_out.ap
        self.add(eng, lambda e: e.tensor_scalar(o, a, x1, x2, op0, **kw), [in0, s1, s2], [out, accum_out])

    def stt(self, out, in0, scalar, in1, op0, op1, eng="dve"):
        o, a, b = out.ap, in0.ap, in1.ap
        s = self._ap(scalar)
        self.add(eng, lambda e: e.scalar_tensor_tensor(o, a, s, b, op0, op1), [in0, scalar, in1], [out])

    def copy(self, out, in_, eng="dve"):
        o, i = out.ap, in_.ap
        if eng == "act":
            self.add(eng, lambda e: e.copy(o, i), [in_], [out])
        else:
            self.add(eng, lambda e: e.tensor_copy(o, i), [in_], [out])

    def memset(self, out, val, eng="dve"):
        o = out.ap
        self.add(eng, lambda e: e.memset(o, val), [], [out])

    def reduce(self, out, in_, op, axis=AX.X, eng="dve"):
        o, i = out.ap, in_.ap
        self.add(eng, lambda e: e.tensor_reduce(o, i, axis, op), [in_], [out])

    def recip(self, out, in_):
        o, i = out.ap, in_.ap
        self.add("dve", lambda e: e.reciprocal(o, i), [in_], [out])

    def dma(self, out, in_, q="sp", slow=False):
        o = self._ap(out)
        i = self._ap(in_)
        if slow:
            self.add(q, lambda e: e.dma_start(out=o, in_=i, allow_slow_non_contiguous=True), [in_], [out], dma=True)
        else:
            self.add(q, lambda e: e.dma_start(out=o, in_=i), [in_], [out], dma=True)

    def generic(self, eng, fn, reads, writes):
        self.add(eng, fn, reads, writes)

    def emit(self):
        nc = self.nc
        ops = self.ops
        last_w = {}
        readers = {}
        for i, op in enumerate(ops):
            if op.eng == "all":
                continue
            deps = set()
            for c in op.rc:
                w = last_w.get(c)
                if w is not None:
                    deps.add(w)
                if c[0].startswith("PS:"):
                    rd = readers.get(c)
                    if rd:
                        for kk, vv in rd.items():
                            if kk != op.eng:
                                deps.add(vv)
            for c in op.wc:
                w = last_w.get(c)
                if w is not None:
                    deps.add(w)
                rd = readers.get(c)
                if rd:
                    deps.update(rd.values())
            deps.discard(i)
            for c in op.wc:
                last_w[c] = i
                readers[c] = {}
            key = ("d", i) if op.dma else op.eng
            for c in op.rc:
                if c in op.wc:
                    continue
                readers.setdefault(c, {})[key] = i
            best = {}
            keep = []
            for d in deps:
                od = ops[d]
                if od.dma:
                    keep.append(d)
                else:
                    if od.eng == "pe" and op.eng == "pe" and not op.dma:
                        continue
                    if best.get(od.eng, -1) < d:
                        best[od.eng] = d
            keep.extend(best.values())
            op.deps = keep
            for d in keep:
                ops[d].signal = True
        last_on = {}
        bar_deps = {}
        for i, op in enumerate(ops):
            if op.eng == "all":
                bar_deps[i] = dict(last_on)
                for d in last_on.values():
                    ops[d].signal = True
            elif op.dma:
                op.signal = True
                last_on[("d", i)] = i
            else:
                last_on[op.eng] = i
        final = dict(last_on)
        for d in final.values():
            ops[d].signal = True

        sems = {}
        semctx = []

        def new_sem(nm):
            cm = nc.semaphore(nm)
            h = cm.__enter__()
            semctx.append(cm)
            return h

        cnt = {e: 0 for e in ENGS}
        eng_sems = {e: [] for e in ENGS}
        dma_cnt = {e: 0 for e in ENGS}
        dma_sems = {e: [] for e in ENGS}
        for i, op in enumerate(ops):
            if op.eng == "all" or not op.signal:
                continue
            if op.dma:
                q = op.eng
                n = dma_cnt[q]
                dma_cnt[q] += 1
                if len(dma_sems[q]) < DMA_K:
                    dma_sems[q].append(new_sem(f"dq_{q}_{len(dma_sems[q])}"))
                op.ev = (dma_sems[q][n % DMA_K], 16 * (n // DMA_K + 1), 16)
            else:
                e = op.eng
                n = cnt[e]
                cnt[e] += 1
                si = n // SEM_CAP
                if len(eng_sems[e]) <= si:
                    eng_sems[e].append(new_sem(f"c_{e}_{si}"))
                op.ev = (eng_sems[e][si], n % SEM_CAP + 1, 1)

        waited = {e: {} for e in ENGS}

        def wait(e, ev):
            sem, val = ev[0], ev[1]
            k = id(sem)
            if waited[e].get(k, 0) >= val:
                return
            waited[e][k] = val
            self.eobj[e].wait_ge(sem, val)

        self.n_emitted = {e: 0 for e in ENGS}
        for i, op in enumerate(ops):
            if op.eng == "all":
                for e in ENGS:
                    for d in bar_deps[i].values():
                        if ops[d].ev is not None:
                            wait(e, ops[d].ev)
                continue
            e = op.eng
            for d in op.deps:
                wait(e, ops[d].ev)
            if op.dma:
                sem, val, inc = op.ev
                if val > 16:
                    wait(e, (sem, val - 16))
            inst = op.fn(self.eobj[e])
            self.n_emitted[e] += 1
            if op.signal:
                inst.then_inc(op.ev[0], op.ev[2])
        for d in final.values():
            if ops[d].ev is not None:
                wait("sp", ops[d].ev)
        self._semctx = semctx

    def close(self):
        while self.stack:
            self.stack.pop().__exit__(None, None, None)
        for cm in reversed(getattr(self, "_semctx", [])):
            cm.__exit__(None, None, None)


D = 1024
L = 2048
NT = 16
NK = 8
ALPHA = 8.0 ** 0.25
LN_EPS = 1e-5
NORM_EPS = 1e-6
NEG = -1.0e30

C_ID, C_IOTA, C_UF, C_UB, C_NMF, C_NMB, C_COS, C_SIN, C_END = 0, 128, 384, 512, 640, 768, 896, 1408, 1920


def make_consts():
    c = np.zeros((128, C_END), np.float32)
    c[:, C_ID:C_ID + 128] = np.eye(128, dtype=np.float32)
    c[:, C_IOTA:C_IOTA + 256] = np.arange(256, dtype=np.float32)[None, :]
    k = np.arange(128)[:, None]
    i = np.arange(128)[None, :]
    c[:, C_UF:C_UF + 128] = (k <= i).astype(np.float32)
    c[:, C_UB:C_UB + 128] = (k >= i).astype(np.float32)
    c[:, C_NMF:C_NMF + 128] = np.where(i >= k, 0.0, NEG)
    c[:, C_NMB:C_NMB + 128] = np.where(i <= k, 0.0, NEG)
    t = np.arange(L)
    row = (t // 64).astype(np.float32)
    col = (t % 64).astype(np.float32)
    freqs = (10000.0 ** (-np.arange(0, 32, 2, dtype=np.float32) / 32)).astype(np.float32)
    ang = np.concatenate([row[:, None] * freqs, col[:, None] * freqs], axis=-1).astype(np.float32)
    cos = np.cos(ang).astype(np.float32).reshape(NT, 128, 32).transpose(1, 0, 2).reshape(128, 512)
    sin = np.sin(ang).astype(np.float32).reshape(NT, 128, 32).transpose(1, 0, 2).reshape(128, 512)
    c[:, C_COS:C_COS + 512] = cos
    c[:, C_SIN:C_SIN + 512] = sin
    return c


class Ctx:
    pass


def setup(P, nc, dram):
    G = Ctx()
    G.nc = nc
    G.P = P
    G.d = dram
    G.X = P.sb("X", [128, NK, L], F32, blk=128)
    G.PB = [P.ps(f"pb{i}", [128, 512], F32, blk=128) for i in range(8)]
    G.cst = P.sb("cst", [128, C_END], F32, blk=128)
    P.dma(G.cst.all(), dram["consts"])
    G.ones = P.sb("ones", [128, 128], F32)
    P.memset(G.ones.all(), 1.0, eng="pool")
    G.onesD = P.sb("onesD", [128, 128], F32)
    P.ts(G.onesD.all().r(), G.ones.all(), 1.0 / D, ALU.mult)
    G.id16 = P.sb("id16", [128, 128], F16)
    P.copy(G.id16.all(), G.cst[:, C_ID:C_ID + 128])
    G.ident = G.cst[:, C_ID:C_ID + 128]
    G.eps_ln = P.sb("epsln", [128, 1], F32)
    P.memset(G.eps_ln.all(), LN_EPS, eng="pool")
    G.rr = 0
    return G


def evac_eng(G):
    G.rr += 1
    return "act" if G.rr % 2 else "dve"


def load_x(G, xd):
    P, X, PB = G.P, G.X, G.PB
    m = P.mark()
    tmp = [P.sb(f"ldx{i}", [128, D], F32, blk=128) for i in range(2)]
    for i in range(NT):
        tb = tmp[i % 2]
        P.dma(tb.all(), xd[i * 128:(i + 1) * 128, :])
        for kk in range(2):
            pb = PB[(2 * i + kk) % 2]
            for j in range(4):
                k = kk * 4 + j
                P.transpose(pb[:, j * 128:(j + 1) * 128], tb[:, k * 128:(k + 1) * 128], G.ident)
            dst = X[:, kk * 4:(kk + 1) * 4, i * 128:(i + 1) * 128].r()
            src = pb.all().v(lambda a: a.rearrange("p (j t) -> p j t", j=4))
            P.copy(dst, src, eng=evac_eng(G))
    P.release(m)


def store_x(G, od):
    P, X, PB = G.P, G.X, G.PB
    m = P.mark()
    tmp = [P.sb(f"stx{i}", [128, D], F32, blk=128) for i in range(2)]
    for i in range(NT):
        tb = tmp[i % 2]
        for kk in range(2):
            pb = PB[(2 * i + kk) % 2]
            for j in range(4):
                k = kk * 4 + j
                P.transpose(pb[:, j * 128:(j + 1) * 128], X[:, k, i * 128:(i + 1) * 128], G.ident)
            P.copy(tb[:, kk * 512:(kk + 1) * 512], pb.all(), eng=evac_eng(G))
        P.dma(od[i * 128:(i + 1) * 128, :], tb.all())
    P.release(m)


def layer_norm(G, gd, bd, eps=LN_EPS):
    P, X, PB = G.P, G.X, G.PB
    m = P.mark()
    gb = P.sb("ln_gb", [128, 2, NK], F32)
    P.dma(gb[:, 0, :], gd.rearrange("(k p) -> p k", p=128), slow=True)
    P.dma(gb[:, 1, :], bd.rearrange("(k p) -> p k", p=128), slow=True)
    sq = [P.sb(f"ln_sq{i}", [128, 512], F32) for i in range(2)]
    epst = P.sb("ln_eps", [128, 1], F32)
    P.memset(epst.all(), eps, eng="pool")
    m2 = [P.sb(f"ln_m2{i}", [128, 512], F32) for i in range(2)]
    rs = [P.sb(f"ln_rs{i}", [128, 512], F32) for i in range(2)]
    tmp = [P.sb(f"ln_t{i}", [128, 512], F32) for i in range(8)]
    def stats(tb):
        ts_ = slice(tb * 512, (tb + 1) * 512)
        o = 3 * (tb % 2)
        ps_s, ps_q, ps_r = PB[o], PB[o + 1], PB[o + 2]
        for k in range(NK):
            P.mm(ps_s.all(), G.onesD.all().r(), X[:, k, ts_].r(), start=(k == 0), stop=(k == NK - 1))
        for k in range(NK):
            s_ = sq[k % 2]
            if k % 2 == 0:
                P.act(s_.all().r(), X[:, k, ts_], AF.Square)
            else:
                P.tt(s_.all().r(), X[:, k, ts_], X[:, k, ts_], ALU.mult)
            P.mm(ps_q.all(), G.onesD.all().r(), s_.all().r(), start=(k == 0), stop=(k == NK - 1))
        m2_, rs_ = m2[tb % 2], rs[tb % 2]
        P.act(m2_.all(), ps_s.all(), AF.Square)
        P.tt(rs_.all(), ps_q.all(), m2_.all(), ALU.subtract)
        P.act(rs_.all(), rs_.all(), AF.Sqrt, bias=epst.all(), scale=1.0)
        P.recip(rs_.all(), rs_.all())

    def norm(tb):
        ts_ = slice(tb * 512, (tb + 1) * 512)
        o = 3 * (tb % 2)
        ps_s = PB[o]
        rs_ = rs[tb % 2]
        for k in range(NK):
            t = tmp[k % 8]
            P.tt(t.all(), X[:, k, ts_], ps_s.all(), ALU.subtract)
            P.tt(t.all(), t.all(), rs_.all(), ALU.mult, eng="pool")
            P.act(X[:, k, ts_].r(), t.all(), AF.Identity, bias=gb[:, 1, k:k + 1], scale=gb[:, 0, k:k + 1])

    stats(0)
    for tb in range(4):
        if tb + 1 < 4:
            stats(tb + 1)
        norm(tb)
    P.release(m)


NE = 16
CAP = 256
FF = 2048
FB = 512
NFB = FF // FB


def moe_phase(G, l, upto=99):
    P, X, PB, d = G.P, G.X, G.PB, G.d
    m0 = P.mark()
    x16 = P.sb("x16", [128, NT, D], F16, blk=128)
    pos_tok = P.sb("pos_tok", [128, NT, NE], F32, blk=NT * NE)
    mask_tok = P.sb("mask_tok", [128, NT, NE], F32, blk=NT * NE)
    gm_tok = P.sb("gm_tok", [128, NT, NE], F32, blk=NT * NE)
    m1 = P.mark()
    rw = P.sb("rw", [128, NK, NE], F32)
    P.dma(rw.all(), d["router_w"][l].rearrange("(k p) e -> p k e", p=128))
    lg = PB[2]
    for i in range(NT):
        for k in range(NK):
            P.mm(lg[:, i * NE:(i + 1) * NE], X[:, k, i * 128:(i + 1) * 128], rw[:, k, :],
                 start=(k == 0), stop=(k == NK - 1))
    v3 = lambda a: a.rearrange("p (i e) -> p i e", e=NE)
    mx = P.sb("r_mx", [128, NT], F32)
    sh = P.sb("r_sh", [128, NT * NE], F32)
    aff = P.sb("r_aff", [128, NT * NE], F32)
    P.reduce(mx.all(), lg[:, 0:NT * NE].v(v3), ALU.max)
    P.tt(sh.all().v(v3), lg[:, 0:NT * NE].v(v3), mx.all().v(lambda a: a.unsqueeze(2).to_broadcast([128, NT, NE])), ALU.subtract)
    P.act(sh.all(), sh.all(), AF.Exp)
    P.reduce(mx.all(), sh.all().v(v3), ALU.add)
    P.recip(mx.all(), mx.all())
    P.tt(aff.all().v(v3), sh.all().v(v3), mx.all().v(lambda a: a.unsqueeze(2).to_broadcast([128, NT, NE])), ALU.mult)
    affT = P.sb("affT", [NE, L], F32, blk=128)
    work = P.sb("r_work", [NE, L], F32, blk=L)
    maskT = P.sb("maskT", [NE, L], F32, blk=128)
    posT = P.sb("posT", [NE, L], F32, blk=128)
    for b in range(4):
        pb = PB[4 + b]
        for j in range(4):
            i = b * 4 + j
            P.transpose(pb[0:NE, j * 128:(j + 1) * 128], aff[:, i * NE:(i + 1) * NE], G.ident)
        P.copy(affT[:, b * 512:(b + 1) * 512], pb[0:NE, :], eng=evac_eng(G))
    P.copy(work.all(), affT.all(), eng="pool")
    for i in range(NT):
        for kk in range(2):
            pb = PB[(2 * i + kk) % 2]
            for j in range(4):
                k = kk * 4 + j
                P.transpose(pb[:, j * 128:(j + 1) * 128], X[:, k, i * 128:(i + 1) * 128], G.ident)
            P.copy(x16[:, i, kk * 512:(kk + 1) * 512], pb.all(), eng="act")
    m8 = P.sb("r_m8", [NE, 8], F32)
    for it in range(CAP // 8):
        w_, m_ = work.all(), m8.all()
        P.add("dve", lambda e, w_=w_, m_=m_: e.max(out=m_.ap, in_=w_.ap), [w_], [m_])
        if it < CAP // 8 - 1:
            P.add("dve", lambda e, w_=w_, m_=m_: e.match_replace(out=w_.ap, in_to_replace=m_.ap, in_values=w_.ap, imm_value=NEG),
                  [w_, m_], [w_])
    P.ts(maskT.all(), affT.all(), m8[:, 7:8], ALU.is_ge)
    one16 = G.ones[0:NE, 0:1].bc([NE, L])
    mt_, pt_ = maskT.all(), posT.all()
    P.add("dve", lambda e: e.tensor_tensor_scan(pt_.ap, one16.ap, mt_.ap, 0.0, ALU.mult, ALU.add), [mt_, one16], [pt_])
    P.tt(posT.all(), posT.all(), maskT.all(), ALU.subtract)
    pm, pp = PB[2], PB[3]
    for i in range(NT):
        P.transpose(pm[:, i * NE:(i + 1) * NE], maskT[:, i * 128:(i + 1) * 128], G.cst[0:NE, C_ID:C_ID + NE])
        P.transpose(pp[:, i * NE:(i + 1) * NE], posT[:, i * 128:(i + 1) * 128], G.cst[0:NE, C_ID:C_ID + NE])
    P.copy(mask_tok.all().v(lambda a: a.rearrange("p i e -> p (i e)")), pm[:, 0:NT * NE], eng="act")
    P.copy(pos_tok.all().v(lambda a: a.rearrange("p i e -> p (i e)")), pp[:, 0:NT * NE], eng="dve")
    P.stt(gm_tok.all().v(lambda a: a.rearrange("p i e -> p (i e)")), aff.all(), 1.0 / ALPHA, pm[:, 0:NT * NE], ALU.mult, ALU.mult)
    P.release(m1)
    if upto < 2:
        P.release(m0)
        return

    Wb = [[P.sb(f"w1b{i}", [128, NK, FB], F16, blk=FB), P.sb(f"w3b{i}", [128, NK, FB], F16, blk=FB),
           P.sb(f"w2b{i}", [128, FB // 128, D], F16, blk=D)] for i in range(2)]
    Se = P.sb("Se", [128, NT, 2 * CAP], F16, blk=CAP)
    S2 = P.sb("S2", [128, NT, CAP], F16, blk=CAP)
    SeT = P.sb("SeT", [128, 2, 2, L], F16, blk=512)
    xsT = P.sb("xsT", [128, NK, 2 * CAP], F16, blk=CAP)
    hT = [P.sb(f"hT{i}", [128, CAP], F16) for i in range(2)]
    sl = [P.sb(f"sl{i}", [128, CAP], F32) for i in range(2)]
    ysb = P.sb("ysb", [128, 2, 2, D], F16, blk=512)
    iota = G.cst[:, C_IOTA:C_IOTA + CAP]
    w1d, w3d, w2d = d["moe_w1"], d["moe_w3"], d["moe_w2"]
    SUB, NEX = 9, NE
    G.nblk = 0
    for e in range(NEX):
        ej = e % 2
        cofs = ej * CAP
        if ej == 0:
            for i in range(NT):
                for jj in range(2):
                    P.ts(Se[:, i, jj * CAP:(jj + 1) * CAP], iota, pos_tok[:, i, e + jj:e + jj + 1], ALU.is_equal,
                         mask_tok[:, i, e + jj:e + jj + 1], ALU.mult)
            for k in range(NK):
                pb = PB[4 + (k % 2)]
                for i in range(NT):
                    P.mm(pb.all(), x16[:, i, k * 128:(k + 1) * 128], Se[:, i, :], start=(i == 0), stop=(i == NT - 1))
                P.copy(xsT[:, k, :], pb.all(), eng=evac_eng(G))
        trp = [PB[6], PB[7]]
        for i in range(NT):
            s2 = S2
            P.ts(s2[:, i, :], Se[:, i, cofs:cofs + CAP], gm_tok[:, i, e:e + 1], ALU.mult)
            for cc in range(2):
                pbv = trp[cc].all().v(lambda a: a.bitcast(F16))
                col = (i % 8) * 128
                dst = Acc(pbv.ap[:, col:col + 128], pbv.cells)
                P.transpose(dst, s2[:, i, cc * 128:(cc + 1) * 128], G.id16.all())
            if i % 8 == 7:
                for cc in range(2):
                    pbv = trp[cc].all().v(lambda a: a.bitcast(F16))
                    h0 = (i // 8) * 1024
                    P.copy(SeT[:, ej, cc, h0:h0 + 1024], pbv, eng=("act" if cc else "dve"))
        if SUB < 2:
            continue
        yps = [PB[0], PB[1], PB[2], PB[3]]
        NFC = FF // 128
        blocks = {}

        def h_stage(fc):
            fb, jj = divmod(fc, FB // 128)
            if jj == 0:
                W1, W3, W2 = Wb[G.nblk % 2]
                G.nblk += 1
                f0 = fb * FB
                P.dma(W1.all(), w1d[l, e].rearrange("(k p) f -> p k f", p=128)[:, :, f0:f0 + FB], q="pool")
                P.dma(W3.all(), w3d[l, e].rearrange("(k p) f -> p k f", p=128)[:, :, f0:f0 + FB], q="pool")
                P.dma(W2.all(), w2d[l, e, f0:f0 + FB, :].rearrange("(j p) n -> p j n", p=128), q="pool")
                blocks[fb] = (W1, W3, W2)
            W1, W3, W2 = blocks[fb]
            pb = PB[4 + (fc % 2)]
            for k in range(NK):
                P.mm(pb[:, 0:CAP], W1[:, k, jj * 128:(jj + 1) * 128], xsT[:, k, cofs:cofs + CAP], start=(k == 0), stop=(k == NK - 1))
            for k in range(NK):
                P.mm(pb[:, CAP:2 * CAP], W3[:, k, jj * 128:(jj + 1) * 128], xsT[:, k, cofs:cofs + CAP], start=(k == 0), stop=(k == NK - 1))
            s_, h_ = sl[fc % 2], hT[fc % 2]
            P.act(s_.all(), pb[:, 0:CAP], AF.Silu)
            P.tt(h_.all(), s_.all(), pb[:, CAP:2 * CAP], ALU.mult)

        def w2_stage(fc):
            fb, jj = divmod(fc, FB // 128)
            W2 = blocks[fb][2]
            h_ = hT[fc % 2]
            for cc in range(2):
                for dh in range(2):
                    P.mm(yps[cc * 2 + dh].all(), h_[:, cc * 128:(cc + 1) * 128], W2[:, jj, dh * 512:(dh + 1) * 512],
                         start=(fc == 0), stop=(fc == NFC - 1))

        h_stage(0)
        for fc in range(NFC):
            if fc + 1 < NFC:
                h_stage(fc + 1)
            w2_stage(fc)
        for cc in range(2):
            for dh in range(2):
                P.copy(ysb[:, ej, cc, dh * 512:(dh + 1) * 512], yps[cc * 2 + dh].all(), eng=evac_eng(G))
        if SUB < 3 or ej == 0:
            continue
        for k in range(NK):
            for tb in range(4):
                pb = PB[(k * 4 + tb) % 4]
                n = 0
                for e2 in range(2):
                    for cc in range(2):
                        P.mm(pb.all(), ysb[:, e2, cc, k * 128:(k + 1) * 128], SeT[:, e2, cc, tb * 512:(tb + 1) * 512],
                             start=(n == 0), stop=(n == 3))
                        n += 1
                xs_ = X[:, k, tb * 512:(tb + 1) * 512]
                P.tt(xs_.r(), pb.all(), xs_, ALU.add)
    P.release(m0)
    if upto < 3:
        return
    layer_norm(G, d["ln2_g"][l], d["ln2_b"][l], eps=LN_EPS / (ALPHA * ALPHA))


U32 = mybir.dt.uint32


def wcols(wd, c0, n):
    return wd.rearrange("(k p) c -> p k c", p=128)[:, :, c0:c0 + n]


def out_proj_ln1(G, wo_d, yT, g_d, b_d):
    P, X, PB = G.P, G.X, G.PB
    m = P.mark()
    wst = [P.sb(f"wo_st{i}", [128, NK, 128], F32, blk=128) for i in range(2)]
    w16 = [P.sb(f"wo_16{i}", [128, NK, 128], F16, blk=128) for i in range(2)]
    for kd in range(NK):
        st, wh = wst[kd % 2], w16[kd % 2]
        P.dma(st.all(), wo_d.rearrange("(k p) c -> p k c", p=128)[:, :, kd * 128:(kd + 1) * 128])
        P.copy(wh.all(), st.all(), eng="act")
        for tb in range(4):
            pb = PB[(kd * 4 + tb) % 2]
            for kc in range(NK):
                P.mm(pb.all(), wh[:, kc, :], yT[:, kc, tb * 512:(tb + 1) * 512], start=(kc == 0), stop=(kc == NK - 1))
            xs_ = X[:, kd, tb * 512:(tb + 1) * 512]
            P.stt(xs_.r(), xs_, ALPHA, pb.all(), ALU.mult, ALU.add)
    P.release(m)
    layer_norm(G, g_d, b_d)


POOL_W = (2, 4, 8, 16)
HQ = 128
NCH = L // HQ


def mixer_cd(G, l):
    P, X, PB, d = G.P, G.X, G.PB, G.d
    j = l // 2
    wd = d["w_in_cd"][j]
    m0 = P.mark()
    yT = P.sb("yT", [128, NK, L], F16, blk=128)
    wbuf = [P.sb(f"wcd{i}", [128, NK, 128], F32, blk=128) for i in range(3)]
    G.wrr = 0

    def load_w(c0):
        w = wbuf[G.wrr % 3]
        G.wrr += 1
        P.dma(w.all().r(), wcols(wd, c0, 128))
        return w

    def proj_fm(w, tb, pb):
        for k in range(NK):
            P.mm(pb.all(), w[:, k, :].r(), X[:, k, tb * 512:(tb + 1) * 512].r(), start=(k == 0), stop=(k == NK - 1))

    m1 = P.mark()
    HB = 16
    ub = P.sb("pl_u", [128, L + 2 * HB], F32, blk=L + 2 * HB)
    pa = P.sb("pl_a", [128, L + 2 * HB], F32, blk=L + 2 * HB)
    pbuf = P.sb("pl_b", [128, L + 2 * HB], F32, blk=L + 2 * HB)
    pooled = P.sb("pl_p", [128, L], F32, blk=512)
    pw = P.sb("pl_w", [128, 4, 128], F32, blk=128)
    psc = P.sb("pl_sc", [128, 4], F32)
    P.dma(pw.all().r(), d["pool_w"][j].rearrange("g c d -> c g d"))
    P.dma(psc.all(), d["pool_scale"][j].rearrange("(g p) -> p g", p=128), slow=True)
    for t_ in (ub, pa, pbuf):
        P.memset(t_.all(), 0.0, eng="pool")
    for gi, w_ in enumerate(POOL_W):
        wt = load_w(gi * 128)
        for tb in range(4):
            pb = PB[tb % 2]
            proj_fm(wt, tb, pb)
            P.copy(ub[:, HB + tb * 512:HB + (tb + 1) * 512], pb.all(), eng=evac_eng(G))
        src = ub
        bufs = [pa, pbuf]
        lo, hi = HB - 8, HB + L + 8
        for s in range(gi + 1):
            dst = bufs[s % 2]
            if s == 0:
                P.tt(dst[:, lo:hi], src[:, lo - 1:hi - 1], src[:, lo:hi], ALU.add)
            else:
                sh = 1 << (s - 1)
                P.tt(dst[:, lo:hi], src[:, lo - sh:hi - sh], src[:, lo + sh:hi + sh], ALU.add)
            src = dst
        P.stt(pooled.all().r(), src[:, HB:HB + L], 1.0 / w_, ub[:, HB:HB + L], ALU.mult, ALU.subtract)
        for t in list(range(0, w_ // 2)) + list(range(L - w_ // 2 + 1, L)):
            cnt = min(t - w_ // 2 + w_, L) - max(t - w_ // 2, 0)
            P.stt(pooled[:, t:t + 1].r(), src[:, HB + t:HB + t + 1], 1.0 / cnt, ub[:, HB + t:HB + t + 1], ALU.mult, ALU.subtract)
        for tb in range(4):
            pb = PB[2 + tb % 2]
            P.mm(pb.all(), pw[:, gi, :].r(), pooled[:, tb * 512:(tb + 1) * 512].r())
            P.act(yT[:, gi, tb * 512:(tb + 1) * 512], pb.all(), AF.Identity, scale=psc[:, gi:gi + 1])
    P.release(m1)

    m1 = P.mark()
    lbl = P.sb("lb_l", [128, 4, 4], F32)
    P.dma(lbl.all(), d["hgrn_lb_logits"].rearrange("l (h p) -> p l h", p=128), slow=True)
    lbv = P.sb("lb_v", [128, 4], F32)
    oml = P.sb("lb_om", [128, 4], F32)
    lsum = P.sb("lb_s", [128, 4], F32)
    P.act(lbl.all(), lbl.all(), AF.Exp)
    P.reduce(lsum.all(), lbl.all().v(lambda a: a.rearrange("p l h -> p h l")), ALU.add)
    P.recip(lsum.all(), lsum.all())
    P.copy(lbv.all(), lbl[:, 1, :])
    for ll in range(2, l + 1):
        P.tt(lbv.all(), lbv.all(), lbl[:, ll, :], ALU.add)
    P.tt(lbv.all(), lbv.all(), lsum.all(), ALU.mult)
    P.ts(oml.all(), lbv.all(), -1.0, ALU.mult, 1.0, ALU.add)
    rmask = P.sb("hg_rm", [128, L], BF16, blk=L)
    P.memset(rmask.all(), 1.0, eng="pool")
    P.memset(rmask.all().v(lambda a: a.rearrange("p (c q) -> p c q", q=HQ)[:, :, 0:1]), 0.0, eng="pool")
    nwp = P.sb("hg_nwp", [128, 4], F32)
    P.dma(nwp.all(), d["hgrn_norm_w"][j].rearrange("(h p) -> p h", p=128), slow=True)
    qT = P.sb("hg_qT", [128, L], F32, blk=512)
    fb_ = P.sb("hg_f", [128, L], F32, blk=L)
    lf = P.sb("hg_lf", [128, L], F32, blk=L)
    cum = P.sb("hg_cum", [128, L], F32, blk=L)
    qt_ = [P.sb(f"hg_qt{i}", [128, L], F32, blk=HQ) for i in range(2)]
    kt_ = [P.sb(f"hg_kt{i}", [128, L], F32, blk=HQ) for i in range(2)]
    esc = [[P.sb(f"hg_es{i}{n}", [128, NCH], F32) for n in range(3)] for i in range(2)]
    vtok = P.sb("hg_v", [128, NT, 128], F32, blk=128)
    sgT = P.sb("hg_sgT", [128, L], F16, blk=128)
    Sbst = P.sb("hg_Sb", [128, NCH, 128], F32, blk=128)
    Sst = [P.sb(f"hg_S{i}", [128, 128], F32) for i in range(2)]
    Stl = [P.sb(f"hg_St{i}", [128, 128], F32) for i in range(2)]
    ktok = [P.sb(f"hg_ktok{i}", [128, 128], F32) for i in range(2)]
    sT = [[P.sb(f"hg_sT{i}{n}", [128, 128], F32) for n in range(2)] for i in range(2)]
    sTraw = [[P.sb(f"hg_sTr{i}{n}", [128, 128], F32) for n in range(2)] for i in range(2)]
    for i in range(2):
        for n in range(2):
            P.memset(sTraw[i][n].all(), 0.0, eng="pool")
    ssq = P.sb("hg_ssq", [128, 2], F32)
    tot = P.sb("hg_tot", [128, NCH], F32)
    refc = P.sb("hg_refc", [128, NCH], F32)
    junk = P.sb("hg_junk", [128, 128], F32)
    ytk = [P.sb(f"hg_y{i}", [128, 128], F16) for i in range(2)]
    epsn = P.sb("hg_eps", [128, 1], F32)
    P.memset(epsn.all(), NORM_EPS, eng="pool")
    masks = [G.cst[:, C_UF:C_UF + 128].v(lambda a: a.bitcast(U32)), G.cst[:, C_UB:C_UB + 128].v(lambda a: a.bitcast(U32))]
    c3 = lambda a: a.rearrange("p (c q) -> p c q", q=HQ)

    for h in range(4):
        wq = load_w(512 + h * 128)
        for tb in range(4):
            pb = PB[tb % 2]
            proj_fm(wq, tb, pb)
            P.copy(qT[:, tb * 512:(tb + 1) * 512], pb.all(), eng=evac_eng(G))
        wv = load_w(2048 + h * 128)
        wg = load_w(2560 + h * 128)
        for tb in range(4):
            pb = PB[tb % 2]
            proj_fm(wv, tb, pb)
            P.copy(fb_[:, tb * 512:(tb + 1) * 512], pb.all(), eng=evac_eng(G))
            pg = PB[2 + tb % 2]
            proj_fm(wg, tb, pg)
            P.act(sgT[:, tb * 512:(tb + 1) * 512], pg.all(), AF.Sigmoid)
        P.ts(sgT.all(), sgT.all(), nwp[:, h:h + 1], ALU.mult, eng="pool")
        for i4 in range(4):
            pb = PB[4 + i4 % 2]
            for jj in range(4):
                i = i4 * 4 + jj
                P.transpose(pb[:, jj * 128:(jj + 1) * 128], fb_[:, i * 128:(i + 1) * 128], G.ident)
            P.copy(vtok[:, i4 * 4:(i4 + 1) * 4, :].r(), pb.all().v(lambda a: a.rearrange("p (i v) -> p i v", v=128)), eng=evac_eng(G))
        for di in range(2):
            wf = load_w(1024 + di * 512 + h * 128)
            for tb in range(4):
                pb = PB[tb % 2]
                proj_fm(wf, tb, pb)
                P.act(fb_[:, tb * 512:(tb + 1) * 512], pb.all(), AF.Sigmoid)
            P.ts(fb_.all(), fb_.all(), oml[:, h:h + 1], ALU.mult, lbv[:, h:h + 1], ALU.add)
            P.act(lf.all(), fb_.all(), AF.Ln)
            c_, r_, l_ = cum.all(), rmask.all(), lf.all()
            P.add("dve", lambda e, c_=c_, r_=r_, l_=l_: e.tensor_tensor_scan(c_.ap, r_.ap, l_.ap, 0.0, ALU.mult, ALU.add), [r_, l_], [c_])
            if di == 1:
                P.tt(lf.all(), lf.all(), cum.all(), ALU.subtract)
                P.copy(tot.all().v(lambda a: a.unsqueeze(2)), cum.all().v(lambda a: c3(a)[:, :, HQ - 1:HQ]))
                P.tt(cum.all().v(c3), lf.all().v(c3), tot.all().v(lambda a: a.unsqueeze(2).to_broadcast([128, NCH, HQ])), ALU.add)
            lastcol = (HQ - 1) if di == 0 else 0
            refv = cum.all().v(lambda a: c3(a)[:, :, HQ // 2:HQ // 2 + 1])
            lastv = cum.all().v(lambda a, lc=lastcol: c3(a)[:, :, lc:lc + 1])
            v2 = lambda a: a.unsqueeze(2)
            e_ref, e_last, e_lr = esc[di]
            P.act(e_ref.all().v(v2), refv, AF.Exp)
            P.act(e_last.all().v(v2), lastv, AF.Exp)
            P.tt(e_lr.all().v(v2), lastv, refv, ALU.subtract)
            P.act(e_lr.all(), e_lr.all(), AF.Exp)
            P.copy(refc.all().v(v2), refv)
            P.tt(lf.all().v(c3), cum.all().v(c3), refc.all().v(lambda a: a.unsqueeze(2).to_broadcast([128, NCH, HQ])), ALU.subtract)
            P.act(cum.all(), lf.all(), AF.Exp)
            P.tt(qt_[di].all().r(), qT.all(), cum.all(), ALU.mult, eng="pool")
            P.act(cum.all(), lf.all(), AF.Exp, scale=-1.0)
            P.ts(fb_.all(), fb_.all(), -1.0, ALU.mult, 1.0, ALU.add, eng="pool")
            P.tt(kt_[di].all().r(), fb_.all(), cum.all(), ALU.mult)

        def state_step(di, c, S):
            e_ref, e_last, e_lr = esc[di]
            kk = ktok[di]
            pt = PB[4 + di]
            P.transpose(pt[:, 0:128], kt_[di][:, c * HQ:(c + 1) * HQ], G.ident)
            P.copy(kk.all().r(), pt[:, 0:128], eng="act")
            pu = PB[6 + di]
            P.mm(pu[:, 0:128], kk.all().r(), vtok[:, c, :].r())
            P.ts(S.all(), S.all(), e_last[:, c:c + 1], ALU.mult, eng="pool")
            P.stt(S.all(), pu[:, 0:128], e_lr[:, c:c + 1], S.all(), ALU.mult, ALU.add)

        P.memset(Sst[1].all(), 0.0, eng="pool")
        for c in range(NCH - 1, -1, -1):
            P.ts(Sbst[:, c, :].r(), Sst[1].all(), esc[1][0][:, c:c + 1], ALU.mult)
            if c > 0:
                state_step(1, c, Sst[1])
        P.memset(Sst[0].all(), 0.0, eng="pool")
        for c in range(NCH):
            cs = slice(c * HQ, (c + 1) * HQ)
            stl = Stl[c % 2]
            P.ts(stl.all().r(), Sst[0].all(), esc[0][0][:, c:c + 1], ALU.mult, eng="pool")
            for di in range(2):
                ps_ = PB[di]
                P.mm(ps_[:, 0:128], kt_[di][:, cs].r(), qt_[di][:, cs].r())
                o_, m_, p_ = sTraw[di][c % 2].all(), masks[di], ps_[:, 0:128]
                P.add("dve", lambda e, o_=o_, m_=m_, p_=p_: e.copy_predicated(o_.ap, m_.ap, p_.ap), [m_, p_], [o_])
                P.copy(sT[di][c % 2].all().r(), o_, eng="act")
            po = PB[2 + c % 2]
            P.mm(po[:, 0:128], sT[0][c % 2].all().r(), vtok[:, c, :].r(), start=True, stop=False)
            P.mm(po[:, 0:128], sT[1][c % 2].all().r(), vtok[:, c, :].r(), start=False, stop=False)
            P.mm(po[:, 0:128], qt_[0][:, cs].r(), stl.all().r(), start=False, stop=False)
            P.mm(po[:, 0:128], qt_[1][:, cs].r(), Sbst[:, c, :].r(), start=False, stop=True)
            if c < NCH - 1:
                state_step(0, c, Sst[0])
            P.act(junk.all(), po[:, 0:128], AF.Square, accum_out=ssq[:, 0:1])
            P.act(ssq[:, 1:2], ssq[:, 0:1], AF.Sqrt, bias=epsn.all(), scale=1.0 / 128)
            P.recip(ssq[:, 1:2], ssq[:, 1:2])
            yk = ytk[c % 2]
            P.ts(yk.all(), po[:, 0:128], ssq[:, 1:2], ALU.mult)
            pt = PB[4 + 0].all().v(lambda a: a.bitcast(F16))
            ptc = Acc(pt.ap[:, 0:128], pt.cells)
            P.transpose(ptc, yk.all(), G.id16.all())
            P.tt(yT[:, 4 + h, cs], ptc, sgT[:, cs], ALU.mult)
    P.release(m1)
    out_proj_ln1(G, d["w_out_cd"][j], yT, d["ln1_g"][l], d["ln1_b"][l])
    P.release(m0)


def mixer_ab(G, l, do_attn=True, do_ssd=True):
    P, X, PB, d = G.P, G.X, G.PB, G.d
    j = l // 2
    wd = d["w_in_ab"][j]
    m0 = P.mark()
    yT = P.sb("yT", [128, NK, L], F16, blk=128)
    hd = lambda a: a.rearrange("p (h e) -> p h e", e=64)
    f16v = lambda pb: pb.all().v(lambda a: a.bitcast(F16))

    if do_attn:
        m1 = P.mark()
        qT = P.sb("at_qT", [128, 4, L], F16, blk=128)
        kT = P.sb("at_kT", [128, 2, 2, L], F16, blk=128)
        vaug = P.sb("at_v", [128, NT, 2, 66], F16, blk=132)
        P.memset(vaug.all(), 1.0, eng="pool")
        m2 = P.mark()
        Wq = P.sb("at_W", [128, NK, 768], F32, blk=768)
        P.dma(Wq.all().r(), wcols(wd, 1552, 768))
        wqk = P.sb("at_wqk", [128, 2, 64], F32)
        P.dma(wqk[:, 0, :], d["attn_q_norm"][j].partition_broadcast(128))
        P.dma(wqk[:, 1, :], d["attn_k_norm"][j].partition_broadcast(128))
        epsn = P.sb("at_eps", [128, 1], F32)
        P.memset(epsn.all(), NORM_EPS, eng="pool")
        sq_ = [P.sb(f"at_sq{i}", [128, 640], F32) for i in range(2)]
        ssq_ = [P.sb(f"at_ssq{i}", [128, 10], F32) for i in range(2)]
        qn_ = [P.sb(f"at_qn{i}", [128, 640], F32) for i in range(2)]
        ra_ = [P.sb(f"at_ra{i}", [128, 320], F32) for i in range(2)]
        rb_ = [P.sb(f"at_rb{i}", [128, 320], F32) for i in range(2)]
        ra2_ = [P.sb(f"at_rc{i}", [128, 320], F32) for i in range(2)]
        rb2_ = [P.sb(f"at_rd{i}", [128, 320], F32) for i in range(2)]
        qr = [P.sb(f"at_qr{i}", [128, 512], F32) for i in range(2)]
        kr = [P.sb(f"at_kr{i}", [128, 2, 2, 128], F32, blk=512) for i in range(2)]
        for t_ in kr:
            P.memset(t_.all(), 0.0, eng="pool")
        for i in range(NT):
            ts_ = slice(i * 128, (i + 1) * 128)
            sq, ssq, qn, ra, rb = sq_[i % 2], ssq_[i % 2], qn_[i % 2], ra_[i % 2], rb_[i % 2]
            pq, pk = PB[(2 * i) % 4], PB[(2 * i + 1) % 4]
            for k in range(NK):
                P.mm(pq.all(), X[:, k, ts_].r(), Wq[:, k, 0:512].r(), start=(k == 0), stop=(k == NK - 1))
            for k in range(NK):
                P.mm(pk[:, 0:256], X[:, k, ts_].r(), Wq[:, k, 512:768].r(), start=(k == 0), stop=(k == NK - 1))
            P.copy(vaug[:, i, :, 0:64], pk[:, 128:256].v(lambda a: a.rearrange("p (g e) -> p g e", e=64)), eng="dve")
            P.act(sq[:, 0:512], pq.all(), AF.Square)
            P.act(sq[:, 512:640], pk[:, 0:128], AF.Square)
            P.reduce(ssq.all(), sq.all().v(hd), ALU.add)
            P.act(ssq.all(), ssq.all(), AF.Sqrt, bias=epsn.all(), scale=1.0 / 64)
            P.recip(ssq.all(), ssq.all())
            P.tt(qn[:, 0:512].v(hd), pq.all().v(hd), ssq[:, 0:8].v(lambda a: a.unsqueeze(2).to_broadcast([128, 8, 64])), ALU.mult)
            P.tt(qn[:, 512:640].v(hd), pk[:, 0:128].v(hd), ssq[:, 8:10].v(lambda a: a.unsqueeze(2).to_broadcast([128, 2, 64])), ALU.mult)
            P.tt(qn[:, 0:512].v(hd), qn[:, 0:512].v(hd), wqk[:, 0, :].v(lambda a: a.unsqueeze(1).to_broadcast([128, 8, 64])), ALU.mult, eng="pool")
            P.tt(qn[:, 512:640].v(hd), qn[:, 512:640].v(hd), wqk[:, 1, :].v(lambda a: a.unsqueeze(1).to_broadcast([128, 2, 64])), ALU.mult, eng="pool")
            pr = lambda a: a.rearrange("p (h e two) -> p h e two", e=32, two=2)
            xe = qn.all().v(lambda a: pr(a)[:, :, :, 0])
            xo = qn.all().v(lambda a: pr(a)[:, :, :, 1])
            cs_ = G.cst[:, C_COS + i * 32:C_COS + (i + 1) * 32].v(lambda a: a.unsqueeze(1).to_broadcast([128, 10, 32]))
            sn_ = G.cst[:, C_SIN + i * 32:C_SIN + (i + 1) * 32].v(lambda a: a.unsqueeze(1).to_broadcast([128, 10, 32]))
            h3 = lambda a: a.rearrange("p (h e) -> p h e", e=32)
            qro, kro = qr[i % 2], kr[i % 2]
            P.tt(ra.all().v(h3), xe, cs_, ALU.mult)
            P.tt(rb.all().v(h3), xo, sn_, ALU.mult, eng="pool")
            oute_q = qro.all().v(lambda a: pr(a)[:, :, :, 0])
            outo_q = qro.all().v(lambda a: pr(a)[:, :, :, 1])
            P.tt(oute_q, ra[:, 0:256].v(h3), rb[:, 0:256].v(h3), ALU.subtract)
            for dup in range(2):
                oe = kro[:, :, dup, dup * 64:(dup + 1) * 64].v(lambda a: a.rearrange("p g (e two) -> p g e two", two=2)[:, :, :, 0])
                P.tt(oe, ra[:, 256:320].v(h3), rb[:, 256:320].v(h3), ALU.subtract, eng="pool")
            ra2, rb2 = ra2_[i % 2], rb2_[i % 2]
            P.tt(ra2.all().v(h3), xe, sn_, ALU.mult)
            P.tt(rb2.all().v(h3), xo, cs_, ALU.mult, eng="pool")
            P.tt(outo_q, ra2[:, 0:256].v(h3), rb2[:, 0:256].v(h3), ALU.add)
            for dup in range(2):
                oo = kro[:, :, dup, dup * 64:(dup + 1) * 64].v(lambda a: a.rearrange("p g (e two) -> p g e two", two=2)[:, :, :, 1])
                P.tt(oo, ra2[:, 256:320].v(h3), rb2[:, 256:320].v(h3), ALU.add, eng="pool")
            pt = PB[4 + i % 2]
            for hp in range(4):
                P.transpose(pt[:, hp * 128:(hp + 1) * 128], qro[:, hp * 128:(hp + 1) * 128], G.ident)
            P.copy(qT[:, :, ts_], pt.all().v(lambda a: a.rearrange("p (h t) -> p h t", t=128)), eng="act")
            pt2 = PB[6 + i % 2]
            for g in range(2):
                for v in range(2):
                    P.transpose(pt2[:, (g * 2 + v) * 128:(g * 2 + v + 1) * 128], kro[:, g, v, :], G.ident)
            P.copy(kT[:, :, :, ts_], pt2.all().v(lambda a: a.rearrange("p (g v t) -> p g v t", g=2, v=2)), eng="act")
        P.release(m2)
        PT = [P.sb(f"at_PT{i}", [128, 512], BF16) for i in range(3)]
        OTs = [P.sb(f"at_OT{i}", [66, 512], F32) for i in range(2)]
        rsb = [P.sb(f"at_rs{i}", [128, 512], F32) for i in range(2)]
        sel = P.sb("at_sel", [66, 3, 128], F32, blk=128)
        P.copy(sel[:, 0, :].r(), G.cst[0:66, C_ID:C_ID + 128])
        P.ts(sel[:, 1, 0:64].r(), G.ones[0:66, 0:64], 0.0, ALU.mult)
        P.copy(sel[:, 1, 64:128].r(), G.cst[0:66, C_ID:C_ID + 64])
        P.ts(sel[:, 2, :].r(), G.cst[0:66, C_ID + 64:C_ID + 65].bc([66, 128]), 1.0, ALU.mult)
        G.npt = 0
        nit = 0
        for h in range(8):
            g, hp, hf = h // 4, h // 2, h % 2
            prt = slice(hf * 64, (hf + 1) * 64)
            for qb in range(4):
                qs = slice(qb * 512, (qb + 1) * 512)
                po = PB[2 + nit % 2]
                pts = {}

                def score(kt):
                    ps = PB[kt % 2]
                    P.mm(ps.all(), kT[:, g, hf, kt * 128:(kt + 1) * 128], qT[:, hp, qs])
                    pt_ = PT[G.npt % 3]
                    G.npt += 1
                    P.act(pt_.all(), ps.all(), AF.Exp, scale=0.125)
                    pts[kt] = pt_

                score(0)
                for kt in range(NT):
                    if kt + 1 < NT:
                        score(kt + 1)
                    P.mm(po[0:66, :], vaug[:, kt, g, :], pts[kt].all(), start=(kt == 0), stop=(kt == NT - 1))
                ot = OTs[nit % 2]
                P.copy(ot.all().r(), po[0:66, :], eng="dve")
                pso, pss = PB[4 + 2 * (nit % 2)], PB[5 + 2 * (nit % 2)]
                P.mm(pso.all(), sel[:, hf, :].r(), ot.all().r())
                P.mm(pss.all(), sel[:, 2, :].r(), ot.all().r())
                r_ = rsb[nit % 2]
                P.recip(r_[prt, :], pss[prt, :])
                P.tt(yT[prt, 4 + hp, qs], pso[prt, :], r_[prt, :], ALU.mult)
                nit += 1
        P.release(m1)
    else:
        for k in range(4, 8):
            P.memset(yT[:, k, :], 0.0, eng="pool")

    if do_ssd:
        m1 = P.mark()
        xtok = P.sb("sd_x", [128, NT, 512], F16, blk=512)
        BT = P.sb("sd_BT", [128, 2, L], F16, blk=128)
        CT = P.sb("sd_CT", [128, 2, L], F16, blk=128)
        Btok = P.sb("sd_Bt", [128, NT, 256], F16, blk=256)
        sc = lambda nm: P.sb(nm, [128, NT, 16], F32, blk=256)
        dt_, lndt, la, cumb_a, tot, A2, wgt, ea, dec = [sc(n) for n in ("sd_dt", "sd_lndt", "sd_la", "sd_A", "sd_tot", "sd_A2", "sd_w", "sd_ea", "sd_dec")]
        f2 = lambda a: a.rearrange("p i e -> p (i e)")
        m2 = P.mark()
        wdt = P.sb("sd_wdt", [128, NK, 16], F32)
        P.dma(wdt.all(), d["w_dt"][j].rearrange("(k p) c -> p k c", p=128))
        bb = P.sb("sd_bias", [128, 16], F32)
        P.dma(bb.all(), d["ssm_dt_bias"][j].rearrange("a h -> (a h)").partition_broadcast(128))
        ab = P.sb("sd_alog", [128, 16], F32)
        P.dma(ab.all(), d["ssm_a_log"][j].rearrange("a h -> (a h)").partition_broadcast(128))
        P.act(ab.all(), ab.all(), AF.Exp)
        pd = PB[0]
        for i in range(NT):
            for k in range(NK):
                P.mm(pd[:, i * 16:(i + 1) * 16], X[:, k, i * 128:(i + 1) * 128], wdt[:, k, :], start=(k == 0), stop=(k == NK - 1))
        b3 = lambda t_: t_.all().v(lambda a: a.unsqueeze(1).to_broadcast([128, NT, 16]))
        P.tt(dt_.all(), pd[:, 0:256].v(lambda a: a.rearrange("p (i e) -> p i e", e=16)), b3(bb), ALU.add)
        P.act(dt_.all(), dt_.all(), AF.Exp)
        P.act(dt_.all(), dt_.all(), AF.Ln, bias=G.ones[:, 0:1], scale=1.0)
        P.act(lndt.all(), dt_.all(), AF.Ln)
        P.stt(la.all(), dt_.all(), -1.0, b3(ab), ALU.mult, ALU.mult)
        pc = PB[1]
        for i in range(NT):
            P.mm(pc[:, i * 32:i * 32 + 8], G.cst[:, C_UF:C_UF + 128], la[:, i, 0:8])
            P.mm(pc[:, i * 32 + 8:i * 32 + 16], G.cst[:, C_UB:C_UB + 128], la[:, i, 8:16])
            P.mm(pc[:, i * 32 + 16:i * 32 + 32], G.ones.all(), la[:, i, :])
        pc3 = pc.all().v(lambda a: a.rearrange("p (i e) -> p i e", e=32))
        P.copy(cumb_a.all(), Acc(pc3.ap[:, :, 0:16], pc3.cells), eng="act")
        P.copy(tot.all(), Acc(pc3.ap[:, :, 16:32], pc3.cells), eng="dve")
        P.tt(A2.all(), cumb_a.all(), lndt.all(), ALU.subtract)
        P.tt(wgt.all(), tot.all(), A2.all(), ALU.subtract)
        P.act(wgt.all(), wgt.all(), AF.Exp)
        P.act(ea.all(), cumb_a.all(), AF.Exp)
        P.act(dec.all(), tot.all(), AF.Exp)
        wbuf = [P.sb(f"sd_w{i}", [128, NK, 128], F32, blk=128) for i in range(2)]
        cw = P.sb("sd_cw", [128, 5, 8], F32)
        cbi = P.sb("sd_cb", [128, 8], F32)
        for kk in range(5):
            P.dma(cw[:, kk, :], d["ssm_conv_w"][j, kk].rearrange("(c p) -> p c", p=128), slow=True)
        P.dma(cbi.all(), d["ssm_conv_b"][j].rearrange("(c p) -> p c", p=128), slow=True)
        ubuf = P.sb("sd_u", [128, L + 4], F32, blk=L + 4)
        so = P.sb("sd_so", [128, L], F16, blk=128)
        dg = [P.sb(f"sd_dg{i}", [128, 5, 128], F32, blk=128) for i in range(2)]
        P.ts(ubuf.all().r(), G.ones[:, 0:1].bc([128, L + 4]), 0.0, ALU.mult)
        for ch in range(8):
            w = wbuf[ch % 2]
            P.dma(w.all().r(), wcols(wd, 512 + ch * 128, 128))
            for tb in range(4):
                pb = PB[2 + tb % 2]
                for k in range(NK):
                    P.mm(pb.all(), w[:, k, :].r(), X[:, k, tb * 512:(tb + 1) * 512].r(), start=(k == 0), stop=(k == NK - 1))
                P.copy(ubuf[:, 2 + tb * 512:2 + (tb + 1) * 512].r(), pb.all(), eng=evac_eng(G))
            dgc = dg[ch % 2]
            for kk in range(5):
                P.ts(dgc[:, kk, :].r(), G.ident, cw[:, kk, ch:ch + 1], ALU.mult)
            for tb in range(4):
                pc_ = PB[tb % 2]
                for kk in range(5):
                    P.mm(pc_.all(), dgc[:, kk, :].r(), ubuf[:, tb * 512 + kk:tb * 512 + kk + 512].r(), start=(kk == 0), stop=(kk == 4))
                tsl = slice(tb * 512, (tb + 1) * 512)
                if ch < 4:
                    dst = so[:, tsl]
                elif ch < 6:
                    dst = BT[:, ch - 4, tsl]
                else:
                    dst = CT[:, ch - 6, tsl]
                P.act(dst, pc_.all(), AF.Silu, bias=cbi[:, ch:ch + 1])
            if ch < 4:
                for i8 in range(2):
                    pb = PB[4 + i8]
                    pv = f16v(pb)
                    for i in range(8):
                        ti = i8 * 8 + i
                        P.transpose(Acc(pv.ap[:, i * 128:(i + 1) * 128], pv.cells), so[:, ti * 128:(ti + 1) * 128], G.id16.all())
                    P.copy(xtok[:, i8 * 8:(i8 + 1) * 8, ch * 128:(ch + 1) * 128], Acc(pv.ap.rearrange("p (i c) -> p i c", c=128), pv.cells), eng=evac_eng(G))
            elif ch < 6:
                g = ch - 4
                for i8 in range(2):
                    pb = PB[6 + i8]
                    pv = f16v(pb)
                    for i in range(8):
                        ti = i8 * 8 + i
                        P.transpose(Acc(pv.ap[:, i * 128:(i + 1) * 128], pv.cells), BT[:, g, ti * 128:(ti + 1) * 128], G.id16.all())
                    P.copy(Btok[:, i8 * 8:(i8 + 1) * 8, g * 128:(g + 1) * 128], Acc(pv.ap.rearrange("p (i c) -> p i c", c=128), pv.cells), eng=evac_eng(G))
        P.release(m2)
        Wz = P.sb("sd_Wz", [128, NK, 512], F32, blk=512)
        P.dma(Wz.all().r(), wcols(wd, 0, 512))
        Sbs = P.sb("sd_Sbs", [128, NT, 512], F16, blk=512)
        S32 = [P.sb(f"sd_S32{i}", [128, 512], F32) for i in range(2)]
        S16 = [P.sb(f"sd_S16{i}", [128, 512], F16) for i in range(2)]
        xw = [P.sb(f"sd_xw{i}", [128, 512], F16) for i in range(2)]
        dsk = P.sb("sd_dsk", [128, 8], F32)
        P.dma(dsk.all(), d["ssm_d"][j].partition_broadcast(128))
        nwb = P.sb("sd_nw", [128, 512], F32)
        P.dma(nwb.all(), d["ssm_norm_w"][j].partition_broadcast(128))
        epsn = P.sb("sd_eps", [128, 1], F32)
        P.memset(epsn.all(), NORM_EPS, eng="pool")
        rhsb = [P.sb("sd_rhs", [128, 8, 128], F32, blk=1024)] * 2
        E = [P.sb(f"sd_E{i}", [128, 8, 128], F32, blk=128) for i in range(2)]
        Tt = [P.sb(f"sd_T{i}", [128, 128], F32) for i in range(2)]
        MT = [P.sb(f"sd_MT{i}", [128, 128], F16) for i in range(3)]
        t1 = P.sb("sd_t1", [128, 512], F32)
        t2 = P.sb("sd_t2", [128, 512], F32)
        sz = P.sb("sd_sz", [128, 512], F16)
        yk = P.sb("sd_yk", [128, 512], F16)
        ssq = P.sb("sd_ssq", [128, 2], F32)
        hb = lambda t_, c, d0: t_[:, c, d0:d0 + 8].v(lambda a: a.unsqueeze(2).to_broadcast([128, 8, 64]))

        def state_update(di, c):
            xw_ = xw[di]
            P.tt(xw_.all().v(hd), xtok[:, c, :].v(hd), hb(wgt, c, di * 8), ALU.mult, eng="pool")
            pst = PB[7]
            for g in range(2):
                P.mm(pst[:, g * 256:(g + 1) * 256], Btok[:, c, g * 128:(g + 1) * 128], xw_[:, g * 256:(g + 1) * 256])
            P.tt(S32[di].all().v(hd), S32[di].all().v(hd), hb(dec, c, di * 8), ALU.mult, eng="pool")
            P.tt(S32[di].all(), S32[di].all(), pst.all(), ALU.add)

        P.memset(S32[1].all(), 0.0, eng="pool")
        for c in range(NT - 1, -1, -1):
            P.copy(Sbs[:, c, :], S32[1].all(), eng="act")
            if c > 0:
                state_update(1, c)
        P.memset(S32[0].all(), 0.0, eng="pool")
        G.nmt = 0

        def front(c):
            cs = slice(c * 128, (c + 1) * 128)
            pyd = PB[3] if c % 2 == 0 else PB[6]
            for di in range(2):
                U = G.cst[:, (C_UF if di == 0 else C_UB):(C_UF if di == 0 else C_UB) + 128]
                nm = G.cst[:, (C_NMF if di == 0 else C_NMB):(C_NMF if di == 0 else C_NMB) + 128]
                rh = rhsb[di]
                P.tt(rh.all(), U.v(lambda a: a.unsqueeze(1).to_broadcast([128, 8, 128])),
                     la[:, c, di * 8:di * 8 + 8].v(lambda a: a.unsqueeze(2).to_broadcast([128, 8, 128])), ALU.mult,
                     eng=("dve" if di == 0 else "pool"))
                for hh in range(2):
                    P.mm(PB[hh].all(), G.ones.all(), rh[:, hh * 4:(hh + 1) * 4, :].v(lambda a: a.rearrange("p h i -> p (h i)")))
                for h in range(8):
                    tt_ = Tt[h % 2]
                    P.stt(tt_.all(), PB[h // 4][:, (h % 4) * 128:(h % 4 + 1) * 128], A2[:, c, di * 8 + h:di * 8 + h + 1], nm, ALU.subtract, ALU.add)
                    P.act(E[di][:, h, :], tt_.all(), AF.Exp)
            for g in range(2):
                P.mm(PB[2][:, g * 128:(g + 1) * 128], BT[:, g, cs], CT[:, g, cs])
            P.tt(E[0].all(), E[0].all(), E[1].all(), ALU.add, eng="pool")
            for h in range(8):
                mt = MT[G.nmt % 3]
                G.nmt += 1
                P.tt(mt.all(), PB[2][:, (h // 4) * 128:(h // 4 + 1) * 128], E[0][:, h, :], ALU.mult)
                P.mm(pyd[:, h * 64:(h + 1) * 64], mt.all(), xtok[:, c, h * 64:(h + 1) * 64])

        def back(c):
            cs = slice(c * 128, (c + 1) * 128)
            pyd = PB[3] if c % 2 == 0 else PB[6]
            P.copy(S16[0].all(), S32[0].all(), eng="act")
            for g in range(2):
                P.mm(PB[4][:, g * 256:(g + 1) * 256], CT[:, g, cs], S16[0][:, g * 256:(g + 1) * 256])
                P.mm(PB[5][:, g * 256:(g + 1) * 256], CT[:, g, cs], Sbs[:, c, g * 256:(g + 1) * 256])
            if c < NT - 1:
                state_update(0, c)
            pz = PB[0]
            for k in range(NK):
                P.mm(pz.all(), X[:, k, cs].r(), Wz[:, k, :].r(), start=(k == 0), stop=(k == NK - 1))
            P.act(sz.all(), pz.all(), AF.Silu)
            P.tt(t1.all().v(hd), PB[4].all().v(hd), hb(ea, c, 0), ALU.mult)
            P.tt(t2.all().v(hd), PB[5].all().v(hd), hb(ea, c, 8), ALU.mult)
            P.tt(t1.all(), t1.all(), t2.all(), ALU.add, eng="pool")
            P.tt(t2.all().v(hd), xtok[:, c, :].v(hd), dsk.all().v(lambda a: a.unsqueeze(2).to_broadcast([128, 8, 64])), ALU.mult, eng="pool")
            P.tt(t1.all(), t1.all(), t2.all(), ALU.add, eng="pool")
            P.tt(t1.all(), pyd.all(), t1.all(), ALU.add)
            P.tt(t1.all(), t1.all(), sz.all(), ALU.mult)
            P.act(t2.all(), t1.all(), AF.Square, accum_out=ssq[:, 0:1])
            P.act(ssq[:, 1:2], ssq[:, 0:1], AF.Sqrt, bias=epsn.all(), scale=1.0 / 512)
            P.recip(ssq[:, 1:2], ssq[:, 1:2])
            P.stt(yk.all(), t1.all(), ssq[:, 1:2], nwb.all(), ALU.mult, ALU.mult)
            pv = f16v(PB[7])
            for c4 in range(4):
                P.transpose(Acc(pv.ap[:, c4 * 128:(c4 + 1) * 128], pv.cells), yk[:, c4 * 128:(c4 + 1) * 128], G.id16.all())
            P.copy(yT[:, 0:4, cs], Acc(pv.ap[:, 0:512].rearrange("p (h t) -> p h t", t=128), pv.cells), eng="act")

        front(0)
        for c in range(NT):
            if c + 1 < NT:
                front(c + 1)
            back(c)
        P.release(m1)
    else:
        for k in range(4):
            P.memset(yT[:, k, :], 0.0, eng="pool")
    out_proj_ln1(G, d["w_out_ab"][j], yT, d["ln1_g"][l], d["ln1_b"][l])
    P.release(m0)


DEPTH = 4
_DRAM_SPECS = [
    ("consts", [128, C_END], F32), ("x", [L, D], F32),
    ("w_in_ab", [2, D, 2320], F32R), ("w_dt", [2, D, 16], F32), ("ssm_conv_w", [2, 5, 1024], F32), ("ssm_conv_b", [2, 1024], F32),
    ("ssm_dt_bias", [2, 2, 8], F32), ("ssm_a_log", [2, 2, 8], F32), ("ssm_d", [2, 8], F32), ("ssm_norm_w", [2, 512], F32),
    ("attn_q_norm", [2, 64], F32), ("attn_k_norm", [2, 64], F32), ("w_out_ab", [2, D, D], F32),
    ("w_in_cd", [2, D, 3072], F32R), ("pool_w", [2, 4, 128, 128], F32R), ("pool_scale", [2, 512], F32),
    ("hgrn_lb_logits", [4, 512], F32), ("hgrn_norm_w", [2, 512], F32), ("w_out_cd", [2, D, D], F32),
    ("router_w", [4, D, NE], F32), ("moe_w1", [4, NE, D, FF], F32), ("moe_w3", [4, NE, D, FF], F32), ("moe_w2", [4, NE, FF, D], F32),
    ("ln1_g", [4, D], F32), ("ln1_b", [4, D], F32), ("ln2_g", [4, D], F32), ("ln2_b", [4, D], F32),
]


def build_program(layers=range(DEPTH)):
    nc = bass.Bass("TRN2", target_bir_lowering=False, dynamic_dma_scratch_size=8192)
    nc.dge_precook = False
    dram = {}
    for name, shape, dt in _DRAM_SPECS:
        dram[name] = nc.dram_tensor(name, list(shape), dt, kind="ExternalInput").ap()
    out = nc.dram_tensor("out", [L, D], F32, kind="ExternalOutput").ap()
    P = Prog(nc)
    G = setup(P, nc, dram)
    load_x(G, dram["x"])
    for l in layers:
        if l % 2 == 0:
            mixer_ab(G, l)
        else:
            mixer_cd(G, l)
        moe_phase(G, l)
    store_x(G, out)
    P.emit()
    P.close()
    return nc


def kernel(**inputs):
    x = np.ascontiguousarray(np.asarray(inputs["x"], dtype=np.float32))
    nb = x.shape[0]
    shared = {"consts": make_consts()}
    for name, shape, dt in _DRAM_SPECS:
        if name in ("consts", "x", "w_dt"):
            continue
        shared[name] = np.ascontiguousarray(np.asarray(inputs[name], dtype=np.float32))
    shared["w_dt"] = np.ascontiguousarray(shared["w_in_ab"][:, :, 1536:1552])
    nc = build_program()
    in_maps = []
    for b in range(nb):
        m = dict(shared)
        m["x"] = x[b]
        in_maps.append(m)
    res = run_bass_kernel_spmd(nc, in_maps, core_ids=list(range(nb)))
    return np.stack([np.asarray(r["out"], dtype=np.float32) for r in res.results], axis=0)
```

```python
import numpy as np
import concourse.bass as bass
import concourse.mybir as mybir
from concourse.bass_utils import run_bass_kernel_spmd

F32 = mybir.dt.float32
F32R = mybir.dt.float32r
F16 = mybir.dt.float16
BF16 = mybir.dt.bfloat16
I32 = mybir.dt.int32
AF = mybir.ActivationFunctionType
ALU = mybir.AluOpType
AX = mybir.AxisListType

ENGS = ("pe", "dve", "act", "pool", "sp")
SEM_CAP = 30000
DMA_K = 6


class Acc:
    __slots__ = ("ap", "cells")

    def __init__(self, ap, cells):
        self.ap = ap
        self.cells = cells

    def v(self, fn):
        return Acc(fn(self.ap), self.cells)

    def r(self):
        return Acc(self.ap.bitcast(F32R), self.cells)

    def bc(self, shape):
        return Acc(self.ap.to_broadcast(shape), self.cells)


class TT:
    def __init__(self, prog, name, handle, shape, dtype, blk):
        self.prog = prog
        self.name = name
        self.h = handle
        self.shape = list(shape)
        self.dtype = dtype
        self.blk = blk
        self.fstr = []
        s = 1
        for d in reversed(self.shape[1:]):
            self.fstr.insert(0, s)
            s *= d
        self.fsize = s

    def cells_of(self, idx):
        rngs = [(0, 0)]
        fd = self.shape[1:]
        idx = list(idx) + [slice(None)] * (len(self.shape) - len(idx))
        dims = []
        for i, d in enumerate(fd):
            ix = idx[i + 1]
            if isinstance(ix, int):
                dims.append((ix, ix + 1))
            else:
                a = 0 if ix.start is None else ix.start
                b = d if ix.stop is None else ix.stop
                assert ix.step is None
                dims.append((a, b))
        starts = [0]
        n = len(fd)
        tail = n
        while tail > 0 and dims[tail - 1] == (0, fd[tail - 1]):
            tail -= 1
        if tail == 0:
            return {(self.name, c) for c in range(0, (self.fsize - 1) // self.blk + 1)}
        outer = dims[: tail - 1]
        a, b = dims[tail - 1]
        st = self.fstr[tail - 1]
        starts = [0]
        for (lo, hi), s in zip(outer, self.fstr[: tail - 1]):
            starts = [x + k * s for x in starts for k in range(lo, hi)]
        cells = set()
        for x in starts:
            s0 = x + a * st
            e0 = x + b * st
            for c in range(s0 // self.blk, (e0 - 1) // self.blk + 1):
                cells.add((self.name, c))
        return cells

    def __getitem__(self, idx):
        if not isinstance(idx, tuple):
            idx = (idx,)
        return Acc(self.h[idx], self.cells_of(idx))

    def all(self):
        return self[tuple([slice(None)] * len(self.shape))]


class Op:
    __slots__ = ("eng", "fn", "rc", "wc", "dma", "deps", "signal", "ev", "note")

    def __init__(self, eng, fn, rc, wc, dma=False, note=""):
        self.eng = eng
        self.fn = fn
        self.rc = rc
        self.wc = wc
        self.dma = dma
        self.deps = ()
        self.signal = False
        self.ev = None
        self.note = note


class Prog:
    def __init__(self, nc):
        self.nc = nc
        self.ops = []
        self.stack = []
        self._uid = 0
        self.eobj = {"pe": nc.tensor, "dve": nc.vector, "act": nc.scalar, "pool": nc.gpsimd, "sp": nc.sync}

    def sb(self, name, shape, dtype=F32, blk=None):
        self._uid += 1
        nm = f"{name}_{self._uid}"
        cm = self.nc.sbuf_tensor(nm, list(shape), dtype)
        h = cm.__enter__()
        self.stack.append(cm)
        if blk is None:
            blk = shape[-1]
        return TT(self, nm, h, shape, dtype, blk)

    def ps(self, name, shape, dtype=F32, blk=None):
        self._uid += 1
        nm = f"{name}_{self._uid}"
        cm = self.nc.psum_tensor(nm, list(shape), dtype)
        h = cm.__enter__()
        self.stack.append(cm)
        blk = 2048 // mybir.dt.size(dtype)
        return TT(self, "PS:" + nm, h, shape, dtype, blk)

    def mark(self):
        return len(self.stack)

    def release(self, mark):
        self.barrier()
        while len(self.stack) > mark:
            cm = self.stack.pop()
            cm.__exit__(None, None, None)

    def barrier(self):
        self.ops.append(Op("all", None, set(), set(), note="barrier"))

    def add(self, eng, fn, reads, writes, dma=False, note=""):
        rc = set()
        for a in reads:
            if a is not None and isinstance(a, Acc):
                rc |= a.cells
        wc = set()
        for a in writes:
            if a is not None and isinstance(a, Acc):
                wc |= a.cells
        self.ops.append(Op(eng, fn, rc, wc, dma, note))

    @staticmethod
    def _ap(a):
        return a.ap if isinstance(a, Acc) else a

    def mm(self, out, lhsT, rhs, start=True, stop=True, **kw):
        o, l, r = out.ap, lhsT.ap, rhs.ap
        self.add("pe", lambda e: e.matmul(o, l, r, start=start, stop=stop, **kw), [lhsT, rhs], [out])

    def transpose(self, out, in_, ident):
        o, i, d = out.ap, in_.ap, ident.ap
        self.add("pe", lambda e: e.transpose(o, i, d), [in_, ident], [out])

    def act(self, out, in_, func, bias=None, scale=1.0, accum_out=None, eng="act"):
        o, i = out.ap, in_.ap
        b = self._ap(bias)
        s = self._ap(scale)
        kw = {}
        if b is not None:
            kw["bias"] = b
        if accum_out is not None:
            kw["accum_out"] = accum_out.ap
        self.add(eng, lambda e: e.activation(o, i, func, scale=s, **kw),
                 [in_, bias, scale], [out, accum_out])

    def tt(self, out, in0, in1, op, eng="dve"):
        o, a, b = out.ap, in0.ap, in1.ap
        self.add(eng, lambda e: e.tensor_tensor(o, a, b, op), [in0, in1], [out])

    def ts(self, out, in0, s1, op0, s2=None, op1=None, accum_out=None, eng="dve"):
        o, a = out.ap, in0.ap
        if eng == "pool" and op1 is None and op0 == ALU.mult:
            op1, s2 = ALU.mult, 1.0
        x1 = self._ap(s1)
        x2 = self._ap(s2)
        kw = {}
        if op1 is not None:
            kw["op1"] = op1
        if accum_out is not None:
            kw["accum_out"] = accum_out.ap
        self.add(eng, lambda e: e.tensor_scalar(o, a, x1, x2, op0, **kw), [in0, s1, s2], [out, accum_out])

    def stt(self, out, in0, scalar, in1, op0, op1, eng="dve"):
        o, a, b = out.ap, in0.ap, in1.ap
        s = self._ap(scalar)
        self.add(eng, lambda e: e.scalar_tensor_tensor(o, a, s, b, op0, op1), [in0, scalar, in1], [out])

    def copy(self, out, in_, eng="dve"):
        o, i = out.ap, in_.ap
        if eng == "act":
            self.add(eng, lambda e: e.copy(o, i), [in_], [out])
        else:
            self.add(eng, lambda e: e.tensor_copy(o, i), [in_], [out])

    def memset(self, out, val, eng="dve"):
        o = out.ap
        self.add(eng, lambda e: e.memset(o, val), [], [out])

    def reduce(self, out, in_, op, axis=AX.X, eng="dve"):
        o, i = out.ap, in_.ap
        self.add(eng, lambda e: e.tensor_reduce(o, i, axis, op), [in_], [out])

    def recip(self, out, in_):
        o, i = out.ap, in_.ap
        self.add("dve", lambda e: e.reciprocal(o, i), [in_], [out])

    def dma(self, out, in_, q="sp", slow=False):
        o = self._ap(out)
        i = self._ap(in_)
        if slow:
            self.add(q, lambda e: e.dma_start(out=o, in_=i, allow_slow_non_contiguous=True), [in_], [out], dma=True)
        else:
            self.add(q, lambda e: e.dma_start(out=o, in_=i), [in_], [out], dma=True)

    def generic(self, eng, fn, reads, writes):
        self.add(eng, fn, reads, writes)

    def emit(self):
        nc = self.nc
        ops = self.ops
        last_w = {}
        readers = {}
        for i, op in enumerate(ops):
            if op.eng == "all":
                continue
            deps = set()
            for c in op.rc:
                w = last_w.get(c)
                if w is not None:
                    deps.add(w)
                if c[0].startswith("PS:"):
                    rd = readers.get(c)
                    if rd:
                        for kk, vv in rd.items():
                            if kk != op.eng:
                                deps.add(vv)
            for c in op.wc:
                w = last_w.get(c)
                if w is not None:
                    deps.add(w)
                rd = readers.get(c)
                if rd:
                    deps.update(rd.values())
            deps.discard(i)
            for c in op.wc:
                last_w[c] = i
                readers[c] = {}
            key = ("d", i) if op.dma else op.eng
            for c in op.rc:
                if c in op.wc:
                    continue
                readers.setdefault(c, {})[key] = i
            best = {}
            keep = []
            for d in deps:
                od = ops[d]
                if od.dma:
                    keep.append(d)
                else:
                    if od.eng == "pe" and op.eng == "pe" and not op.dma:
                        continue
                    if best.get(od.eng, -1) < d:
                        best[od.eng] = d
            keep.extend(best.values())
            op.deps = keep
            for d in keep:
                ops[d].signal = True
        last_on = {}
        bar_deps = {}
        for i, op in enumerate(ops):
            if op.eng == "all":
                bar_deps[i] = dict(last_on)
                for d in last_on.values():
                    ops[d].signal = True
            elif op.dma:
                op.signal = True
                last_on[("d", i)] = i
            else:
                last_on[op.eng] = i
        final = dict(last_on)
        for d in final.values():
            ops[d].signal = True

        sems = {}
        semctx = []

        def new_sem(nm):
            cm = nc.semaphore(nm)
            h = cm.__enter__()
            semctx.append(cm)
            return h

        cnt = {e: 0 for e in ENGS}
        eng_sems = {e: [] for e in ENGS}
        dma_cnt = {e: 0 for e in ENGS}
        dma_sems = {e: [] for e in ENGS}
        for i, op in enumerate(ops):
            if op.eng == "all" or not op.signal:
                continue
            if op.dma:
                q = op.eng
                n = dma_cnt[q]
                dma_cnt[q] += 1
                if len(dma_sems[q]) < DMA_K:
                    dma_sems[q].append(new_sem(f"dq_{q}_{len(dma_sems[q])}"))
                op.ev = (dma_sems[q][n % DMA_K], 16 * (n // DMA_K + 1), 16)
            else:
                e = op.eng
                n = cnt[e]
                cnt[e] += 1
                si = n // SEM_CAP
                if len(eng_sems[e]) <= si:
                    eng_sems[e].append(new_sem(f"c_{e}_{si}"))
                op.ev = (eng_sems[e][si], n % SEM_CAP + 1, 1)

        waited = {e: {} for e in ENGS}

        def wait(e, ev):
            sem, val = ev[0], ev[1]
            k = id(sem)
            if waited[e].get(k, 0) >= val:
                return
            waited[e][k] = val
            self.eobj[e].wait_ge(sem, val)

        self.n_emitted = {e: 0 for e in ENGS}
        for i, op in enumerate(ops):
            if op.eng == "all":
                for e in ENGS:
                    for d in bar_deps[i].values():
                        if ops[d].ev is not None:
                            wait(e, ops[d].ev)
                continue
            e = op.eng
            for d in op.deps:
                wait(e, ops[d].ev)
            if op.dma:
                sem, val, inc = op.ev
                if val > 16:
                    wait(e, (sem, val - 16))
            inst = op.fn(self.eobj[e])
            self.n_emitted[e] += 1
            if op.signal:
                inst.then_inc(op.ev[0], op.ev[2])
        for d in final.values():
            if ops[d].ev is not None:
                wait("sp", ops[d].ev)
        self._semctx = semctx

    def close(self):
        while self.stack:
            self.stack.pop().__exit__(None, None, None)
        for cm in reversed(getattr(self, "_semctx", [])):
            cm.__exit__(None, None, None)


D = 1024
L = 2048
NT = 16
NK = 8
ALPHA = 8.0 ** 0.25
LN_EPS = 1e-5
NORM_EPS = 1e-6
NEG = -1.0e30

C_ID, C_IOTA, C_UF, C_UB, C_NMF, C_NMB, C_COS, C_SIN, C_G, C_END = 0, 128, 384, 512, 640, 768, 896, 1408, 1920, 2048


def make_consts():
    c = np.zeros((128, C_END), np.float32)
    c[:, C_ID:C_ID + 128] = np.eye(128, dtype=np.float32)
    c[:, C_IOTA:C_IOTA + 256] = np.arange(256, dtype=np.float32)[None, :]
    k = np.arange(128)[:, None]
    i = np.arange(128)[None, :]
    c[:, C_UF:C_UF + 128] = (k <= i).astype(np.float32)
    c[:, C_UB:C_UB + 128] = (k >= i).astype(np.float32)
    c[:, C_NMF:C_NMF + 128] = np.where(i >= k, 0.0, NEG)
    c[:, C_NMB:C_NMB + 128] = np.where(i <= k, 0.0, NEG)
    t = np.arange(L)
    row = (t // 64).astype(np.float32)
    col = (t % 64).astype(np.float32)
    freqs = (10000.0 ** (-np.arange(0, 32, 2, dtype=np.float32) / 32)).astype(np.float32)
    ang = np.concatenate([row[:, None] * freqs, col[:, None] * freqs], axis=-1).astype(np.float32)
    cos = np.cos(ang).astype(np.float32).reshape(NT, 128, 32).transpose(1, 0, 2).reshape(128, 512)
    sin = np.sin(ang).astype(np.float32).reshape(NT, 128, 32).transpose(1, 0, 2).reshape(128, 512)
    c[:, C_COS:C_COS + 512] = cos
    c[:, C_SIN:C_SIN + 512] = sin
    pp = np.arange(128)
    c[:, C_G:C_G + 128] = (pp[:, None] % 16 == pp[None, :] % 16).astype(np.float32)
    return c


class Ctx:
    pass


def setup(P, nc, dram):
    G = Ctx()
    G.nc = nc
    G.P = P
    G.d = dram
    G.X = P.sb("X", [128, NK, L], F32, blk=128)
    G.PB = [P.ps(f"pb{i}", [128, 512], F32, blk=128) for i in range(8)]
    G.cst = P.sb("cst", [128, C_END], F32, blk=128)
    P.dma(G.cst.all(), dram["consts"])
    G.ones = P.sb("ones", [128, 128], F32)
    P.memset(G.ones.all(), 1.0, eng="pool")
    G.onesD = P.sb("onesD", [128, 128], F32)
    P.ts(G.onesD.all().r(), G.ones.all(), 1.0 / D, ALU.mult)
    G.id16 = P.sb("id16", [128, 128], F16)
    P.copy(G.id16.all(), G.cst[:, C_ID:C_ID + 128])
    G.ident = G.cst[:, C_ID:C_ID + 128]
    G.eps_ln = P.sb("epsln", [128, 1], F32)
    P.memset(G.eps_ln.all(), LN_EPS, eng="pool")
    G.rr = 0
    return G


def evac_eng(G):
    G.rr += 1
    return "act" if G.rr % 2 else "dve"


def load_x(G, xd):
    P, X, PB = G.P, G.X, G.PB
    m = P.mark()
    tmp = [P.sb(f"ldx{i}", [128, D], F32, blk=128) for i in range(2)]
    for i in range(NT):
        tb = tmp[i % 2]
        P.dma(tb.all(), xd[i * 128:(i + 1) * 128, :])
        for kk in range(2):
            pb = PB[(2 * i + kk) % 2]
            for j in range(4):
                k = kk * 4 + j
                P.transpose(pb[:, j * 128:(j + 1) * 128], tb[:, k * 128:(k + 1) * 128], G.ident)
            dst = X[:, kk * 4:(kk + 1) * 4, i * 128:(i + 1) * 128].r()
            src = pb.all().v(lambda a: a.rearrange("p (j t) -> p j t", j=4))
            P.copy(dst, src, eng=evac_eng(G))
    P.release(m)


def store_x(G, od):
    P, X, PB = G.P, G.X, G.PB
    m = P.mark()
    tmp = [P.sb(f"stx{i}", [128, D], F32, blk=128) for i in range(2)]
    for i in range(NT):
        tb = tmp[i % 2]
        for kk in range(2):
            pb = PB[(2 * i + kk) % 2]
            for j in range(4):
                k = kk * 4 + j
                P.transpose(pb[:, j * 128:(j + 1) * 128], X[:, k, i * 128:(i + 1) * 128], G.ident)
            P.copy(tb[:, kk * 512:(kk + 1) * 512], pb.all(), eng=evac_eng(G))
        P.dma(od[i * 128:(i + 1) * 128, :], tb.all())
    P.release(m)


def layer_norm(G, gd, bd, eps=LN_EPS):
    P, X, PB = G.P, G.X, G.PB
    m = P.mark()
    gb = P.sb("ln_gb", [128, 2, NK], F32)
    P.dma(gb[:, 0, :], gd.rearrange("(k p) -> p k", p=128), slow=True)
    P.dma(gb[:, 1, :], bd.rearrange("(k p) -> p k", p=128), slow=True)
    sq = [P.sb(f"ln_sq{i}", [128, 512], F32) for i in range(2)]
    epst = P.sb("ln_eps", [128, 1], F32)
    P.memset(epst.all(), eps, eng="pool")
    m2 = [P.sb(f"ln_m2{i}", [128, 512], F32) for i in range(2)]
    rs = [P.sb(f"ln_rs{i}", [128, 512], F32) for i in range(2)]
    tmp = [P.sb(f"ln_t{i}", [128, 512], F32) for i in range(8)]
    def stats(tb):
        ts_ = slice(tb * 512, (tb + 1) * 512)
        o = 3 * (tb % 2)
        ps_s, ps_q, ps_r = PB[o], PB[o + 1], PB[o + 2]
        for k in range(NK):
            P.mm(ps_s.all(), G.onesD.all().r(), X[:, k, ts_].r(), start=(k == 0), stop=(k == NK - 1))
        for k in range(NK):
            s_ = sq[k % 2]
            if k % 2 == 0:
                P.act(s_.all().r(), X[:, k, ts_], AF.Square)
            else:
                P.tt(s_.all().r(), X[:, k, ts_], X[:, k, ts_], ALU.mult)
            P.mm(ps_q.all(), G.onesD.all().r(), s_.all().r(), start=(k == 0), stop=(k == NK - 1))
        m2_, rs_ = m2[tb % 2], rs[tb % 2]
        P.act(m2_.all(), ps_s.all(), AF.Square)
        P.tt(rs_.all(), ps_q.all(), m2_.all(), ALU.subtract)
        P.act(rs_.all(), rs_.all(), AF.Sqrt, bias=epst.all(), scale=1.0)
        P.recip(rs_.all(), rs_.all())

    def norm(tb):
        ts_ = slice(tb * 512, (tb + 1) * 512)
        o = 3 * (tb % 2)
        ps_s = PB[o]
        rs_ = rs[tb % 2]
        for k in range(NK):
            t = tmp[k % 8]
            P.tt(t.all(), X[:, k, ts_], ps_s.all(), ALU.subtract)
            P.tt(t.all(), t.all(), rs_.all(), ALU.mult, eng="pool")
            P.act(X[:, k, ts_].r(), t.all(), AF.Identity, bias=gb[:, 1, k:k + 1], scale=gb[:, 0, k:k + 1])

    stats(0)
    for tb in range(4):
        if tb + 1 < 4:
            stats(tb + 1)
        norm(tb)
    P.release(m)


NE = 16
CAP = 256
FF = 2048
FB = 512
NFB = FF // FB


def moe_phase(G, l, upto=99):
    P, X, PB, d = G.P, G.X, G.PB, G.d
    m0 = P.mark()
    x16 = P.sb("x16", [128, NT, D], F16, blk=128)
    pos_tok = P.sb("pos_tok", [128, NT, NE], F32, blk=NT * NE)
    mask_tok = P.sb("mask_tok", [128, NT, NE], F32, blk=NT * NE)
    gm_tok = P.sb("gm_tok", [128, NT, NE], F32, blk=NT * NE)
    m1 = P.mark()
    rw = P.sb("rw", [128, NK, NE], F32)
    P.dma(rw.all(), d["router_w"][l].rearrange("(k p) e -> p k e", p=128))
    lg = PB[2]
    for i in range(NT):
        for k in range(NK):
            P.mm(lg[:, i * NE:(i + 1) * NE], X[:, k, i * 128:(i + 1) * 128], rw[:, k, :],
                 start=(k == 0), stop=(k == NK - 1))
    v3 = lambda a: a.rearrange("p (i e) -> p i e", e=NE)
    mx = P.sb("r_mx", [128, NT], F32)
    sh = P.sb("r_sh", [128, NT * NE], F32)
    aff = P.sb("r_aff", [128, NT * NE], F32)
    P.reduce(mx.all(), lg[:, 0:NT * NE].v(v3), ALU.max)
    P.tt(sh.all().v(v3), lg[:, 0:NT * NE].v(v3), mx.all().v(lambda a: a.unsqueeze(2).to_broadcast([128, NT, NE])), ALU.subtract)
    P.act(sh.all(), sh.all(), AF.Exp)
    P.reduce(mx.all(), sh.all().v(v3), ALU.add)
    P.recip(mx.all(), mx.all())
    P.tt(aff.all().v(v3), sh.all().v(v3), mx.all().v(lambda a: a.unsqueeze(2).to_broadcast([128, NT, NE])), ALU.mult)
    A = P.sb("r_A", [128, 256], F32)
    pa = PB[4]
    for half in range(2):
        P.transpose(pa[:, half * 128:(half + 1) * 128], aff[:, half * 128:(half + 1) * 128], G.ident)
    P.copy(A.all(), pa[:, 0:256], eng="act")
    for i in range(NT):
        for kk in range(2):
            pb = PB[(2 * i + kk) % 2]
            for j in range(4):
                k = kk * 4 + j
                P.transpose(pb[:, j * 128:(j + 1) * 128], X[:, k, i * 128:(i + 1) * 128], G.ident)
            P.copy(x16[:, i, kk * 512:(kk + 1) * 512], pb.all(), eng="act")
    lo = [P.sb(f"r_lo{i}", [128, 1], F32) for i in range(2)]
    cand = P.sb("r_cand", [128, 1], F32)
    cseg = P.sb("r_cseg", [128, 2], F32)
    ss = P.sb("r_ss", [128, 1], F32)
    junk = P.sb("r_junk", [128, 256], F32)
    P.memset(lo[0].all(), 0.0)
    P.memset(cseg.all(), 0.0)
    P.memset(cand.all(), 0.5)
    Gm = G.cst[:, C_G:C_G + 128]
    NIT = 30
    for it in range(NIT):
        step = 0.5 ** (it + 1)
        lo_o, lo_n = lo[it % 2], lo[(it + 1) % 2]
        P.ts(junk.all(), A.all(), cand[:, 0:1], ALU.is_ge, None, ALU.add, accum_out=cseg[:, 0:1])
        pcn = PB[5 + it % 2]
        P.mm(pcn[:, 0:2], Gm, cseg.all())
        P.ts(ss.all(), pcn[:, 0:1], CAP - 0.5, ALU.is_ge, step, ALU.mult)
        P.tt(lo_n.all(), lo_o.all(), ss.all(), ALU.add)
        if it + 1 < NIT:
            P.ts(cand.all(), lo_n.all(), 0.5 ** (it + 2), ALU.add)
    thr = lo[NIT % 2]
    maskA = P.sb("r_maskA", [128, 256], F32)
    P.ts(maskA.all(), A.all(), thr[:, 0:1], ALU.is_ge)
    pm = PB[2]
    for half in range(2):
        P.transpose(pm[:, half * 128:(half + 1) * 128], maskA[:, half * 128:(half + 1) * 128], G.ident)
    P.copy(mask_tok.all().v(lambda a: a.rearrange("p i e -> p (i e)")), pm[:, 0:256], eng="act")
    mb16 = P.sb("r_mb16", [128, 256], BF16)
    P.copy(mb16.all(), pm[:, 0:256], eng="dve")
    ust = P.sb("r_ust", [128, 128], BF16)
    one16 = P.sb("r_one16", [128, 128], BF16)
    P.tt(ust.all(), G.cst[:, C_UF:C_UF + 128], G.ident, ALU.subtract)
    P.memset(one16.all(), 1.0)
    pw_, pt_ = PB[3], PB[7]
    P.mm(pw_[:, 0:256], ust.all(), mb16.all())
    P.mm(pt_[:, 0:256], one16.all(), mb16.all())
    ca = P.sb("r_ca", [128, 256], F32)
    cb_ = P.sb("r_cb", [128, 256], F32)
    P.copy(ca.all(), pt_[:, 0:256], eng="act")
    src_, dst_ = ca, cb_
    for sft in (16, 32, 64, 128):
        P.copy(dst_[:, 0:sft], src_[:, 0:sft], eng="pool")
        P.tt(dst_[:, sft:256], src_[:, sft:256], src_[:, 0:256 - sft], ALU.add)
        src_, dst_ = dst_, src_
    pflat = pos_tok.all().v(lambda a: a.rearrange("p i e -> p (i e)"))
    P.copy(Acc(pflat.ap[:, 0:16], pflat.cells), pw_[:, 0:16], eng="act")
    P.tt(Acc(pflat.ap[:, 16:256], pflat.cells), pw_[:, 16:256], src_[:, 0:240], ALU.add)
    P.stt(gm_tok.all().v(lambda a: a.rearrange("p i e -> p (i e)")), aff.all(), 1.0 / ALPHA, pm[:, 0:256], ALU.mult, ALU.mult)
    P.release(m1)
    if upto < 2:
        P.release(m0)
        return

    Wb = [[P.sb(f"w1b{i}", [128, NK, FB], F16, blk=FB), P.sb(f"w3b{i}", [128, NK, FB], F16, blk=FB),
           P.sb(f"w2b{i}", [128, FB // 128, D], F16, blk=D)] for i in range(2)]
    Se = P.sb("Se", [128, NT, 2 * CAP], F16, blk=CAP)
    S2 = P.sb("S2", [128, NT, CAP], F16, blk=CAP)
    SeT = P.sb("SeT", [128, 2, 2, L], F16, blk=512)
    xsT = P.sb("xsT", [128, NK, 2 * CAP], F16, blk=CAP)
    hT = [P.sb(f"hT{i}", [128, CAP], F16) for i in range(2)]
    sl = [P.sb(f"sl{i}", [128, CAP], F32) for i in range(2)]
    ysb = P.sb("ysb", [128, 2, 2, D], F16, blk=512)
    iota = G.cst[:, C_IOTA:C_IOTA + CAP]
    w1d, w3d, w2d = d["moe_w1"], d["moe_w3"], d["moe_w2"]
    SUB, NEX = 9, NE
    G.nblk = 0
    for e in range(NEX):
        ej = e % 2
        cofs = ej * CAP
        if ej == 0:
            for i in range(NT):
                for jj in range(2):
                    P.ts(Se[:, i, jj * CAP:(jj + 1) * CAP], iota, pos_tok[:, i, e + jj:e + jj + 1], ALU.is_equal,
                         mask_tok[:, i, e + jj:e + jj + 1], ALU.mult)
            for k in range(NK):
                pb = PB[4 + (k % 2)]
                for i in range(NT):
                    P.mm(pb.all(), x16[:, i, k * 128:(k + 1) * 128], Se[:, i, :], start=(i == 0), stop=(i == NT - 1))
                P.copy(xsT[:, k, :], pb.all(), eng=evac_eng(G))
        trp = [PB[6], PB[7]]
        for i in range(NT):
            s2 = S2
            P.ts(s2[:, i, :], Se[:, i, cofs:cofs + CAP], gm_tok[:, i, e:e + 1], ALU.mult)
            for cc in range(2):
                pbv = trp[cc].all().v(lambda a: a.bitcast(F16))
                col = (i % 8) * 128
                dst = Acc(pbv.ap[:, col:col + 128], pbv.cells)
                P.transpose(dst, s2[:, i, cc * 128:(cc + 1) * 128], G.id16.all())
            if i % 8 == 7:
                for cc in range(2):
                    pbv = trp[cc].all().v(lambda a: a.bitcast(F16))
                    h0 = (i // 8) * 1024
                    P.copy(SeT[:, ej, cc, h0:h0 + 1024], pbv, eng=("act" if cc else "dve"))
        if SUB < 2:
            continue
        yps = [PB[0], PB[1], PB[2], PB[3]]
        NFC = FF // 128
        blocks = {}

        def h_stage(fc):
            fb, jj = divmod(fc, FB // 128)
            if jj == 0:
                W1, W3, W2 = Wb[G.nblk % 2]
                G.nblk += 1
                f0 = fb * FB
                P.dma(W1.all(), w1d[l, e].rearrange("(k p) f -> p k f", p=128)[:, :, f0:f0 + FB], q="pool")
                P.dma(W3.all(), w3d[l, e].rearrange("(k p) f -> p k f", p=128)[:, :, f0:f0 + FB], q="pool")
                P.dma(W2.all(), w2d[l, e, f0:f0 + FB, :].rearrange("(j p) n -> p j n", p=128), q="pool")
                blocks[fb] = (W1, W3, W2)
            W1, W3, W2 = blocks[fb]
            pb = PB[4 + (fc % 2)]
            for k in range(NK):
                P.mm(pb[:, 0:CAP], W1[:, k, jj * 128:(jj + 1) * 128], xsT[:, k, cofs:cofs + CAP], start=(k == 0), stop=(k == NK - 1))
            for k in range(NK):
                P.mm(pb[:, CAP:2 * CAP], W3[:, k, jj * 128:(jj + 1) * 128], xsT[:, k, cofs:cofs + CAP], start=(k == 0), stop=(k == NK - 1))
            s_, h_ = sl[fc % 2], hT[fc % 2]
            P.act(s_.all(), pb[:, 0:CAP], AF.Silu)
            P.tt(h_.all(), s_.all(), pb[:, CAP:2 * CAP], ALU.mult)

        def w2_stage(fc):
            fb, jj = divmod(fc, FB // 128)
            W2 = blocks[fb][2]
            h_ = hT[fc % 2]
            for cc in range(2):
                for dh in range(2):
                    P.mm(yps[cc * 2 + dh].all(), h_[:, cc * 128:(cc + 1) * 128], W2[:, jj, dh * 512:(dh + 1) * 512],
                         start=(fc == 0), stop=(fc == NFC - 1))

        h_stage(0)
        for fc in range(NFC):
            if fc + 1 < NFC:
                h_stage(fc + 1)
            w2_stage(fc)
        for cc in range(2):
            for dh in range(2):
                P.copy(ysb[:, ej, cc, dh * 512:(dh + 1) * 512], yps[cc * 2 + dh].all(), eng=evac_eng(G))
        if SUB < 3 or ej == 0:
            continue
        for k in range(NK):
            for tb in range(4):
                pb = PB[(k * 4 + tb) % 4]
                n = 0
                for e2 in range(2):
                    for cc in range(2):
                        P.mm(pb.all(), ysb[:, e2, cc, k * 128:(k + 1) * 128], SeT[:, e2, cc, tb * 512:(tb + 1) * 512],
                             start=(n == 0), stop=(n == 3))
                        n += 1
                xs_ = X[:, k, tb * 512:(tb + 1) * 512]
                P.tt(xs_.r(), pb.all(), xs_, ALU.add)
    P.release(m0)
    if upto < 3:
        return
    layer_norm(G, d["ln2_g"][l], d["ln2_b"][l], eps=LN_EPS / (ALPHA * ALPHA))


U32 = mybir.dt.uint32


def wcols(wd, c0, n):
    return wd.rearrange("(k p) c -> p k c", p=128)[:, :, c0:c0 + n]


def out_proj_ln1(G, wo_d, yT, g_d, b_d):
    P, X, PB = G.P, G.X, G.PB
    m = P.mark()
    wst = [P.sb(f"wo_st{i}", [128, NK, 128], F32, blk=128) for i in range(2)]
    w16 = [P.sb(f"wo_16{i}", [128, NK, 128], F16, blk=128) for i in range(2)]
    for kd in range(NK):
        st, wh = wst[kd % 2], w16[kd % 2]
        P.dma(st.all(), wo_d.rearrange("(k p) c -> p k c", p=128)[:, :, kd * 128:(kd + 1) * 128])
        P.copy(wh.all(), st.all(), eng="act")
        for tb in range(4):
            pb = PB[(kd * 4 + tb) % 2]
            for kc in range(NK):
                P.mm(pb.all(), wh[:, kc, :], yT[:, kc, tb * 512:(tb + 1) * 512], start=(kc == 0), stop=(kc == NK - 1))
            xs_ = X[:, kd, tb * 512:(tb + 1) * 512]
            P.stt(xs_.r(), xs_, ALPHA, pb.all(), ALU.mult, ALU.add)
    P.release(m)
    layer_norm(G, g_d, b_d)


POOL_W = (2, 4, 8, 16)
HQ = 128
NCH = L // HQ


def mixer_cd(G, l):
    P, X, PB, d = G.P, G.X, G.PB, G.d
    j = l // 2
    wd = d["w_in_cd"][j]
    m0 = P.mark()
    yT = P.sb("yT", [128, NK, L], F16, blk=128)
    wbuf = [P.sb(f"wcd{i}", [128, NK, 128], F32, blk=128) for i in range(3)]
    G.wrr = 0

    def load_w(c0):
        w = wbuf[G.wrr % 3]
        G.wrr += 1
        P.dma(w.all().r(), wcols(wd, c0, 128))
        return w

    def proj_fm(w, tb, pb):
        for k in range(NK):
            P.mm(pb.all(), w[:, k, :].r(), X[:, k, tb * 512:(tb + 1) * 512].r(), start=(k == 0), stop=(k == NK - 1))

    m1 = P.mark()
    HB = 16
    ub = P.sb("pl_u", [128, L + 2 * HB], F32, blk=L + 2 * HB)
    pa = P.sb("pl_a", [128, L + 2 * HB], F32, blk=L + 2 * HB)
    pbuf = P.sb("pl_b", [128, L + 2 * HB], F32, blk=L + 2 * HB)
    pooled = P.sb("pl_p", [128, L], F32, blk=512)
    pw = P.sb("pl_w", [128, 4, 128], F32, blk=128)
    psc = P.sb("pl_sc", [128, 4], F32)
    P.dma(pw.all().r(), d["pool_w"][j].rearrange("g c d -> c g d"))
    P.dma(psc.all(), d["pool_scale"][j].rearrange("(g p) -> p g", p=128), slow=True)
    for t_ in (ub, pa, pbuf):
        P.memset(t_.all(), 0.0, eng="pool")
    for gi, w_ in enumerate(POOL_W):
        wt = load_w(gi * 128)
        for tb in range(4):
            pb = PB[tb % 2]
            proj_fm(wt, tb, pb)
            P.copy(ub[:, HB + tb * 512:HB + (tb + 1) * 512], pb.all(), eng=evac_eng(G))
        src = ub
        bufs = [pa, pbuf]
        lo, hi = HB - 8, HB + L + 8
        for s in range(gi + 1):
            dst = bufs[s % 2]
            if s == 0:
                P.tt(dst[:, lo:hi], src[:, lo - 1:hi - 1], src[:, lo:hi], ALU.add)
            else:
                sh = 1 << (s - 1)
                P.tt(dst[:, lo:hi], src[:, lo - sh:hi - sh], src[:, lo + sh:hi + sh], ALU.add)
            src = dst
        P.stt(pooled.all().r(), src[:, HB:HB + L], 1.0 / w_, ub[:, HB:HB + L], ALU.mult, ALU.subtract)
        for t in list(range(0, w_ // 2)) + list(range(L - w_ // 2 + 1, L)):
            cnt = min(t - w_ // 2 + w_, L) - max(t - w_ // 2, 0)
            P.stt(pooled[:, t:t + 1].r(), src[:, HB + t:HB + t + 1], 1.0 / cnt, ub[:, HB + t:HB + t + 1], ALU.mult, ALU.subtract)
        for tb in range(4):
            pb = PB[2 + tb % 2]
            P.mm(pb.all(), pw[:, gi, :].r(), pooled[:, tb * 512:(tb + 1) * 512].r())
            P.act(yT[:, gi, tb * 512:(tb + 1) * 512], pb.all(), AF.Identity, scale=psc[:, gi:gi + 1])
    P.release(m1)

    m1 = P.mark()
    lbl = P.sb("lb_l", [128, 4, 4], F32)
    P.dma(lbl.all(), d["hgrn_lb_logits"].rearrange("l (h p) -> p l h", p=128), slow=True)
    lbv = P.sb("lb_v", [128, 4], F32)
    oml = P.sb("lb_om", [128, 4], F32)
    lsum = P.sb("lb_s", [128, 4], F32)
    P.act(lbl.all(), lbl.all(), AF.Exp)
    P.reduce(lsum.all(), lbl.all().v(lambda a: a.rearrange("p l h -> p h l")), ALU.add)
    P.recip(lsum.all(), lsum.all())
    P.copy(lbv.all(), lbl[:, 1, :])
    for ll in range(2, l + 1):
        P.tt(lbv.all(), lbv.all(), lbl[:, ll, :], ALU.add)
    P.tt(lbv.all(), lbv.all(), lsum.all(), ALU.mult)
    P.ts(oml.all(), lbv.all(), -1.0, ALU.mult, 1.0, ALU.add)
    rmask = P.sb("hg_rm", [128, L], BF16, blk=L)
    P.memset(rmask.all(), 1.0, eng="pool")
    P.memset(rmask.all().v(lambda a: a.rearrange("p (c q) -> p c q", q=HQ)[:, :, 0:1]), 0.0, eng="pool")
    nwp = P.sb("hg_nwp", [128, 4], F32)
    P.dma(nwp.all(), d["hgrn_norm_w"][j].rearrange("(h p) -> p h", p=128), slow=True)
    qT = P.sb("hg_qT", [128, L], F32, blk=512)
    fb_ = P.sb("hg_f", [128, L], F32, blk=L)
    lf = P.sb("hg_lf", [128, L], F32, blk=L)
    cum = P.sb("hg_cum", [128, L], F32, blk=L)
    qt_ = [P.sb(f"hg_qt{i}", [128, L], F32, blk=HQ) for i in range(2)]
    kt_ = [P.sb(f"hg_kt{i}", [128, L], F32, blk=HQ) for i in range(2)]
    esc = [[P.sb(f"hg_es{i}{n}", [128, NCH], F32) for n in range(3)] for i in range(2)]
    vtok = P.sb("hg_v", [128, NT, 128], F32, blk=128)
    sgT = P.sb("hg_sgT", [128, L], F16, blk=128)
    Sbst = P.sb("hg_Sb", [128, NCH, 128], F32, blk=128)
    Sst = [P.sb(f"hg_S{i}", [128, 128], F32) for i in range(2)]
    Stl = [P.sb(f"hg_St{i}", [128, 128], F32) for i in range(2)]
    ktok = [P.sb(f"hg_ktok{i}", [128, 128], F32) for i in range(2)]
    sT = [[P.sb(f"hg_sT{i}{n}", [128, 128], F32) for n in range(2)] for i in range(2)]
    sTraw = [[P.sb(f"hg_sTr{i}{n}", [128, 128], F32) for n in range(2)] for i in range(2)]
    for i in range(2):
        for n in range(2):
            P.memset(sTraw[i][n].all(), 0.0, eng="pool")
    ssq = P.sb("hg_ssq", [128, 2], F32)
    tot = P.sb("hg_tot", [128, NCH], F32)
    refc = P.sb("hg_refc", [128, NCH], F32)
    junk = P.sb("hg_junk", [128, 128], F32)
    ytk = [P.sb(f"hg_y{i}", [128, 128], F16) for i in range(2)]
    epsn = P.sb("hg_eps", [128, 1], F32)
    P.memset(epsn.all(), NORM_EPS, eng="pool")
    masks = [G.cst[:, C_UF:C_UF + 128].v(lambda a: a.bitcast(U32)), G.cst[:, C_UB:C_UB + 128].v(lambda a: a.bitcast(U32))]
    c3 = lambda a: a.rearrange("p (c q) -> p c q", q=HQ)

    for h in range(4):
        wq = load_w(512 + h * 128)
        for tb in range(4):
            pb = PB[tb % 2]
            proj_fm(wq, tb, pb)
            P.copy(qT[:, tb * 512:(tb + 1) * 512], pb.all(), eng=evac_eng(G))
        wv = load_w(2048 + h * 128)
        wg = load_w(2560 + h * 128)
        for tb in range(4):
            pb = PB[tb % 2]
            proj_fm(wv, tb, pb)
            P.copy(fb_[:, tb * 512:(tb + 1) * 512], pb.all(), eng=evac_eng(G))
            pg = PB[2 + tb % 2]
            proj_fm(wg, tb, pg)
            P.act(sgT[:, tb * 512:(tb + 1) * 512], pg.all(), AF.Sigmoid)
        P.ts(sgT.all(), sgT.all(), nwp[:, h:h + 1], ALU.mult, eng="pool")
        for i4 in range(4):
            pb = PB[4 + i4 % 2]
            for jj in range(4):
                i = i4 * 4 + jj
                P.transpose(pb[:, jj * 128:(jj + 1) * 128], fb_[:, i * 128:(i + 1) * 128], G.ident)
            P.copy(vtok[:, i4 * 4:(i4 + 1) * 4, :].r(), pb.all().v(lambda a: a.rearrange("p (i v) -> p i v", v=128)), eng=evac_eng(G))
        for di in range(2):
            wf = load_w(1024 + di * 512 + h * 128)
            for tb in range(4):
                pb = PB[tb % 2]
                proj_fm(wf, tb, pb)
                P.act(fb_[:, tb * 512:(tb + 1) * 512], pb.all(), AF.Sigmoid)
            P.ts(fb_.all(), fb_.all(), oml[:, h:h + 1], ALU.mult, lbv[:, h:h + 1], ALU.add)
            P.act(lf.all(), fb_.all(), AF.Ln)
            c_, r_, l_ = cum.all(), rmask.all(), lf.all()
            P.add("dve", lambda e, c_=c_, r_=r_, l_=l_: e.tensor_tensor_scan(c_.ap, r_.ap, l_.ap, 0.0, ALU.mult, ALU.add), [r_, l_], [c_])
            if di == 1:
                P.tt(lf.all(), lf.all(), cum.all(), ALU.subtract)
                P.copy(tot.all().v(lambda a: a.unsqueeze(2)), cum.all().v(lambda a: c3(a)[:, :, HQ - 1:HQ]))
                P.tt(cum.all().v(c3), lf.all().v(c3), tot.all().v(lambda a: a.unsqueeze(2).to_broadcast([128, NCH, HQ])), ALU.add)
            lastcol = (HQ - 1) if di == 0 else 0
            refv = cum.all().v(lambda a: c3(a)[:, :, HQ // 2:HQ // 2 + 1])
            lastv = cum.all().v(lambda a, lc=lastcol: c3(a)[:, :, lc:lc + 1])
            v2 = lambda a: a.unsqueeze(2)
            e_ref, e_last, e_lr = esc[di]
            P.act(e_ref.all().v(v2), refv, AF.Exp)
            P.act(e_last.all().v(v2), lastv, AF.Exp)
            P.tt(e_lr.all().v(v2), lastv, refv, ALU.subtract)
            P.act(e_lr.all(), e_lr.all(), AF.Exp)
            P.copy(refc.all().v(v2), refv)
            P.tt(lf.all().v(c3), cum.all().v(c3), refc.all().v(lambda a: a.unsqueeze(2).to_broadcast([128, NCH, HQ])), ALU.subtract)
            P.act(cum.all(), lf.all(), AF.Exp)
            P.tt(qt_[di].all().r(), qT.all(), cum.all(), ALU.mult, eng="pool")
            P.act(cum.all(), lf.all(), AF.Exp, scale=-1.0)
            P.ts(fb_.all(), fb_.all(), -1.0, ALU.mult, 1.0, ALU.add, eng="pool")
            P.tt(kt_[di].all().r(), fb_.all(), cum.all(), ALU.mult)

        def state_step(di, c, S):
            e_ref, e_last, e_lr = esc[di]
            kk = ktok[di]
            pt = PB[4 + di]
            P.transpose(pt[:, 0:128], kt_[di][:, c * HQ:(c + 1) * HQ], G.ident)
            P.copy(kk.all().r(), pt[:, 0:128], eng="act")
            pu = PB[6 + di]
            P.mm(pu[:, 0:128], kk.all().r(), vtok[:, c, :].r())
            P.ts(S.all(), S.all(), e_last[:, c:c + 1], ALU.mult, eng="pool")
            P.stt(S.all(), pu[:, 0:128], e_lr[:, c:c + 1], S.all(), ALU.mult, ALU.add)

        P.memset(Sst[1].all(), 0.0, eng="pool")
        for c in range(NCH - 1, -1, -1):
            P.ts(Sbst[:, c, :].r(), Sst[1].all(), esc[1][0][:, c:c + 1], ALU.mult)
            if c > 0:
                state_step(1, c, Sst[1])
        P.memset(Sst[0].all(), 0.0, eng="pool")
        for c in range(NCH):
            cs = slice(c * HQ, (c + 1) * HQ)
            stl = Stl[c % 2]
            P.ts(stl.all().r(), Sst[0].all(), esc[0][0][:, c:c + 1], ALU.mult, eng="pool")
            for di in range(2):
                ps_ = PB[di]
                P.mm(ps_[:, 0:128], kt_[di][:, cs].r(), qt_[di][:, cs].r())
                o_, m_, p_ = sTraw[di][c % 2].all(), masks[di], ps_[:, 0:128]
                P.add("dve", lambda e, o_=o_, m_=m_, p_=p_: e.copy_predicated(o_.ap, m_.ap, p_.ap), [m_, p_], [o_])
                P.copy(sT[di][c % 2].all().r(), o_, eng="act")
            po = PB[2 + c % 2]
            P.mm(po[:, 0:128], sT[0][c % 2].all().r(), vtok[:, c, :].r(), start=True, stop=False)
            P.mm(po[:, 0:128], sT[1][c % 2].all().r(), vtok[:, c, :].r(), start=False, stop=False)
            P.mm(po[:, 0:128], qt_[0][:, cs].r(), stl.all().r(), start=False, stop=False)
            P.mm(po[:, 0:128], qt_[1][:, cs].r(), Sbst[:, c, :].r(), start=False, stop=True)
            if c < NCH - 1:
                state_step(0, c, Sst[0])
            P.act(junk.all(), po[:, 0:128], AF.Square, accum_out=ssq[:, 0:1])
            P.act(ssq[:, 1:2], ssq[:, 0:1], AF.Sqrt, bias=epsn.all(), scale=1.0 / 128)
            P.recip(ssq[:, 1:2], ssq[:, 1:2])
            yk = ytk[c % 2]
            P.ts(yk.all(), po[:, 0:128], ssq[:, 1:2], ALU.mult)
            pt = PB[4 + 0].all().v(lambda a: a.bitcast(F16))
            ptc = Acc(pt.ap[:, 0:128], pt.cells)
            P.transpose(ptc, yk.all(), G.id16.all())
            P.tt(yT[:, 4 + h, cs], ptc, sgT[:, cs], ALU.mult)
    P.release(m1)
    out_proj_ln1(G, d["w_out_cd"][j], yT, d["ln1_g"][l], d["ln1_b"][l])
    P.release(m0)


def mixer_ab(G, l, do_attn=True, do_ssd=True):
    P, X, PB, d = G.P, G.X, G.PB, G.d
    j = l // 2
    wd = d["w_in_ab"][j]
    m0 = P.mark()
    yT = P.sb("yT", [128, NK, L], F16, blk=128)
    hd = lambda a: a.rearrange("p (h e) -> p h e", e=64)
    f16v = lambda pb: pb.all().v(lambda a: a.bitcast(F16))

    if do_attn:
        m1 = P.mark()
        qT = P.sb("at_qT", [128, 4, L], F16, blk=128)
        kT = P.sb("at_kT", [128, 2, 2, L], F16, blk=128)
        vaug = P.sb("at_v", [128, NT, 2, 66], F16, blk=132)
        P.memset(vaug.all(), 1.0, eng="pool")
        m2 = P.mark()
        Wq = P.sb("at_W", [128, NK, 768], F32, blk=768)
        P.dma(Wq.all().r(), wcols(wd, 1552, 768))
        wqk = P.sb("at_wqk", [128, 2, 64], F32)
        P.dma(wqk[:, 0, :], d["attn_q_norm"][j].partition_broadcast(128))
        P.dma(wqk[:, 1, :], d["attn_k_norm"][j].partition_broadcast(128))
        epsn = P.sb("at_eps", [128, 1], F32)
        P.memset(epsn.all(), NORM_EPS, eng="pool")
        sq_ = [P.sb(f"at_sq{i}", [128, 640], F32) for i in range(2)]
        ssq_ = [P.sb(f"at_ssq{i}", [128, 10], F32) for i in range(2)]
        qn_ = [P.sb(f"at_qn{i}", [128, 640], F32) for i in range(2)]
        ra_ = [P.sb(f"at_ra{i}", [128, 320], F32) for i in range(2)]
        rb_ = [P.sb(f"at_rb{i}", [128, 320], F32) for i in range(2)]
        ra2_ = [P.sb(f"at_rc{i}", [128, 320], F32) for i in range(2)]
        rb2_ = [P.sb(f"at_rd{i}", [128, 320], F32) for i in range(2)]
        qr = [P.sb(f"at_qr{i}", [128, 512], F32) for i in range(2)]
        kr = [P.sb(f"at_kr{i}", [128, 2, 2, 128], F32, blk=512) for i in range(2)]
        for t_ in kr:
            P.memset(t_.all(), 0.0, eng="pool")
        for i in range(NT):
            ts_ = slice(i * 128, (i + 1) * 128)
            sq, ssq, qn, ra, rb = sq_[i % 2], ssq_[i % 2], qn_[i % 2], ra_[i % 2], rb_[i % 2]
            pq, pk = PB[(2 * i) % 4], PB[(2 * i + 1) % 4]
            for k in range(NK):
                P.mm(pq.all(), X[:, k, ts_].r(), Wq[:, k, 0:512].r(), start=(k == 0), stop=(k == NK - 1))
            for k in range(NK):
                P.mm(pk[:, 0:256], X[:, k, ts_].r(), Wq[:, k, 512:768].r(), start=(k == 0), stop=(k == NK - 1))
            P.copy(vaug[:, i, :, 0:64], pk[:, 128:256].v(lambda a: a.rearrange("p (g e) -> p g e", e=64)), eng="dve")
            P.act(sq[:, 0:512], pq.all(), AF.Square)
            P.act(sq[:, 512:640], pk[:, 0:128], AF.Square)
            P.reduce(ssq.all(), sq.all().v(hd), ALU.add)
            P.act(ssq.all(), ssq.all(), AF.Sqrt, bias=epsn.all(), scale=1.0 / 64)
            P.recip(ssq.all(), ssq.all())
            P.tt(qn[:, 0:512].v(hd), pq.all().v(hd), ssq[:, 0:8].v(lambda a: a.unsqueeze(2).to_broadcast([128, 8, 64])), ALU.mult)
            P.tt(qn[:, 512:640].v(hd), pk[:, 0:128].v(hd), ssq[:, 8:10].v(lambda a: a.unsqueeze(2).to_broadcast([128, 2, 64])), ALU.mult)
            P.tt(qn[:, 0:512].v(hd), qn[:, 0:512].v(hd), wqk[:, 0, :].v(lambda a: a.unsqueeze(1).to_broadcast([128, 8, 64])), ALU.mult, eng="pool")
            P.tt(qn[:, 512:640].v(hd), qn[:, 512:640].v(hd), wqk[:, 1, :].v(lambda a: a.unsqueeze(1).to_broadcast([128, 2, 64])), ALU.mult, eng="pool")
            pr = lambda a: a.rearrange("p (h e two) -> p h e two", e=32, two=2)
            xe = qn.all().v(lambda a: pr(a)[:, :, :, 0])
            xo = qn.all().v(lambda a: pr(a)[:, :, :, 1])
            cs_ = G.cst[:, C_COS + i * 32:C_COS + (i + 1) * 32].v(lambda a: a.unsqueeze(1).to_broadcast([128, 10, 32]))
            sn_ = G.cst[:, C_SIN + i * 32:C_SIN + (i + 1) * 32].v(lambda a: a.unsqueeze(1).to_broadcast([128, 10, 32]))
            h3 = lambda a: a.rearrange("p (h e) -> p h e", e=32)
            qro, kro = qr[i % 2], kr[i % 2]
            P.tt(ra.all().v(h3), xe, cs_, ALU.mult)
            P.tt(rb.all().v(h3), xo, sn_, ALU.mult, eng="pool")
            oute_q = qro.all().v(lambda a: pr(a)[:, :, :, 0])
            outo_q = qro.all().v(lambda a: pr(a)[:, :, :, 1])
            P.tt(oute_q, ra[:, 0:256].v(h3), rb[:, 0:256].v(h3), ALU.subtract)
            for dup in range(2):
                oe = kro[:, :, dup, dup * 64:(dup + 1) * 64].v(lambda a: a.rearrange("p g (e two) -> p g e two", two=2)[:, :, :, 0])
                P.tt(oe, ra[:, 256:320].v(h3), rb[:, 256:320].v(h3), ALU.subtract, eng="pool")
            ra2, rb2 = ra2_[i % 2], rb2_[i % 2]
            P.tt(ra2.all().v(h3), xe, sn_, ALU.mult)
            P.tt(rb2.all().v(h3), xo, cs_, ALU.mult, eng="pool")
            P.tt(outo_q, ra2[:, 0:256].v(h3), rb2[:, 0:256].v(h3), ALU.add)
            for dup in range(2):
                oo = kro[:, :, dup, dup * 64:(dup + 1) * 64].v(lambda a: a.rearrange("p g (e two) -> p g e two", two=2)[:, :, :, 1])
                P.tt(oo, ra2[:, 256:320].v(h3), rb2[:, 256:320].v(h3), ALU.add, eng="pool")
            pt = PB[4 + i % 2]
            for hp in range(4):
                P.transpose(pt[:, hp * 128:(hp + 1) * 128], qro[:, hp * 128:(hp + 1) * 128], G.ident)
            P.copy(qT[:, :, ts_], pt.all().v(lambda a: a.rearrange("p (h t) -> p h t", t=128)), eng="act")
            pt2 = PB[6 + i % 2]
            for g in range(2):
                for v in range(2):
                    P.transpose(pt2[:, (g * 2 + v) * 128:(g * 2 + v + 1) * 128], kro[:, g, v, :], G.ident)
            P.copy(kT[:, :, :, ts_], pt2.all().v(lambda a: a.rearrange("p (g v t) -> p g v t", g=2, v=2)), eng="act")
        P.release(m2)
        PT = [P.sb(f"at_PT{i}", [128, 512], BF16) for i in range(3)]
        OTs = [P.sb(f"at_OT{i}", [66, 512], F32) for i in range(2)]
        rsb = [P.sb(f"at_rs{i}", [128, 512], F32) for i in range(2)]
        sel = P.sb("at_sel", [66, 3, 128], F32, blk=128)
        P.copy(sel[:, 0, :].r(), G.cst[0:66, C_ID:C_ID + 128])
        P.ts(sel[:, 1, 0:64].r(), G.ones[0:66, 0:64], 0.0, ALU.mult)
        P.copy(sel[:, 1, 64:128].r(), G.cst[0:66, C_ID:C_ID + 64])
        P.ts(sel[:, 2, :].r(), G.cst[0:66, C_ID + 64:C_ID + 65].bc([66, 128]), 1.0, ALU.mult)
        G.npt = 0
        nit = 0
        for h in range(8):
            g, hp, hf = h // 4, h // 2, h % 2
            prt = slice(hf * 64, (hf + 1) * 64)
            for qb in range(4):
                qs = slice(qb * 512, (qb + 1) * 512)
                po = PB[2 + nit % 2]
                pts = {}

                def score(kt):
                    ps = PB[kt % 2]
                    P.mm(ps.all(), kT[:, g, hf, kt * 128:(kt + 1) * 128], qT[:, hp, qs])
                    pt_ = PT[G.npt % 3]
                    G.npt += 1
                    P.act(pt_.all(), ps.all(), AF.Exp, scale=0.125)
                    pts[kt] = pt_

                score(0)
                for kt in range(NT):
                    if kt + 1 < NT:
                        score(kt + 1)
                    P.mm(po[0:66, :], vaug[:, kt, g, :], pts[kt].all(), start=(kt == 0), stop=(kt == NT - 1))
                ot = OTs[nit % 2]
                P.copy(ot.all().r(), po[0:66, :], eng="dve")
                pso, pss = PB[4 + 2 * (nit % 2)], PB[5 + 2 * (nit % 2)]
                P.mm(pso.all(), sel[:, hf, :].r(), ot.all().r())
                P.mm(pss.all(), sel[:, 2, :].r(), ot.all().r())
                r_ = rsb[nit % 2]
                P.recip(r_[prt, :], pss[prt, :])
                P.tt(yT[prt, 4 + hp, qs], pso[prt, :], r_[prt, :], ALU.mult)
                nit += 1
        P.release(m1)
    else:
        for k in range(4, 8):
            P.memset(yT[:, k, :], 0.0, eng="pool")

    if do_ssd:
        m1 = P.mark()
        xtok = P.sb("sd_x", [128, NT, 512], F16, blk=512)
        BT = P.sb("sd_BT", [128, 2, L], F16, blk=128)
        CT = P.sb("sd_CT", [128, 2, L], F16, blk=128)
        Btok = P.sb("sd_Bt", [128, NT, 256], F16, blk=256)
        sc = lambda nm: P.sb(nm, [128, NT, 16], F32, blk=256)
        dt_, lndt, la, cumb_a, tot, A2, wgt, ea, dec = [sc(n) for n in ("sd_dt", "sd_lndt", "sd_la", "sd_A", "sd_tot", "sd_A2", "sd_w", "sd_ea", "sd_dec")]
        f2 = lambda a: a.rearrange("p i e -> p (i e)")
        m2 = P.mark()
        wdt = P.sb("sd_wdt", [128, NK, 16], F32)
        P.dma(wdt.all(), d["w_dt"][j].rearrange("(k p) c -> p k c", p=128))
        bb = P.sb("sd_bias", [128, 16], F32)
        P.dma(bb.all(), d["ssm_dt_bias"][j].rearrange("a h -> (a h)").partition_broadcast(128))
        ab = P.sb("sd_alog", [128, 16], F32)
        P.dma(ab.all(), d["ssm_a_log"][j].rearrange("a h -> (a h)").partition_broadcast(128))
        P.act(ab.all(), ab.all(), AF.Exp)
        pd = PB[0]
        for i in range(NT):
            for k in range(NK):
                P.mm(pd[:, i * 16:(i + 1) * 16], X[:, k, i * 128:(i + 1) * 128], wdt[:, k, :], start=(k == 0), stop=(k == NK - 1))
        b3 = lambda t_: t_.all().v(lambda a: a.unsqueeze(1).to_broadcast([128, NT, 16]))
        P.tt(dt_.all(), pd[:, 0:256].v(lambda a: a.rearrange("p (i e) -> p i e", e=16)), b3(bb), ALU.add)
        P.act(dt_.all(), dt_.all(), AF.Exp)
        P.act(dt_.all(), dt_.all(), AF.Ln, bias=G.ones[:, 0:1], scale=1.0)
        P.act(lndt.all(), dt_.all(), AF.Ln)
        P.stt(la.all(), dt_.all(), -1.0, b3(ab), ALU.mult, ALU.mult)
        pc = PB[1]
        for i in range(NT):
            P.mm(pc[:, i * 32:i * 32 + 8], G.cst[:, C_UF:C_UF + 128], la[:, i, 0:8])
            P.mm(pc[:, i * 32 + 8:i * 32 + 16], G.cst[:, C_UB:C_UB + 128], la[:, i, 8:16])
            P.mm(pc[:, i * 32 + 16:i * 32 + 32], G.ones.all(), la[:, i, :])
        pc3 = pc.all().v(lambda a: a.rearrange("p (i e) -> p i e", e=32))
        P.copy(cumb_a.all(), Acc(pc3.ap[:, :, 0:16], pc3.cells), eng="act")
        P.copy(tot.all(), Acc(pc3.ap[:, :, 16:32], pc3.cells), eng="dve")
        P.tt(A2.all(), cumb_a.all(), lndt.all(), ALU.subtract)
        P.tt(wgt.all(), tot.all(), A2.all(), ALU.subtract)
        P.act(wgt.all(), wgt.all(), AF.Exp)
        P.act(ea.all(), cumb_a.all(), AF.Exp)
        P.act(dec.all(), tot.all(), AF.Exp)
        wbuf = [P.sb(f"sd_w{i}", [128, NK, 128], F32, blk=128) for i in range(2)]
        cw = P.sb("sd_cw", [128, 5, 8], F32)
        cbi = P.sb("sd_cb", [128, 8], F32)
        for kk in range(5):
            P.dma(cw[:, kk, :], d["ssm_conv_w"][j, kk].rearrange("(c p) -> p c", p=128), slow=True)
        P.dma(cbi.all(), d["ssm_conv_b"][j].rearrange("(c p) -> p c", p=128), slow=True)
        ubuf = P.sb("sd_u", [128, L + 4], F32, blk=L + 4)
        so = P.sb("sd_so", [128, L], F16, blk=128)
        dg = [P.sb(f"sd_dg{i}", [128, 5, 128], F32, blk=128) for i in range(2)]
        P.ts(ubuf.all().r(), G.ones[:, 0:1].bc([128, L + 4]), 0.0, ALU.mult)
        for ch in range(8):
            w = wbuf[ch % 2]
            P.dma(w.all().r(), wcols(wd, 512 + ch * 128, 128))
            for tb in range(4):
                pb = PB[2 + tb % 2]
                for k in range(NK):
                    P.mm(pb.all(), w[:, k, :].r(), X[:, k, tb * 512:(tb + 1) * 512].r(), start=(k == 0), stop=(k == NK - 1))
                P.copy(ubuf[:, 2 + tb * 512:2 + (tb + 1) * 512].r(), pb.all(), eng=evac_eng(G))
            dgc = dg[ch % 2]
            for kk in range(5):
                P.ts(dgc[:, kk, :].r(), G.ident, cw[:, kk, ch:ch + 1], ALU.mult)
            for tb in range(4):
                pc_ = PB[tb % 2]
                for kk in range(5):
                    P.mm(pc_.all(), dgc[:, kk, :].r(), ubuf[:, tb * 512 + kk:tb * 512 + kk + 512].r(), start=(kk == 0), stop=(kk == 4))
                tsl = slice(tb * 512, (tb + 1) * 512)
                if ch < 4:
                    dst = so[:, tsl]
                elif ch < 6:
                    dst = BT[:, ch - 4, tsl]
                else:
                    dst = CT[:, ch - 6, tsl]
                P.act(dst, pc_.all(), AF.Silu, bias=cbi[:, ch:ch + 1])
            if ch < 4:
                for i8 in range(2):
                    pb = PB[4 + i8]
                    pv = f16v(pb)
                    for i in range(8):
                        ti = i8 * 8 + i
                        P.transpose(Acc(pv.ap[:, i * 128:(i + 1) * 128], pv.cells), so[:, ti * 128:(ti + 1) * 128], G.id16.all())
                    P.copy(xtok[:, i8 * 8:(i8 + 1) * 8, ch * 128:(ch + 1) * 128], Acc(pv.ap.rearrange("p (i c) -> p i c", c=128), pv.cells), eng=evac_eng(G))
            elif ch < 6:
                g = ch - 4
                for i8 in range(2):
                    pb = PB[6 + i8]
                    pv = f16v(pb)
                    for i in range(8):
                        ti = i8 * 8 + i
                        P.transpose(Acc(pv.ap[:, i * 128:(i + 1) * 128], pv.cells), BT[:, g, ti * 128:(ti + 1) * 128], G.id16.all())
                    P.copy(Btok[:, i8 * 8:(i8 + 1) * 8, g * 128:(g + 1) * 128], Acc(pv.ap.rearrange("p (i c) -> p i c", c=128), pv.cells), eng=evac_eng(G))
        P.release(m2)
        Wz = P.sb("sd_Wz", [128, NK, 512], F32, blk=512)
        P.dma(Wz.all().r(), wcols(wd, 0, 512))
        Sbs = P.sb("sd_Sbs", [128, NT, 512], F16, blk=512)
        S32 = [P.sb(f"sd_S32{i}", [128, 512], F32) for i in range(2)]
        S16 = [P.sb(f"sd_S16{i}", [128, 512], F16) for i in range(2)]
        xw = [P.sb("sd_xw", [128, 512], F16)] * 2
        dsk = P.sb("sd_dsk", [128, 8], F32)
        P.dma(dsk.all(), d["ssm_d"][j].partition_broadcast(128))
        nwb = P.sb("sd_nw", [128, 512], F32)
        P.dma(nwb.all(), d["ssm_norm_w"][j].partition_broadcast(128))
        epsn = P.sb("sd_eps", [128, 1], F32)
        P.memset(epsn.all(), NORM_EPS, eng="pool")
        rhsb = [P.sb("sd_rhs", [128, 8, 128], F32, blk=1024)] * 2
        E = [P.sb(f"sd_E{i}", [128, 8, 128], F32, blk=128) for i in range(2)]
        Tt = [P.sb(f"sd_T{i}", [128, 128], F32) for i in range(2)]
        MT = [P.sb(f"sd_MT{i}", [128, 128], F16) for i in range(3)]
        t1 = P.sb("sd_t1", [128, 512], F32)
        t2 = P.sb("sd_t2", [128, 512], F32)
        sz = P.sb("sd_sz", [128, 512], F16)
        yk = P.sb("sd_yk", [128, 512], F16)
        ssq = P.sb("sd_ssq", [128, 2], F32)
        hb = lambda t_, c, d0: t_[:, c, d0:d0 + 8].v(lambda a: a.unsqueeze(2).to_broadcast([128, 8, 64]))

        def state_update(di, c):
            xw_ = xw[di]
            P.tt(xw_.all().v(hd), xtok[:, c, :].v(hd), hb(wgt, c, di * 8), ALU.mult, eng="pool")
            pst = PB[7]
            for g in range(2):
                P.mm(pst[:, g * 256:(g + 1) * 256], Btok[:, c, g * 128:(g + 1) * 128], xw_[:, g * 256:(g + 1) * 256])
            P.tt(S32[di].all().v(hd), S32[di].all().v(hd), hb(dec, c, di * 8), ALU.mult, eng="pool")
            P.tt(S32[di].all(), S32[di].all(), pst.all(), ALU.add)

        P.memset(S32[1].all(), 0.0, eng="pool")
        for c in range(NT - 1, -1, -1):
            P.copy(Sbs[:, c, :], S32[1].all(), eng="act")
            if c > 0:
                state_update(1, c)
        P.memset(S32[0].all(), 0.0, eng="pool")
        G.nmt = 0

        def front(c):
            cs = slice(c * 128, (c + 1) * 128)
            pyd = PB[3] if c % 2 == 0 else PB[6]
            for di in range(2):
                U = G.cst[:, (C_UF if di == 0 else C_UB):(C_UF if di == 0 else C_UB) + 128]
                nm = G.cst[:, (C_NMF if di == 0 else C_NMB):(C_NMF if di == 0 else C_NMB) + 128]
                rh = rhsb[di]
                P.tt(rh.all(), U.v(lambda a: a.unsqueeze(1).to_broadcast([128, 8, 128])),
                     la[:, c, di * 8:di * 8 + 8].v(lambda a: a.unsqueeze(2).to_broadcast([128, 8, 128])), ALU.mult,
                     eng=("dve" if di == 0 else "pool"))
                for hh in range(2):
                    P.mm(PB[hh].all(), G.ones.all(), rh[:, hh * 4:(hh + 1) * 4, :].v(lambda a: a.rearrange("p h i -> p (h i)")))
                for h in range(8):
                    tt_ = Tt[h % 2]
                    P.stt(tt_.all(), PB[h // 4][:, (h % 4) * 128:(h % 4 + 1) * 128], A2[:, c, di * 8 + h:di * 8 + h + 1], nm, ALU.subtract, ALU.add)
                    P.act(E[di][:, h, :], tt_.all(), AF.Exp)
            for g in range(2):
                P.mm(PB[2][:, g * 128:(g + 1) * 128], BT[:, g, cs], CT[:, g, cs])
            P.tt(E[0].all(), E[0].all(), E[1].all(), ALU.add, eng="pool")
            for h in range(8):
                mt = MT[G.nmt % 3]
                G.nmt += 1
                P.tt(mt.all(), PB[2][:, (h // 4) * 128:(h // 4 + 1) * 128], E[0][:, h, :], ALU.mult)
                P.mm(pyd[:, h * 64:(h + 1) * 64], mt.all(), xtok[:, c, h * 64:(h + 1) * 64])

        def back(c):
            cs = slice(c * 128, (c + 1) * 128)
            pyd = PB[3] if c % 2 == 0 else PB[6]
            P.copy(S16[0].all(), S32[0].all(), eng="act")
            for g in range(2):
                P.mm(PB[4][:, g * 256:(g + 1) * 256], CT[:, g, cs], S16[0][:, g * 256:(g + 1) * 256])
                P.mm(PB[5][:, g * 256:(g + 1) * 256], CT[:, g, cs], Sbs[:, c, g * 256:(g + 1) * 256])
            if c < NT - 1:
                state_update(0, c)
            pz = PB[0]
            for k in range(NK):
                P.mm(pz.all(), X[:, k, cs].r(), Wz[:, k, :].r(), start=(k == 0), stop=(k == NK - 1))
            P.act(sz.all(), pz.all(), AF.Silu)
            P.tt(t1.all().v(hd), PB[4].all().v(hd), hb(ea, c, 0), ALU.mult)
            P.tt(t2.all().v(hd), PB[5].all().v(hd), hb(ea, c, 8), ALU.mult)
            P.tt(t1.all(), t1.all(), t2.all(), ALU.add, eng="pool")
            P.tt(t2.all().v(hd), xtok[:, c, :].v(hd), dsk.all().v(lambda a: a.unsqueeze(2).to_broadcast([128, 8, 64])), ALU.mult, eng="pool")
            P.tt(t1.all(), t1.all(), t2.all(), ALU.add, eng="pool")
            P.tt(t1.all(), pyd.all(), t1.all(), ALU.add)
            P.tt(t1.all(), t1.all(), sz.all(), ALU.mult)
            P.act(t2.all(), t1.all(), AF.Square, accum_out=ssq[:, 0:1])
            P.act(ssq[:, 1:2], ssq[:, 0:1], AF.Sqrt, bias=epsn.all(), scale=1.0 / 512)
            P.recip(ssq[:, 1:2], ssq[:, 1:2])
            P.stt(yk.all(), t1.all(), ssq[:, 1:2], nwb.all(), ALU.mult, ALU.mult)
            pv = f16v(PB[7])
            for c4 in range(4):
                P.transpose(Acc(pv.ap[:, c4 * 128:(c4 + 1) * 128], pv.cells), yk[:, c4 * 128:(c4 + 1) * 128], G.id16.all())
            P.copy(yT[:, 0:4, cs], Acc(pv.ap[:, 0:512].rearrange("p (h t) -> p h t", t=128), pv.cells), eng="act")

        front(0)
        for c in range(NT):
            if c + 1 < NT:
                front(c + 1)
            back(c)
        P.release(m1)
    else:
        for k in range(4):
            P.memset(yT[:, k, :], 0.0, eng="pool")
    out_proj_ln1(G, d["w_out_ab"][j], yT, d["ln1_g"][l], d["ln1_b"][l])
    P.release(m0)


DEPTH = 4
_DRAM_SPECS = [
    ("consts", [128, C_END], F32), ("x", [L, D], F32),
    ("w_in_ab", [2, D, 2320], F32R), ("w_dt", [2, D, 16], F32), ("ssm_conv_w", [2, 5, 1024], F32), ("ssm_conv_b", [2, 1024], F32),
    ("ssm_dt_bias", [2, 2, 8], F32), ("ssm_a_log", [2, 2, 8], F32), ("ssm_d", [2, 8], F32), ("ssm_norm_w", [2, 512], F32),
    ("attn_q_norm", [2, 64], F32), ("attn_k_norm", [2, 64], F32), ("w_out_ab", [2, D, D], F32),
    ("w_in_cd", [2, D, 3072], F32R), ("pool_w", [2, 4, 128, 128], F32R), ("pool_scale", [2, 512], F32),
    ("hgrn_lb_logits", [4, 512], F32), ("hgrn_norm_w", [2, 512], F32), ("w_out_cd", [2, D, D], F32),
    ("router_w", [4, D, NE], F32), ("moe_w1", [4, NE, D, FF], F32), ("moe_w3", [4, NE, D, FF], F32), ("moe_w2", [4, NE, FF, D], F32),
    ("ln1_g", [4, D], F32), ("ln1_b", [4, D], F32), ("ln2_g", [4, D], F32), ("ln2_b", [4, D], F32),
]


def build_program(layers=range(DEPTH)):
    nc = bass.Bass("TRN2", target_bir_lowering=False, dynamic_dma_scratch_size=8192)
    nc.dge_precook = False
    dram = {}
    for name, shape, dt in _DRAM_SPECS:
        dram[name] = nc.dram_tensor(name, list(shape), dt, kind="ExternalInput").ap()
    out = nc.dram_tensor("out", [L, D], F32, kind="ExternalOutput").ap()
    P = Prog(nc)
    G = setup(P, nc, dram)
    load_x(G, dram["x"])
    for l in layers:
        if l % 2 == 0:
            mixer_ab(G, l)
        else:
            mixer_cd(G, l)
        moe_phase(G, l)
    store_x(G, out)
    P.emit()
    P.close()
    return nc


def kernel(**inputs):
    x = np.ascontiguousarray(np.asarray(inputs["x"], dtype=np.float32))
    nb = x.shape[0]
    shared = {"consts": make_consts()}
    for name, shape, dt in _DRAM_SPECS:
        if name in ("consts", "x", "w_dt"):
            continue
        shared[name] = np.ascontiguousarray(np.asarray(inputs[name], dtype=np.float32))
    shared["w_dt"] = np.ascontiguousarray(shared["w_in_ab"][:, :, 1536:1552])
    nc = build_program()
    in_maps = []
    for b in range(nb):
        m = dict(shared)
        m["x"] = x[b]
        in_maps.append(m)
    res = run_bass_kernel_spmd(nc, in_maps, core_ids=list(range(nb)))
    return np.stack([np.asarray(r["out"], dtype=np.float32) for r in res.results], axis=0)
```

```python
import numpy as np
import concourse.bass as bass
import concourse.mybir as mybir
from concourse.bass_utils import run_bass_kernel_spmd

F32 = mybir.dt.float32
F32R = mybir.dt.float32r
F16 = mybir.dt.float16
BF16 = mybir.dt.bfloat16
I32 = mybir.dt.int32
AF = mybir.ActivationFunctionType
ALU = mybir.AluOpType
AX = mybir.AxisListType

ENGS = ("pe", "dve", "act", "pool", "sp")
SEM_CAP = 30000
DMA_K = 6


class Acc:
    __slots__ = ("ap", "cells")

    def __init__(self, ap, cells):
        self.ap = ap
        self.cells = cells

    def v(self, fn):
        return Acc(fn(self.ap), self.cells)

    def r(self):
        return Acc(self.ap.bitcast(F32R), self.cells)

    def bc(self, shape):
        return Acc(self.ap.to_broadcast(shape), self.cells)


class TT:
    def __init__(self, prog, name, handle, shape, dtype, blk):
        self.prog = prog
        self.name = name
        self.h = handle
        self.shape = list(shape)
        self.dtype = dtype
        self.blk = blk
        self.fstr = []
        s = 1
        for d in reversed(self.shape[1:]):
            self.fstr.insert(0, s)
            s *= d
        self.fsize = s

    def cells_of(self, idx):
        rngs = [(0, 0)]
        fd = self.shape[1:]
        idx = list(idx) + [slice(None)] * (len(self.shape) - len(idx))
        dims = []
        for i, d in enumerate(fd):
            ix = idx[i + 1]
            if isinstance(ix, int):
                dims.append((ix, ix + 1))
            else:
                a = 0 if ix.start is None else ix.start
                b = d if ix.stop is None else ix.stop
                assert ix.step is None
                dims.append((a, b))
        starts = [0]
        n = len(fd)
        tail = n
        while tail > 0 and dims[tail - 1] == (0, fd[tail - 1]):
            tail -= 1
        if tail == 0:
            return {(self.name, c) for c in range(0, (self.fsize - 1) // self.blk + 1)}
        outer = dims[: tail - 1]
        a, b = dims[tail - 1]
        st = self.fstr[tail - 1]
        starts = [0]
        for (lo, hi), s in zip(outer, self.fstr[: tail - 1]):
            starts = [x + k * s for x in starts for k in range(lo, hi)]
        cells = set()
        for x in starts:
            s0 = x + a * st
            e0 = x + b * st
            for c in range(s0 // self.blk, (e0 - 1) // self.blk + 1):
                cells.add((self.name, c))
        return cells

    def __getitem__(self, idx):
        if not isinstance(idx, tuple):
            idx = (idx,)
        return Acc(self.h[idx], self.cells_of(idx))

    def all(self):
        return self[tuple([slice(None)] * len(self.shape))]


class Op:
    __slots__ = ("eng", "fn", "rc", "wc", "dma", "deps", "signal", "ev", "note")

    def __init__(self, eng, fn, rc, wc, dma=False, note=""):
        self.eng = eng
        self.fn = fn
        self.rc = rc
        self.wc = wc
        self.dma = dma
        self.deps = ()
        self.signal = False
        self.ev = None
        self.note = note


class Prog:
    def __init__(self, nc):
        self.nc = nc
        self.ops = []
        self.stack = []
        self._uid = 0
        self.eobj = {"pe": nc.tensor, "dve": nc.vector, "act": nc.scalar, "pool": nc.gpsimd, "sp": nc.sync}

    def sb(self, name, shape, dtype=F32, blk=None):
        self._uid += 1
        nm = f"{name}_{self._uid}"
        cm = self.nc.sbuf_tensor(nm, list(shape), dtype)
        h = cm.__enter__()
        self.stack.append(cm)
        if blk is None:
            blk = shape[-1]
        return TT(self, nm, h, shape, dtype, blk)

    def ps(self, name, shape, dtype=F32, blk=None):
        self._uid += 1
        nm = f"{name}_{self._uid}"
        cm = self.nc.psum_tensor(nm, list(shape), dtype)
        h = cm.__enter__()
        self.stack.append(cm)
        blk = 2048 // mybir.dt.size(dtype)
        return TT(self, "PS:" + nm, h, shape, dtype, blk)

    def mark(self):
        return len(self.stack)

    def release(self, mark):
        self.barrier()
        while len(self.stack) > mark:
            cm = self.stack.pop()
            cm.__exit__(None, None, None)

    def barrier(self):
        self.ops.append(Op("all", None, set(), set(), note="barrier"))

    def add(self, eng, fn, reads, writes, dma=False, note=""):
        rc = set()
        for a in reads:
            if a is not None and isinstance(a, Acc):
                rc |= a.cells
        wc = set()
        for a in writes:
            if a is not None and isinstance(a, Acc):
                wc |= a.cells
        self.ops.append(Op(eng, fn, rc, wc, dma, note))

    @staticmethod
    def _ap(a):
        return a.ap if isinstance(a, Acc) else a

    def mm(self, out, lhsT, rhs, start=True, stop=True, **kw):
        o, l, r = out.ap, lhsT.ap, rhs.ap
        self.add("pe", lambda e: e.matmul(o, l, r, start=start, stop=stop, **kw), [lhsT, rhs], [out])

    def transpose(self, out, in_, ident):
        o, i, d = out.ap, in_.ap, ident.ap
        self.add("pe", lambda e: e.transpose(o, i, d), [in_, ident], [out])

    def act(self, out, in_, func, bias=None, scale=1.0, accum_out=None, eng="act"):
        o, i = out.ap, in_.ap
        b = self._ap(bias)
        s = self._ap(scale)
        kw = {}
        if b is not None:
            kw["bias"] = b
        if accum_out is not None:
            kw["accum_out"] = accum_out.ap
        self.add(eng, lambda e: e.activation(o, i, func, scale=s, **kw),
                 [in_, bias, scale], [out, accum_out])

    def tt(self, out, in0, in1, op, eng="dve"):
        o, a, b = out.ap, in0.ap, in1.ap
        self.add(eng, lambda e: e.tensor_tensor(o, a, b, op), [in0, in1], [out])

    def ts(self, out, in0, s1, op0, s2=None, op1=None, accum_out=None, eng="dve"):
        o, a = out.ap, in0.ap
        if eng == "pool" and op1 is None and op0 == ALU.mult:
            op1, s2 = ALU.mult, 1.0
        x1 = self._ap(s1)
        x2 = self._ap(s2)
        kw = {}
        if op1 is not None:
            kw["op1"] = op1
        if accum_out is not None:
            kw["accum_out"] = accum_out.ap
        self.add(eng, lambda e: e.tensor_scalar(o, a, x1, x2, op0, **kw), [in0, s1, s2], [out, accum_out])

    def stt(self, out, in0, scalar, in1, op0, op1, eng="dve"):
        o, a, b = out.ap, in0.ap, in1.ap
        s = self._ap(scalar)
        self.add(eng, lambda e: e.scalar_tensor_tensor(o, a, s, b, op0, op1), [in0, scalar, in1], [out])

    def copy(self, out, in_, eng="dve"):
        o, i = out.ap, in_.ap
        if eng == "act":
            self.add(eng, lambda e: e.copy(o, i), [in_], [out])
        else:
            self.add(eng, lambda e: e.tensor_copy(o, i), [in_], [out])

    def memset(self, out, val, eng="dve"):
        o = out.ap
        self.add(eng, lambda e: e.memset(o, val), [], [out])

    def reduce(self, out, in_, op, axis=AX.X, eng="dve"):
        o, i = out.ap, in_.ap
        self.add(eng, lambda e: e.tensor_reduce(o, i, axis, op), [in_], [out])

    def recip(self, out, in_):
        o, i = out.ap, in_.ap
        self.add("dve", lambda e: e.reciprocal(o, i), [in_], [out])

    def dma(self, out, in_, q="sp", slow=False):
        o = self._ap(out)
        i = self._ap(in_)
        if slow:
            self.add(q, lambda e: e.dma_start(out=o, in_=i, allow_slow_non_contiguous=True), [in_], [out], dma=True)
        else:
            self.add(q, lambda e: e.dma_start(out=o, in_=i), [in_], [out], dma=True)

    def generic(self, eng, fn, reads, writes):
        self.add(eng, fn, reads, writes)

    def emit(self):
        nc = self.nc
        ops = self.ops
        last_w = {}
        readers = {}
        for i, op in enumerate(ops):
            if op.eng == "all":
                continue
            deps = set()
            for c in op.rc:
                w = last_w.get(c)
                if w is not None:
                    deps.add(w)
                if c[0].startswith("PS:"):
                    rd = readers.get(c)
                    if rd:
                        for kk, vv in rd.items():
                            if kk != op.eng:
                                deps.add(vv)
            for c in op.wc:
                w = last_w.get(c)
                if w is not None:
                    deps.add(w)
                rd = readers.get(c)
                if rd:
                    deps.update(rd.values())
            deps.discard(i)
            for c in op.wc:
                last_w[c] = i
                readers[c] = {}
            key = ("d", i) if op.dma else op.eng
            for c in op.rc:
                if c in op.wc:
                    continue
                readers.setdefault(c, {})[key] = i
            best = {}
            keep = []
            for d in deps:
                od = ops[d]
                if od.dma:
                    keep.append(d)
                else:
                    if od.eng == "pe" and op.eng == "pe" and not op.dma:
                        continue
                    if best.get(od.eng, -1) < d:
                        best[od.eng] = d
            keep.extend(best.values())
            op.deps = keep
            for d in keep:
                ops[d].signal = True
        last_on = {}
        bar_deps = {}
        for i, op in enumerate(ops):
            if op.eng == "all":
                bar_deps[i] = dict(last_on)
                for d in last_on.values():
                    ops[d].signal = True
            elif op.dma:
                op.signal = True
                last_on[("d", i)] = i
            else:
                last_on[op.eng] = i
        final = dict(last_on)
        for d in final.values():
            ops[d].signal = True

        sems = {}
        semctx = []

        def new_sem(nm):
            cm = nc.semaphore(nm)
            h = cm.__enter__()
            semctx.append(cm)
            return h

        cnt = {e: 0 for e in ENGS}
        eng_sems = {e: [] for e in ENGS}
        dma_cnt = {e: 0 for e in ENGS}
        dma_sems = {e: [] for e in ENGS}
        for i, op in enumerate(ops):
            if op.eng == "all" or not op.signal:
                continue
            if op.dma:
                q = op.eng
                n = dma_cnt[q]
                dma_cnt[q] += 1
                if len(dma_sems[q]) < DMA_K:
                    dma_sems[q].append(new_sem(f"dq_{q}_{len(dma_sems[q])}"))
                op.ev = (dma_sems[q][n % DMA_K], 16 * (n // DMA_K + 1), 16)
            else:
                e = op.eng
                n = cnt[e]
                cnt[e] += 1
                si = n // SEM_CAP
                if len(eng_sems[e]) <= si:
                    eng_sems[e].append(new_sem(f"c_{e}_{si}"))
                op.ev = (eng_sems[e][si], n % SEM_CAP + 1, 1)

        waited = {e: {} for e in ENGS}

        def wait(e, ev):
            sem, val = ev[0], ev[1]
            k = id(sem)
            if waited[e].get(k, 0) >= val:
                return
            waited[e][k] = val
            self.eobj[e].wait_ge(sem, val)

        self.n_emitted = {e: 0 for e in ENGS}
        for i, op in enumerate(ops):
            if op.eng == "all":
                for e in ENGS:
                    for d in bar_deps[i].values():
                        if ops[d].ev is not None:
                            wait(e, ops[d].ev)
                continue
            e = op.eng
            for d in op.deps:
                wait(e, ops[d].ev)
            if op.dma:
                sem, val, inc = op.ev
                if val > 16:
                    wait(e, (sem, val - 16))
            inst = op.fn(self.eobj[e])
            self.n_emitted[e] += 1
            if op.signal:
                inst.then_inc(op.ev[0], op.ev[2])
        for d in final.values():
            if ops[d].ev is not None:
                wait("sp", ops[d].ev)
        self._semctx = semctx

    def close(self):
        while self.stack:
            self.stack.pop().__exit__(None, None, None)
        for cm in reversed(getattr(self, "_semctx", [])):
            cm.__exit__(None, None, None)


D = 1024
L = 2048
NT = 16
NK = 8
ALPHA = 8.0 ** 0.25
LN_EPS = 1e-5
NORM_EPS = 1e-6
NEG = -1.0e30

C_ID, C_IOTA, C_UF, C_UB, C_NMF, C_NMB, C_COS, C_SIN, C_G, C_END = 0, 128, 384, 512, 640, 768, 896, 1408, 1920, 2048


def make_consts():
    c = np.zeros((128, C_END), np.float32)
    c[:, C_ID:C_ID + 128] = np.eye(128, dtype=np.float32)
    c[:, C_IOTA:C_IOTA + 256] = np.arange(256, dtype=np.float32)[None, :]
    k = np.arange(128)[:, None]
    i = np.arange(128)[None, :]
    c[:, C_UF:C_UF + 128] = (k <= i).astype(np.float32)
    c[:, C_UB:C_UB + 128] = (k >= i).astype(np.float32)
    c[:, C_NMF:C_NMF + 128] = np.where(i >= k, 0.0, NEG)
    c[:, C_NMB:C_NMB + 128] = np.where(i <= k, 0.0, NEG)
    t = np.arange(L)
    row = (t // 64).astype(np.float32)
    col = (t % 64).astype(np.float32)
    freqs = (10000.0 ** (-np.arange(0, 32, 2, dtype=np.float32) / 32)).astype(np.float32)
    ang = np.concatenate([row[:, None] * freqs, col[:, None] * freqs], axis=-1).astype(np.float32)
    cos = np.cos(ang).astype(np.float32).reshape(NT, 128, 32).transpose(1, 0, 2).reshape(128, 512)
    sin = np.sin(ang).astype(np.float32).reshape(NT, 128, 32).transpose(1, 0, 2).reshape(128, 512)
    c[:, C_COS:C_COS + 512] = cos
    c[:, C_SIN:C_SIN + 512] = sin
    pp = np.arange(128)
    c[:, C_G:C_G + 128] = (pp[:, None] % 16 == pp[None, :] % 16).astype(np.float32)
    return c


class Ctx:
    pass


def setup(P, nc, dram):
    G = Ctx()
    G.nc = nc
    G.P = P
    G.d = dram
    G.X = P.sb("X", [128, NK, L], F32, blk=128)
    G.PB = [P.ps(f"pb{i}", [128, 512], F32, blk=128) for i in range(8)]
    G.cst = P.sb("cst", [128, C_END], F32, blk=128)
    P.dma(G.cst.all(), dram["consts"])
    G.ones = P.sb("ones", [128, 128], F32)
    P.memset(G.ones.all(), 1.0, eng="pool")
    G.onesD = P.sb("onesD", [128, 128], F32)
    P.ts(G.onesD.all().r(), G.ones.all(), 1.0 / D, ALU.mult)
    G.id16 = P.sb("id16", [128, 128], F16)
    P.copy(G.id16.all(), G.cst[:, C_ID:C_ID + 128])
    G.ident = G.cst[:, C_ID:C_ID + 128]
    G.eps_ln = P.sb("epsln", [128, 1], F32)
    P.memset(G.eps_ln.all(), LN_EPS, eng="pool")
    G.rr = 0
    return G


def evac_eng(G):
    G.rr += 1
    return "act" if G.rr % 2 else "dve"


def load_x(G, xd):
    P, X, PB = G.P, G.X, G.PB
    m = P.mark()
    tmp = [P.sb(f"ldx{i}", [128, D], F32, blk=128) for i in range(2)]
    for i in range(NT):
        tb = tmp[i % 2]
        P.dma(tb.all(), xd[i * 128:(i + 1) * 128, :])
        for kk in range(2):
            pb = PB[(2 * i + kk) % 2]
            for j in range(4):
                k = kk * 4 + j
                P.transpose(pb[:, j * 128:(j + 1) * 128], tb[:, k * 128:(k + 1) * 128], G.ident)
            dst = X[:, kk * 4:(kk + 1) * 4, i * 128:(i + 1) * 128].r()
            src = pb.all().v(lambda a: a.rearrange("p (j t) -> p j t", j=4))
            P.copy(dst, src, eng=evac_eng(G))
    P.release(m)


def store_x(G, od):
    P, X, PB = G.P, G.X, G.PB
    m = P.mark()
    tmp = [P.sb(f"stx{i}", [128, D], F32, blk=128) for i in range(2)]
    for i in range(NT):
        tb = tmp[i % 2]
        for kk in range(2):
            pb = PB[(2 * i + kk) % 2]
            for j in range(4):
                k = kk * 4 + j
                P.transpose(pb[:, j * 128:(j + 1) * 128], X[:, k, i * 128:(i + 1) * 128], G.ident)
            P.copy(tb[:, kk * 512:(kk + 1) * 512], pb.all(), eng=evac_eng(G))
        P.dma(od[i * 128:(i + 1) * 128, :], tb.all())
    P.release(m)


def layer_norm(G, gd, bd, eps=LN_EPS):
    P, X, PB = G.P, G.X, G.PB
    m = P.mark()
    gb = P.sb("ln_gb", [128, 2, NK], F32)
    P.dma(gb[:, 0, :], gd.rearrange("(k p) -> p k", p=128), slow=True)
    P.dma(gb[:, 1, :], bd.rearrange("(k p) -> p k", p=128), slow=True)
    sq = [P.sb(f"ln_sq{i}", [128, 512], F32) for i in range(2)]
    epst = P.sb("ln_eps", [128, 1], F32)
    P.memset(epst.all(), eps, eng="pool")
    m2 = [P.sb(f"ln_m2{i}", [128, 512], F32) for i in range(2)]
    rs = [P.sb(f"ln_rs{i}", [128, 512], F32) for i in range(2)]
    tmp = [P.sb(f"ln_t{i}", [128, 512], F32) for i in range(8)]
    def stats(tb):
        ts_ = slice(tb * 512, (tb + 1) * 512)
        o = 3 * (tb % 2)
        ps_s, ps_q, ps_r = PB[o], PB[o + 1], PB[o + 2]
        for k in range(NK):
            P.mm(ps_s.all(), G.onesD.all().r(), X[:, k, ts_].r(), start=(k == 0), stop=(k == NK - 1))
        for k in range(NK):
            s_ = sq[k % 2]
            if k % 2 == 0:
                P.act(s_.all().r(), X[:, k, ts_], AF.Square)
            else:
                P.tt(s_.all().r(), X[:, k, ts_], X[:, k, ts_], ALU.mult)
            P.mm(ps_q.all(), G.onesD.all().r(), s_.all().r(), start=(k == 0), stop=(k == NK - 1))
        m2_, rs_ = m2[tb % 2], rs[tb % 2]
        P.act(m2_.all(), ps_s.all(), AF.Square)
        P.tt(rs_.all(), ps_q.all(), m2_.all(), ALU.subtract)
        P.act(rs_.all(), rs_.all(), AF.Sqrt, bias=epst.all(), scale=1.0)
        P.recip(rs_.all(), rs_.all())

    def norm(tb):
        ts_ = slice(tb * 512, (tb + 1) * 512)
        o = 3 * (tb % 2)
        ps_s = PB[o]
        rs_ = rs[tb % 2]
        for k in range(NK):
            t = tmp[k % 8]
            P.tt(t.all(), X[:, k, ts_], ps_s.all(), ALU.subtract)
            P.tt(t.all(), t.all(), rs_.all(), ALU.mult, eng="pool")
            P.act(X[:, k, ts_].r(), t.all(), AF.Identity, bias=gb[:, 1, k:k + 1], scale=gb[:, 0, k:k + 1])

    stats(0)
    for tb in range(4):
        if tb + 1 < 4:
            stats(tb + 1)
        norm(tb)
    P.release(m)


NE = 16
CAP = 256
FF = 2048
FB = 512
NFB = FF // FB


def moe_phase(G, l, upto=99):
    P, X, PB, d = G.P, G.X, G.PB, G.d
    m0 = P.mark()
    x16 = P.sb("x16", [128, NT, D], F16, blk=128)
    pos_tok = P.sb("pos_tok", [128, NT, NE], F32, blk=NT * NE)
    gm_tok = P.sb("gm_tok", [128, NT, NE], F32, blk=NT * NE)
    m1 = P.mark()
    mask_tok = P.sb("mask_tok", [128, NT, NE], F32, blk=NT * NE)
    rw = P.sb("rw", [128, NK, NE], F32)
    P.dma(rw.all(), d["router_w"][l].rearrange("(k p) e -> p k e", p=128))
    lg = PB[2]
    for i in range(NT):
        for k in range(NK):
            P.mm(lg[:, i * NE:(i + 1) * NE], X[:, k, i * 128:(i + 1) * 128], rw[:, k, :],
                 start=(k == 0), stop=(k == NK - 1))
    v3 = lambda a: a.rearrange("p (i e) -> p i e", e=NE)
    mx = P.sb("r_mx", [128, NT], F32)
    sh = P.sb("r_sh", [128, NT * NE], F32)
    aff = P.sb("r_aff", [128, NT * NE], F32)
    P.reduce(mx.all(), lg[:, 0:NT * NE].v(v3), ALU.max)
    P.tt(sh.all().v(v3), lg[:, 0:NT * NE].v(v3), mx.all().v(lambda a: a.unsqueeze(2).to_broadcast([128, NT, NE])), ALU.subtract)
    P.act(sh.all(), sh.all(), AF.Exp)
    P.reduce(mx.all(), sh.all().v(v3), ALU.add)
    P.recip(mx.all(), mx.all())
    P.tt(aff.all().v(v3), sh.all().v(v3), mx.all().v(lambda a: a.unsqueeze(2).to_broadcast([128, NT, NE])), ALU.mult)
    A = P.sb("r_A", [128, 256], F32)
    pa = PB[4]
    for half in range(2):
        P.transpose(pa[:, half * 128:(half + 1) * 128], aff[:, half * 128:(half + 1) * 128], G.ident)
    P.copy(A.all(), pa[:, 0:256], eng="act")
    for i in range(NT):
        for kk in range(2):
            pb = PB[(2 * i + kk) % 2]
            for j in range(4):
                k = kk * 4 + j
                P.transpose(pb[:, j * 128:(j + 1) * 128], X[:, k, i * 128:(i + 1) * 128], G.ident)
            P.copy(x16[:, i, kk * 512:(kk + 1) * 512], pb.all(), eng="act")
    lo = [P.sb(f"r_lo{i}", [128, 1], F32) for i in range(2)]
    cand = P.sb("r_cand", [128, 1], F32)
    cseg = P.sb("r_cseg", [128, 2], F32)
    ss = P.sb("r_ss", [128, 1], F32)
    junk = P.sb("r_junk", [128, 256], F32)
    P.memset(lo[0].all(), 0.0)
    P.memset(cseg.all(), 0.0)
    P.memset(cand.all(), 0.5)
    Gm = G.cst[:, C_G:C_G + 128]
    NIT = 30
    for it in range(NIT):
        step = 0.5 ** (it + 1)
        lo_o, lo_n = lo[it % 2], lo[(it + 1) % 2]
        P.ts(junk.all(), A.all(), cand[:, 0:1], ALU.is_ge, None, ALU.add, accum_out=cseg[:, 0:1])
        pcn = PB[5 + it % 2]
        P.mm(pcn[:, 0:2], Gm, cseg.all())
        P.ts(ss.all(), pcn[:, 0:1], CAP - 0.5, ALU.is_ge, step, ALU.mult)
        P.tt(lo_n.all(), lo_o.all(), ss.all(), ALU.add)
        if it + 1 < NIT:
            P.ts(cand.all(), lo_n.all(), 0.5 ** (it + 2), ALU.add)
    thr = lo[NIT % 2]
    maskA = P.sb("r_maskA", [128, 256], F32)
    P.ts(maskA.all(), A.all(), thr[:, 0:1], ALU.is_ge)
    pm = PB[2]
    for half in range(2):
        P.transpose(pm[:, half * 128:(half + 1) * 128], maskA[:, half * 128:(half + 1) * 128], G.ident)
    P.copy(mask_tok.all().v(lambda a: a.rearrange("p i e -> p (i e)")), pm[:, 0:256], eng="act")
    mb16 = P.sb("r_mb16", [128, 256], BF16)
    P.copy(mb16.all(), pm[:, 0:256], eng="dve")
    ust = P.sb("r_ust", [128, 128], BF16)
    one16 = P.sb("r_one16", [128, 128], BF16)
    P.tt(ust.all(), G.cst[:, C_UF:C_UF + 128], G.ident, ALU.subtract)
    P.memset(one16.all(), 1.0)
    pw_, pt_ = PB[3], PB[7]
    P.mm(pw_[:, 0:256], ust.all(), mb16.all())
    P.mm(pt_[:, 0:256], one16.all(), mb16.all())
    ca = P.sb("r_ca", [128, 256], F32)
    cb_ = P.sb("r_cb", [128, 256], F32)
    P.copy(ca.all(), pt_[:, 0:256], eng="act")
    src_, dst_ = ca, cb_
    for sft in (16, 32, 64, 128):
        P.copy(dst_[:, 0:sft], src_[:, 0:sft], eng="pool")
        P.tt(dst_[:, sft:256], src_[:, sft:256], src_[:, 0:256 - sft], ALU.add)
        src_, dst_ = dst_, src_
    pflat = pos_tok.all().v(lambda a: a.rearrange("p i e -> p (i e)"))
    P.copy(Acc(pflat.ap[:, 0:16], pflat.cells), pw_[:, 0:16], eng="act")
    P.tt(Acc(pflat.ap[:, 16:256], pflat.cells), pw_[:, 16:256], src_[:, 0:240], ALU.add)
    P.stt(gm_tok.all().v(lambda a: a.rearrange("p i e -> p (i e)")), aff.all(), 1.0 / ALPHA, pm[:, 0:256], ALU.mult, ALU.mult)
    P.stt(pflat, pflat, 1.0, mask_tok.all().v(lambda a: a.rearrange("p i e -> p (i e)")), ALU.add, ALU.mult)
    P.ts(pflat, pflat, -1.0, ALU.add)
    P.release(m1)
    if upto < 2:
        P.release(m0)
        return

    Wb = [[P.sb(f"w1b{i}", [128, NK, FB], F16, blk=FB), P.sb(f"w3b{i}", [128, NK, FB], F16, blk=FB),
           P.sb(f"w2b{i}", [128, FB // 128, D], F16, blk=D)] for i in range(2)]
    Se = P.sb("Se", [128, NT, 2 * CAP], F16, blk=CAP)
    S2 = P.sb("S2", [128, NT, CAP], F16, blk=CAP)
    SeT = P.sb("SeT", [128, 2, 2, L], F16, blk=512)
    xsT = P.sb("xsT", [128, NK, 2 * CAP], F16, blk=CAP)
    hT = [P.sb(f"hT{i}", [128, CAP], F16) for i in range(4)]
    sl = [P.sb(f"sl{i}", [128, CAP], F16) for i in range(4)]
    ysb = P.sb("ysb", [128, 2, 2, D], F16, blk=512)
    iota = G.cst[:, C_IOTA:C_IOTA + CAP]
    w1d, w3d, w2d = d["moe_w1"], d["moe_w3"], d["moe_w2"]
    SUB, NEX = 9, NE
    G.nblk = 0
    for e in range(NEX):
        ej = e % 2
        cofs = ej * CAP
        if ej == 0:
            for i in range(NT):
                for jj in range(2):
                    P.ts(Se[:, i, jj * CAP:(jj + 1) * CAP], iota, pos_tok[:, i, e + jj:e + jj + 1], ALU.is_equal)
            for k in range(NK):
                pb = PB[4 + (k % 2)]
                for i in range(NT):
                    P.mm(pb.all(), x16[:, i, k * 128:(k + 1) * 128], Se[:, i, :], start=(i == 0), stop=(i == NT - 1))
                P.copy(xsT[:, k, :], pb.all(), eng=evac_eng(G))
        trp = [PB[6], PB[7]]
        for i in range(NT):
            s2 = S2
            P.ts(s2[:, i, :], Se[:, i, cofs:cofs + CAP], gm_tok[:, i, e:e + 1], ALU.mult)
            for cc in range(2):
                pbv = trp[cc].all().v(lambda a: a.bitcast(F16))
                col = (i % 8) * 128
                dst = Acc(pbv.ap[:, col:col + 128], pbv.cells)
                P.transpose(dst, s2[:, i, cc * 128:(cc + 1) * 128], G.id16.all())
            if i % 8 == 7:
                for cc in range(2):
                    pbv = trp[cc].all().v(lambda a: a.bitcast(F16))
                    h0 = (i // 8) * 1024
                    P.copy(SeT[:, ej, cc, h0:h0 + 1024], pbv, eng=("act" if cc else "dve"))
        if SUB < 2:
            continue
        yps = [PB[0], PB[1], PB[2], PB[3]]
        NFC = FF // 128
        blocks = {}

        def h_stage(fc):
            fb, jj = divmod(fc, FB // 128)
            if jj == 0:
                W1, W3, W2 = Wb[G.nblk % 2]
                G.nblk += 1
                f0 = fb * FB
                P.dma(W1.all(), w1d[l, e].rearrange("(k p) f -> p k f", p=128)[:, :, f0:f0 + FB], q="pool")
                P.dma(W3.all(), w3d[l, e].rearrange("(k p) f -> p k f", p=128)[:, :, f0:f0 + FB], q="pool")
                P.dma(W2.all(), w2d[l, e, f0:f0 + FB, :].rearrange("(j p) n -> p j n", p=128), q="pool")
                blocks[fb] = (W1, W3, W2)
            W1, W3, W2 = blocks[fb]
            pb = PB[4 + (fc % 4)]
            for k in range(NK):
                P.mm(pb[:, 0:CAP], W1[:, k, jj * 128:(jj + 1) * 128], xsT[:, k, cofs:cofs + CAP], start=(k == 0), stop=(k == NK - 1))
            for k in range(NK):
                P.mm(pb[:, CAP:2 * CAP], W3[:, k, jj * 128:(jj + 1) * 128], xsT[:, k, cofs:cofs + CAP], start=(k == 0), stop=(k == NK - 1))
            s_, h_ = sl[fc % 4], hT[fc % 4]
            P.act(s_.all(), pb[:, 0:CAP], AF.Silu)
            P.tt(h_.all(), s_.all(), pb[:, CAP:2 * CAP], ALU.mult)

        def w2_stage(fc):
            fb, jj = divmod(fc, FB // 128)
            W2 = blocks[fb][2]
            h_ = hT[fc % 4]
            for cc in range(2):
                for dh in range(2):
                    P.mm(yps[cc * 2 + dh].all(), h_[:, cc * 128:(cc + 1) * 128], W2[:, jj, dh * 512:(dh + 1) * 512],
                         start=(fc == 0), stop=(fc == NFC - 1))

        h_stage(0)
        h_stage(1)
        for fc in range(NFC):
            if fc + 2 < NFC:
                h_stage(fc + 2)
            w2_stage(fc)
        for cc in range(2):
            for dh in range(2):
                P.copy(ysb[:, ej, cc, dh * 512:(dh + 1) * 512], yps[cc * 2 + dh].all(), eng=evac_eng(G))
        if SUB < 3 or ej == 0:
            continue
        for k in range(NK):
            for tb in range(4):
                pb = PB[(k * 4 + tb) % 4]
                n = 0
                for e2 in range(2):
                    for cc in range(2):
                        P.mm(pb.all(), ysb[:, e2, cc, k * 128:(k + 1) * 128], SeT[:, e2, cc, tb * 512:(tb + 1) * 512],
                             start=(n == 0), stop=(n == 3))
                        n += 1
                xs_ = X[:, k, tb * 512:(tb + 1) * 512]
                P.tt(xs_.r(), pb.all(), xs_, ALU.add)
    P.release(m0)
    if upto < 3:
        return
    layer_norm(G, d["ln2_g"][l], d["ln2_b"][l], eps=LN_EPS / (ALPHA * ALPHA))


U32 = mybir.dt.uint32


def wcols(wd, c0, n):
    return wd.rearrange("(k p) c -> p k c", p=128)[:, :, c0:c0 + n]


def out_proj_ln1(G, wo_d, yT, g_d, b_d):
    P, X, PB = G.P, G.X, G.PB
    m = P.mark()
    wst = [P.sb(f"wo_st{i}", [128, NK, 128], F32, blk=128) for i in range(2)]
    w16 = [P.sb(f"wo_16{i}", [128, NK, 128], F16, blk=128) for i in range(2)]
    for kd in range(NK):
        st, wh = wst[kd % 2], w16[kd % 2]
        P.dma(st.all(), wo_d.rearrange("(k p) c -> p k c", p=128)[:, :, kd * 128:(kd + 1) * 128])
        P.copy(wh.all(), st.all(), eng="act")
        for tb in range(4):
            pb = PB[(kd * 4 + tb) % 2]
            for kc in range(NK):
                P.mm(pb.all(), wh[:, kc, :], yT[:, kc, tb * 512:(tb + 1) * 512], start=(kc == 0), stop=(kc == NK - 1))
            xs_ = X[:, kd, tb * 512:(tb + 1) * 512]
            P.stt(xs_.r(), xs_, ALPHA, pb.all(), ALU.mult, ALU.add)
    P.release(m)
    layer_norm(G, g_d, b_d)


POOL_W = (2, 4, 8, 16)
HQ = 128
NCH = L // HQ


def mixer_cd(G, l):
    P, X, PB, d = G.P, G.X, G.PB, G.d
    j = l // 2
    wd = d["w_in_cd"][j]
    m0 = P.mark()
    yT = P.sb("yT", [128, NK, L], F16, blk=128)
    wbuf = [P.sb(f"wcd{i}", [128, NK, 128], F32, blk=128) for i in range(3)]
    G.wrr = 0

    def load_w(c0):
        w = wbuf[G.wrr % 3]
        G.wrr += 1
        P.dma(w.all().r(), wcols(wd, c0, 128))
        return w

    def proj_fm(w, tb, pb):
        for k in range(NK):
            P.mm(pb.all(), w[:, k, :].r(), X[:, k, tb * 512:(tb + 1) * 512].r(), start=(k == 0), stop=(k == NK - 1))

    m1 = P.mark()
    HB = 16
    ub = P.sb("pl_u", [128, L + 2 * HB], F32, blk=L + 2 * HB)
    pa = P.sb("pl_a", [128, L + 2 * HB], F32, blk=L + 2 * HB)
    pbuf = P.sb("pl_b", [128, L + 2 * HB], F32, blk=L + 2 * HB)
    pooled = P.sb("pl_p", [128, L], F32, blk=512)
    pw = P.sb("pl_w", [128, 4, 128], F32, blk=128)
    psc = P.sb("pl_sc", [128, 4], F32)
    P.dma(pw.all().r(), d["pool_w"][j].rearrange("g c d -> c g d"))
    P.dma(psc.all(), d["pool_scale"][j].rearrange("(g p) -> p g", p=128), slow=True)
    for t_ in (ub, pa, pbuf):
        P.memset(t_.all(), 0.0, eng="pool")
    for gi, w_ in enumerate(POOL_W):
        wt = load_w(gi * 128)
        for tb in range(4):
            pb = PB[tb % 2]
            proj_fm(wt, tb, pb)
            P.copy(ub[:, HB + tb * 512:HB + (tb + 1) * 512], pb.all(), eng=evac_eng(G))
        src = ub
        bufs = [pa, pbuf]
        lo, hi = HB - 8, HB + L + 8
        for s in range(gi + 1):
            dst = bufs[s % 2]
            if s == 0:
                P.tt(dst[:, lo:hi], src[:, lo - 1:hi - 1], src[:, lo:hi], ALU.add)
            else:
                sh = 1 << (s - 1)
                P.tt(dst[:, lo:hi], src[:, lo - sh:hi - sh], src[:, lo + sh:hi + sh], ALU.add)
            src = dst
        P.stt(pooled.all().r(), src[:, HB:HB + L], 1.0 / w_, ub[:, HB:HB + L], ALU.mult, ALU.subtract)
        for t in list(range(0, w_ // 2)) + list(range(L - w_ // 2 + 1, L)):
            cnt = min(t - w_ // 2 + w_, L) - max(t - w_ // 2, 0)
            P.stt(pooled[:, t:t + 1].r(), src[:, HB + t:HB + t + 1], 1.0 / cnt, ub[:, HB + t:HB + t + 1], ALU.mult, ALU.subtract)
        for tb in range(4):
            pb = PB[2 + tb % 2]
            P.mm(pb.all(), pw[:, gi, :].r(), pooled[:, tb * 512:(tb + 1) * 512].r())
            P.act(yT[:, gi, tb * 512:(tb + 1) * 512], pb.all(), AF.Identity, scale=psc[:, gi:gi + 1])
    P.release(m1)

    m1 = P.mark()
    lbl = P.sb("lb_l", [128, 4, 4], F32)
    P.dma(lbl.all(), d["hgrn_lb_logits"].rearrange("l (h p) -> p l h", p=128), slow=True)
    lbv = P.sb("lb_v", [128, 4], F32)
    oml = P.sb("lb_om", [128, 4], F32)
    lsum = P.sb("lb_s", [128, 4], F32)
    P.act(lbl.all(), lbl.all(), AF.Exp)
    P.reduce(lsum.all(), lbl.all().v(lambda a: a.rearrange("p l h -> p h l")), ALU.add)
    P.recip(lsum.all(), lsum.all())
    P.copy(lbv.all(), lbl[:, 1, :])
    for ll in range(2, l + 1):
        P.tt(lbv.all(), lbv.all(), lbl[:, ll, :], ALU.add)
    P.tt(lbv.all(), lbv.all(), lsum.all(), ALU.mult)
    P.ts(oml.all(), lbv.all(), -1.0, ALU.mult, 1.0, ALU.add)
    rmask = P.sb("hg_rm", [128, L], BF16, blk=L)
    P.memset(rmask.all(), 1.0, eng="pool")
    P.memset(rmask.all().v(lambda a: a.rearrange("p (c q) -> p c q", q=HQ)[:, :, 0:1]), 0.0, eng="pool")
    nwp = P.sb("hg_nwp", [128, 4], F32)
    P.dma(nwp.all(), d["hgrn_norm_w"][j].rearrange("(h p) -> p h", p=128), slow=True)
    qT = P.sb("hg_qT", [128, L], F32, blk=512)
    fb_ = P.sb("hg_f", [128, L], F32, blk=L)
    lf = P.sb("hg_lf", [128, L], F32, blk=L)
    cum = P.sb("hg_cum", [128, L], F32, blk=L)
    qt_ = [P.sb(f"hg_qt{i}", [128, L], F32, blk=HQ) for i in range(2)]
    kt_ = [P.sb(f"hg_kt{i}", [128, L], F32, blk=HQ) for i in range(2)]
    esc = [[P.sb(f"hg_es{i}{n}", [128, NCH], F32) for n in range(3)] for i in range(2)]
    vtok = P.sb("hg_v", [128, NT, 128], F32, blk=128)
    sgT = P.sb("hg_sgT", [128, L], F16, blk=128)
    Sbst = P.sb("hg_Sb", [128, NCH, 128], F32, blk=128)
    Sst = [P.sb(f"hg_S{i}", [128, 128], F32) for i in range(2)]
    Stl = [P.sb(f"hg_St{i}", [128, 128], F32) for i in range(2)]
    ktok = [P.sb(f"hg_ktok{i}", [128, 128], F32) for i in range(2)]
    sT = [[P.sb(f"hg_sT{i}{n}", [128, 128], F32) for n in range(2)] for i in range(2)]
    sTraw = [[P.sb(f"hg_sTr{i}{n}", [128, 128], F32) for n in range(2)] for i in range(2)]
    for i in range(2):
        for n in range(2):
            P.memset(sTraw[i][n].all(), 0.0, eng="pool")
    ssq = P.sb("hg_ssq", [128, 2], F32)
    tot = P.sb("hg_tot", [128, NCH], F32)
    refc = P.sb("hg_refc", [128, NCH], F32)
    junk = P.sb("hg_junk", [128, 128], F32)
    ytk = [P.sb(f"hg_y{i}", [128, 128], F16) for i in range(2)]
    epsn = P.sb("hg_eps", [128, 1], F32)
    P.memset(epsn.all(), NORM_EPS, eng="pool")
    masks = [G.cst[:, C_UF:C_UF + 128].v(lambda a: a.bitcast(U32)), G.cst[:, C_UB:C_UB + 128].v(lambda a: a.bitcast(U32))]
    c3 = lambda a: a.rearrange("p (c q) -> p c q", q=HQ)

    for h in range(4):
        wq = load_w(512 + h * 128)
        for tb in range(4):
            pb = PB[tb % 2]
            proj_fm(wq, tb, pb)
            P.copy(qT[:, tb * 512:(tb + 1) * 512], pb.all(), eng=evac_eng(G))
        wv = load_w(2048 + h * 128)
        wg = load_w(2560 + h * 128)
        for tb in range(4):
            pb = PB[tb % 2]
            proj_fm(wv, tb, pb)
            P.copy(fb_[:, tb * 512:(tb + 1) * 512], pb.all(), eng=evac_eng(G))
            pg = PB[2 + tb % 2]
            proj_fm(wg, tb, pg)
            P.act(sgT[:, tb * 512:(tb + 1) * 512], pg.all(), AF.Sigmoid)
        P.ts(sgT.all(), sgT.all(), nwp[:, h:h + 1], ALU.mult, eng="pool")
        for i4 in range(4):
            pb = PB[4 + i4 % 2]
            for jj in range(4):
                i = i4 * 4 + jj
                P.transpose(pb[:, jj * 128:(jj + 1) * 128], fb_[:, i * 128:(i + 1) * 128], G.ident)
            P.copy(vtok[:, i4 * 4:(i4 + 1) * 4, :].r(), pb.all().v(lambda a: a.rearrange("p (i v) -> p i v", v=128)), eng=evac_eng(G))
        for di in range(2):
            wf = load_w(1024 + di * 512 + h * 128)
            for tb in range(4):
                pb = PB[tb % 2]
                proj_fm(wf, tb, pb)
                P.act(fb_[:, tb * 512:(tb + 1) * 512], pb.all(), AF.Sigmoid)
            P.ts(fb_.all(), fb_.all(), oml[:, h:h + 1], ALU.mult, lbv[:, h:h + 1], ALU.add)
            P.act(lf.all(), fb_.all(), AF.Ln)
            c_, r_, l_ = cum.all(), rmask.all(), lf.all()
            P.add("dve", lambda e, c_=c_, r_=r_, l_=l_: e.tensor_tensor_scan(c_.ap, r_.ap, l_.ap, 0.0, ALU.mult, ALU.add), [r_, l_], [c_])
            if di == 1:
                P.tt(lf.all(), lf.all(), cum.all(), ALU.subtract)
                P.copy(tot.all().v(lambda a: a.unsqueeze(2)), cum.all().v(lambda a: c3(a)[:, :, HQ - 1:HQ]))
                P.tt(cum.all().v(c3), lf.all().v(c3), tot.all().v(lambda a: a.unsqueeze(2).to_broadcast([128, NCH, HQ])), ALU.add)
            lastcol = (HQ - 1) if di == 0 else 0
            refv = cum.all().v(lambda a: c3(a)[:, :, HQ // 2:HQ // 2 + 1])
            lastv = cum.all().v(lambda a, lc=lastcol: c3(a)[:, :, lc:lc + 1])
            v2 = lambda a: a.unsqueeze(2)
            e_ref, e_last, e_lr = esc[di]
            P.act(e_ref.all().v(v2), refv, AF.Exp)
            P.act(e_last.all().v(v2), lastv, AF.Exp)
            P.tt(e_lr.all().v(v2), lastv, refv, ALU.subtract)
            P.act(e_lr.all(), e_lr.all(), AF.Exp)
            P.copy(refc.all().v(v2), refv)
            P.tt(lf.all().v(c3), cum.all().v(c3), refc.all().v(lambda a: a.unsqueeze(2).to_broadcast([128, NCH, HQ])), ALU.subtract)
            P.act(cum.all(), lf.all(), AF.Exp)
            P.tt(qt_[di].all().r(), qT.all(), cum.all(), ALU.mult, eng="pool")
            P.act(cum.all(), lf.all(), AF.Exp, scale=-1.0)
            P.ts(fb_.all(), fb_.all(), -1.0, ALU.mult, 1.0, ALU.add, eng="pool")
            P.tt(kt_[di].all().r(), fb_.all(), cum.all(), ALU.mult)

        def state_step(di, c, S):
            e_ref, e_last, e_lr = esc[di]
            kk = ktok[di]
            pt = PB[4 + di]
            P.transpose(pt[:, 0:128], kt_[di][:, c * HQ:(c + 1) * HQ], G.ident)
            P.copy(kk.all().r(), pt[:, 0:128], eng="act")
            pu = PB[6 + di]
            P.mm(pu[:, 0:128], kk.all().r(), vtok[:, c, :].r())
            P.ts(S.all(), S.all(), e_last[:, c:c + 1], ALU.mult, eng="pool")
            P.stt(S.all(), pu[:, 0:128], e_lr[:, c:c + 1], S.all(), ALU.mult, ALU.add)

        P.memset(Sst[1].all(), 0.0, eng="pool")
        for c in range(NCH - 1, -1, -1):
            P.ts(Sbst[:, c, :].r(), Sst[1].all(), esc[1][0][:, c:c + 1], ALU.mult)
            if c > 0:
                state_step(1, c, Sst[1])
        P.memset(Sst[0].all(), 0.0, eng="pool")
        for c in range(NCH):
            cs = slice(c * HQ, (c + 1) * HQ)
            stl = Stl[c % 2]
            P.ts(stl.all().r(), Sst[0].all(), esc[0][0][:, c:c + 1], ALU.mult, eng="pool")
            for di in range(2):
                ps_ = PB[di]
                P.mm(ps_[:, 0:128], kt_[di][:, cs].r(), qt_[di][:, cs].r())
                o_, m_, p_ = sTraw[di][c % 2].all(), masks[di], ps_[:, 0:128]
                P.add("dve", lambda e, o_=o_, m_=m_, p_=p_: e.copy_predicated(o_.ap, m_.ap, p_.ap), [m_, p_], [o_])
                P.copy(sT[di][c % 2].all().r(), o_, eng="act")
            po = PB[2 + c % 2]
            P.mm(po[:, 0:128], sT[0][c % 2].all().r(), vtok[:, c, :].r(), start=True, stop=False)
            P.mm(po[:, 0:128], sT[1][c % 2].all().r(), vtok[:, c, :].r(), start=False, stop=False)
            P.mm(po[:, 0:128], qt_[0][:, cs].r(), stl.all().r(), start=False, stop=False)
            P.mm(po[:, 0:128], qt_[1][:, cs].r(), Sbst[:, c, :].r(), start=False, stop=True)
            if c < NCH - 1:
                state_step(0, c, Sst[0])
            P.act(junk.all(), po[:, 0:128], AF.Square, accum_out=ssq[:, 0:1])
            P.act(ssq[:, 1:2], ssq[:, 0:1], AF.Sqrt, bias=epsn.all(), scale=1.0 / 128)
            P.recip(ssq[:, 1:2], ssq[:, 1:2])
            yk = ytk[c % 2]
            P.ts(yk.all(), po[:, 0:128], ssq[:, 1:2], ALU.mult)
            pt = PB[4 + 0].all().v(lambda a: a.bitcast(F16))
            ptc = Acc(pt.ap[:, 0:128], pt.cells)
            P.transpose(ptc, yk.all(), G.id16.all())
            P.tt(yT[:, 4 + h, cs], ptc, sgT[:, cs], ALU.mult)
    P.release(m1)
    out_proj_ln1(G, d["w_out_cd"][j], yT, d["ln1_g"][l], d["ln1_b"][l])
    P.release(m0)


def mixer_ab(G, l, do_attn=True, do_ssd=True):
    P, X, PB, d = G.P, G.X, G.PB, G.d
    j = l // 2
    wd = d["w_in_ab"][j]
    m0 = P.mark()
    yT = P.sb("yT", [128, NK, L], F16, blk=128)
    hd = lambda a: a.rearrange("p (h e) -> p h e", e=64)
    f16v = lambda pb: pb.all().v(lambda a: a.bitcast(F16))

    if do_attn:
        m1 = P.mark()
        qT = P.sb("at_qT", [128, 4, L], F16, blk=128)
        kT = P.sb("at_kT", [128, 2, 2, L], F16, blk=128)
        vaug = P.sb("at_v", [128, NT, 2, 66], F16, blk=132)
        P.memset(vaug.all(), 1.0, eng="pool")
        m2 = P.mark()
        Wq = P.sb("at_W", [128, NK, 768], F32, blk=768)
        P.dma(Wq.all().r(), wcols(wd, 1552, 768))
        wqk = P.sb("at_wqk", [128, 2, 64], F32)
        P.dma(wqk[:, 0, :], d["attn_q_norm"][j].partition_broadcast(128))
        P.dma(wqk[:, 1, :], d["attn_k_norm"][j].partition_broadcast(128))
        epsn = P.sb("at_eps", [128, 1], F32)
        P.memset(epsn.all(), NORM_EPS, eng="pool")
        sq_ = [P.sb(f"at_sq{i}", [128, 640], F32) for i in range(2)]
        ssq_ = [P.sb(f"at_ssq{i}", [128, 10], F32) for i in range(2)]
        qn_ = [P.sb(f"at_qn{i}", [128, 640], F32) for i in range(2)]
        ra_ = [P.sb(f"at_ra{i}", [128, 320], F32) for i in range(2)]
        rb_ = [P.sb(f"at_rb{i}", [128, 320], F32) for i in range(2)]
        ra2_ = [P.sb(f"at_rc{i}", [128, 320], F32) for i in range(2)]
        rb2_ = [P.sb(f"at_rd{i}", [128, 320], F32) for i in range(2)]
        qr = [P.sb(f"at_qr{i}", [128, 512], F32) for i in range(2)]
        kr = [P.sb(f"at_kr{i}", [128, 2, 2, 128], F32, blk=512) for i in range(2)]
        for t_ in kr:
            P.memset(t_.all(), 0.0, eng="pool")
        for i in range(NT):
            ts_ = slice(i * 128, (i + 1) * 128)
            sq, ssq, qn, ra, rb = sq_[i % 2], ssq_[i % 2], qn_[i % 2], ra_[i % 2], rb_[i % 2]
            pq, pk = PB[(2 * i) % 4], PB[(2 * i + 1) % 4]
            for k in range(NK):
                P.mm(pq.all(), X[:, k, ts_].r(), Wq[:, k, 0:512].r(), start=(k == 0), stop=(k == NK - 1))
            for k in range(NK):
                P.mm(pk[:, 0:256], X[:, k, ts_].r(), Wq[:, k, 512:768].r(), start=(k == 0), stop=(k == NK - 1))
            P.copy(vaug[:, i, :, 0:64], pk[:, 128:256].v(lambda a: a.rearrange("p (g e) -> p g e", e=64)), eng="dve")
            P.act(sq[:, 0:512], pq.all(), AF.Square)
            P.act(sq[:, 512:640], pk[:, 0:128], AF.Square)
            P.reduce(ssq.all(), sq.all().v(hd), ALU.add)
            P.act(ssq.all(), ssq.all(), AF.Sqrt, bias=epsn.all(), scale=1.0 / 64)
            P.recip(ssq.all(), ssq.all())
            P.tt(qn[:, 0:512].v(hd), pq.all().v(hd), ssq[:, 0:8].v(lambda a: a.unsqueeze(2).to_broadcast([128, 8, 64])), ALU.mult)
            P.tt(qn[:, 512:640].v(hd), pk[:, 0:128].v(hd), ssq[:, 8:10].v(lambda a: a.unsqueeze(2).to_broadcast([128, 2, 64])), ALU.mult)
            P.tt(qn[:, 0:512].v(hd), qn[:, 0:512].v(hd), wqk[:, 0, :].v(lambda a: a.unsqueeze(1).to_broadcast([128, 8, 64])), ALU.mult, eng="pool")
            P.tt(qn[:, 512:640].v(hd), qn[:, 512:640].v(hd), wqk[:, 1, :].v(lambda a: a.unsqueeze(1).to_broadcast([128, 2, 64])), ALU.mult, eng="pool")
            pr = lambda a: a.rearrange("p (h e two) -> p h e two", e=32, two=2)
            xe = qn.all().v(lambda a: pr(a)[:, :, :, 0])
            xo = qn.all().v(lambda a: pr(a)[:, :, :, 1])
            cs_ = G.cst[:, C_COS + i * 32:C_COS + (i + 1) * 32].v(lambda a: a.unsqueeze(1).to_broadcast([128, 10, 32]))
            sn_ = G.cst[:, C_SIN + i * 32:C_SIN + (i + 1) * 32].v(lambda a: a.unsqueeze(1).to_broadcast([128, 10, 32]))
            h3 = lambda a: a.rearrange("p (h e) -> p h e", e=32)
            qro, kro = qr[i % 2], kr[i % 2]
            P.tt(ra.all().v(h3), xe, cs_, ALU.mult)
            P.tt(rb.all().v(h3), xo, sn_, ALU.mult, eng="pool")
            oute_q = qro.all().v(lambda a: pr(a)[:, :, :, 0])
            outo_q = qro.all().v(lambda a: pr(a)[:, :, :, 1])
            P.tt(oute_q, ra[:, 0:256].v(h3), rb[:, 0:256].v(h3), ALU.subtract)
            for dup in range(2):
                oe = kro[:, :, dup, dup * 64:(dup + 1) * 64].v(lambda a: a.rearrange("p g (e two) -> p g e two", two=2)[:, :, :, 0])
                P.tt(oe, ra[:, 256:320].v(h3), rb[:, 256:320].v(h3), ALU.subtract, eng="pool")
            ra2, rb2 = ra2_[i % 2], rb2_[i % 2]
            P.tt(ra2.all().v(h3), xe, sn_, ALU.mult)
            P.tt(rb2.all().v(h3), xo, cs_, ALU.mult, eng="pool")
            P.tt(outo_q, ra2[:, 0:256].v(h3), rb2[:, 0:256].v(h3), ALU.add)
            for dup in range(2):
                oo = kro[:, :, dup, dup * 64:(dup + 1) * 64].v(lambda a: a.rearrange("p g (e two) -> p g e two", two=2)[:, :, :, 1])
                P.tt(oo, ra2[:, 256:320].v(h3), rb2[:, 256:320].v(h3), ALU.add, eng="pool")
            pt = PB[4 + i % 2]
            for hp in range(4):
                P.transpose(pt[:, hp * 128:(hp + 1) * 128], qro[:, hp * 128:(hp + 1) * 128], G.ident)
            P.copy(qT[:, :, ts_], pt.all().v(lambda a: a.rearrange("p (h t) -> p h t", t=128)), eng="act")
            pt2 = PB[6 + i % 2]
            for g in range(2):
                for v in range(2):
                    P.transpose(pt2[:, (g * 2 + v) * 128:(g * 2 + v + 1) * 128], kro[:, g, v, :], G.ident)
            P.copy(kT[:, :, :, ts_], pt2.all().v(lambda a: a.rearrange("p (g v t) -> p g v t", g=2, v=2)), eng="act")
        P.release(m2)
        PT = [P.sb(f"at_PT{i}", [128, 512], BF16) for i in range(6)]
        OTs = [P.sb(f"at_OT{i}", [66, 512], F32) for i in range(2)]
        rsb = [P.sb(f"at_rs{i}", [128, 512], F32) for i in range(2)]
        sel = P.sb("at_sel", [66, 3, 128], F32, blk=128)
        P.copy(sel[:, 0, :].r(), G.cst[0:66, C_ID:C_ID + 128])
        P.ts(sel[:, 1, 0:64].r(), G.ones[0:66, 0:64], 0.0, ALU.mult)
        P.copy(sel[:, 1, 64:128].r(), G.cst[0:66, C_ID:C_ID + 64])
        P.ts(sel[:, 2, :].r(), G.cst[0:66, C_ID + 64:C_ID + 65].bc([66, 128]), 1.0, ALU.mult)
        G.npt = 0
        nit = 0
        for h in range(8):
            g, hp, hf = h // 4, h // 2, h % 2
            prt = slice(hf * 64, (hf + 1) * 64)
            for qb in range(4):
                qs = slice(qb * 512, (qb + 1) * 512)
                po = PB[2 + nit % 2]
                pts = {}

                def score(kt):
                    ps = PB[(0, 1, 6, 7)[kt % 4]]
                    P.mm(ps.all(), kT[:, g, hf, kt * 128:(kt + 1) * 128], qT[:, hp, qs])
                    pt_ = PT[G.npt % 6]
                    G.npt += 1
                    P.act(pt_.all(), ps.all(), AF.Exp, scale=0.125)
                    pts[kt] = pt_

                LA = 3
                for kt in range(LA):
                    score(kt)
                for kt in range(NT):
                    if kt + LA < NT:
                        score(kt + LA)
                    P.mm(po[0:66, :], vaug[:, kt, g, :], pts[kt].all(), start=(kt == 0), stop=(kt == NT - 1))
                ot = OTs[nit % 2]
                P.copy(ot.all().r(), po[0:66, :], eng="dve")
                pso, pss = PB[4], PB[5]
                P.mm(pso.all(), sel[:, hf, :].r(), ot.all().r())
                P.mm(pss.all(), sel[:, 2, :].r(), ot.all().r())
                r_ = rsb[nit % 2]
                P.recip(r_[prt, :], pss[prt, :])
                P.tt(yT[prt, 4 + hp, qs], pso[prt, :], r_[prt, :], ALU.mult)
                nit += 1
        P.release(m1)
    else:
        for k in range(4, 8):
            P.memset(yT[:, k, :], 0.0, eng="pool")

    if do_ssd:
        m1 = P.mark()
        xtok = P.sb("sd_x", [128, NT, 512], F16, blk=512)
        BT = P.sb("sd_BT", [128, 2, L], F16, blk=128)
        CT = P.sb("sd_CT", [128, 2, L], F16, blk=128)
        Btok = P.sb("sd_Bt", [128, NT, 256], F16, blk=256)
        sc = lambda nm: P.sb(nm, [128, NT, 16], F32, blk=256)
        dt_, lndt, la, cumb_a, tot, A2, wgt, ea, dec = [sc(n) for n in ("sd_dt", "sd_lndt", "sd_la", "sd_A", "sd_tot", "sd_A2", "sd_w", "sd_ea", "sd_dec")]
        f2 = lambda a: a.rearrange("p i e -> p (i e)")
        m2 = P.mark()
        wdt = P.sb("sd_wdt", [128, NK, 16], F32)
        P.dma(wdt.all(), d["w_dt"][j].rearrange("(k p) c -> p k c", p=128))
        bb = P.sb("sd_bias", [128, 16], F32)
        P.dma(bb.all(), d["ssm_dt_bias"][j].rearrange("a h -> (a h)").partition_broadcast(128))
        ab = P.sb("sd_alog", [128, 16], F32)
        P.dma(ab.all(), d["ssm_a_log"][j].rearrange("a h -> (a h)").partition_broadcast(128))
        P.act(ab.all(), ab.all(), AF.Exp)
        pd = PB[0]
        for i in range(NT):
            for k in range(NK):
                P.mm(pd[:, i * 16:(i + 1) * 16], X[:, k, i * 128:(i + 1) * 128], wdt[:, k, :], start=(k == 0), stop=(k == NK - 1))
        b3 = lambda t_: t_.all().v(lambda a: a.unsqueeze(1).to_broadcast([128, NT, 16]))
        P.tt(dt_.all(), pd[:, 0:256].v(lambda a: a.rearrange("p (i e) -> p i e", e=16)), b3(bb), ALU.add)
        P.act(dt_.all(), dt_.all(), AF.Exp)
        P.act(dt_.all(), dt_.all(), AF.Ln, bias=G.ones[:, 0:1], scale=1.0)
        P.act(lndt.all(), dt_.all(), AF.Ln)
        P.stt(la.all(), dt_.all(), -1.0, b3(ab), ALU.mult, ALU.mult)
        pc = PB[1]
        for i in range(NT):
            P.mm(pc[:, i * 32:i * 32 + 8], G.cst[:, C_UF:C_UF + 128], la[:, i, 0:8])
            P.mm(pc[:, i * 32 + 8:i * 32 + 16], G.cst[:, C_UB:C_UB + 128], la[:, i, 8:16])
            P.mm(pc[:, i * 32 + 16:i * 32 + 32], G.ones.all(), la[:, i, :])
        pc3 = pc.all().v(lambda a: a.rearrange("p (i e) -> p i e", e=32))
        P.copy(cumb_a.all(), Acc(pc3.ap[:, :, 0:16], pc3.cells), eng="act")
        P.copy(tot.all(), Acc(pc3.ap[:, :, 16:32], pc3.cells), eng="dve")
        P.tt(A2.all(), cumb_a.all(), lndt.all(), ALU.subtract)
        P.tt(wgt.all(), tot.all(), A2.all(), ALU.subtract)
        P.act(wgt.all(), wgt.all(), AF.Exp)
        P.act(ea.all(), cumb_a.all(), AF.Exp)
        P.act(dec.all(), tot.all(), AF.Exp)
        wbuf = [P.sb(f"sd_w{i}", [128, NK, 128], F32, blk=128) for i in range(2)]
        cw = P.sb("sd_cw", [128, 5, 8], F32)
        cbi = P.sb("sd_cb", [128, 8], F32)
        for kk in range(5):
            P.dma(cw[:, kk, :], d["ssm_conv_w"][j, kk].rearrange("(c p) -> p c", p=128), slow=True)
        P.dma(cbi.all(), d["ssm_conv_b"][j].rearrange("(c p) -> p c", p=128), slow=True)
        ubuf = P.sb("sd_u", [128, L + 4], F32, blk=L + 4)
        so = P.sb("sd_so", [128, L], F16, blk=128)
        dg = [P.sb(f"sd_dg{i}", [128, 5, 128], F32, blk=128) for i in range(2)]
        P.ts(ubuf.all().r(), G.ones[:, 0:1].bc([128, L + 4]), 0.0, ALU.mult)
        for ch in range(8):
            w = wbuf[ch % 2]
            P.dma(w.all().r(), wcols(wd, 512 + ch * 128, 128))
            for tb in range(4):
                pb = PB[2 + tb % 2]
                for k in range(NK):
                    P.mm(pb.all(), w[:, k, :].r(), X[:, k, tb * 512:(tb + 1) * 512].r(), start=(k == 0), stop=(k == NK - 1))
                P.copy(ubuf[:, 2 + tb * 512:2 + (tb + 1) * 512].r(), pb.all(), eng=evac_eng(G))
            dgc = dg[ch % 2]
            for kk in range(5):
                P.ts(dgc[:, kk, :].r(), G.ident, cw[:, kk, ch:ch + 1], ALU.mult)
            for tb in range(4):
                pc_ = PB[tb % 2]
                for kk in range(5):
                    P.mm(pc_.all(), dgc[:, kk, :].r(), ubuf[:, tb * 512 + kk:tb * 512 + kk + 512].r(), start=(kk == 0), stop=(kk == 4))
                tsl = slice(tb * 512, (tb + 1) * 512)
                if ch < 4:
                    dst = so[:, tsl]
                elif ch < 6:
                    dst = BT[:, ch - 4, tsl]
                else:
                    dst = CT[:, ch - 6, tsl]
                P.act(dst, pc_.all(), AF.Silu, bias=cbi[:, ch:ch + 1])
            if ch < 4:
                for i8 in range(2):
                    pb = PB[4 + i8]
                    pv = f16v(pb)
                    for i in range(8):
                        ti = i8 * 8 + i
                        P.transpose(Acc(pv.ap[:, i * 128:(i + 1) * 128], pv.cells), so[:, ti * 128:(ti + 1) * 128], G.id16.all())
                    P.copy(xtok[:, i8 * 8:(i8 + 1) * 8, ch * 128:(ch + 1) * 128], Acc(pv.ap.rearrange("p (i c) -> p i c", c=128), pv.cells), eng=evac_eng(G))
            elif ch < 6:
                g = ch - 4
                for i8 in range(2):
                    pb = PB[6 + i8]
                    pv = f16v(pb)
                    for i in range(8):
                        ti = i8 * 8 + i
                        P.transpose(Acc(pv.ap[:, i * 128:(i + 1) * 128], pv.cells), BT[:, g, ti * 128:(ti + 1) * 128], G.id16.all())
                    P.copy(Btok[:, i8 * 8:(i8 + 1) * 8, g * 128:(g + 1) * 128], Acc(pv.ap.rearrange("p (i c) -> p i c", c=128), pv.cells), eng=evac_eng(G))
        P.release(m2)
        Wz = P.sb("sd_Wz", [128, NK, 512], F32, blk=512)
        P.dma(Wz.all().r(), wcols(wd, 0, 512))
        Sbs = P.sb("sd_Sbs", [128, NT, 512], F16, blk=512)
        S32 = [P.sb(f"sd_S32{i}", [128, 512], F32) for i in range(2)]
        S16 = [P.sb(f"sd_S16{i}", [128, 512], F16) for i in range(2)]
        xw = [P.sb("sd_xw", [128, 512], F16)] * 2
        dsk = P.sb("sd_dsk", [128, 8], F32)
        P.dma(dsk.all(), d["ssm_d"][j].partition_broadcast(128))
        nwb = P.sb("sd_nw", [128, 512], F32)
        P.dma(nwb.all(), d["ssm_norm_w"][j].partition_broadcast(128))
        epsn = P.sb("sd_eps", [128, 1], F32)
        P.memset(epsn.all(), NORM_EPS, eng="pool")
        rhsb = [P.sb("sd_rhs", [128, 8, 128], F32, blk=1024)] * 2
        E = [P.sb(f"sd_E{i}", [128, 8, 128], F32, blk=128) for i in range(2)]
        Tt = [P.sb(f"sd_T{i}", [128, 128], F32) for i in range(2)]
        MT = [P.sb(f"sd_MT{i}", [128, 128], F16) for i in range(3)]
        t1 = P.sb("sd_t1", [128, 512], F32)
        t2 = P.sb("sd_t2", [128, 512], F32)
        sz = P.sb("sd_sz", [128, 512], F16)
        yk = P.sb("sd_yk", [128, 512], F16)
        ssq = P.sb("sd_ssq", [128, 2], F32)
        hb = lambda t_, c, d0: t_[:, c, d0:d0 + 8].v(lambda a: a.unsqueeze(2).to_broadcast([128, 8, 64]))

        def state_update(di, c):
            xw_ = xw[di]
            P.tt(xw_.all().v(hd), xtok[:, c, :].v(hd), hb(wgt, c, di * 8), ALU.mult, eng="pool")
            pst = PB[7]
            for g in range(2):
                P.mm(pst[:, g * 256:(g + 1) * 256], Btok[:, c, g * 128:(g + 1) * 128], xw_[:, g * 256:(g + 1) * 256])
            P.tt(S32[di].all().v(hd), S32[di].all().v(hd), hb(dec, c, di * 8), ALU.mult, eng="pool")
            P.tt(S32[di].all(), S32[di].all(), pst.all(), ALU.add)

        P.memset(S32[1].all(), 0.0, eng="pool")
        for c in range(NT - 1, -1, -1):
            P.copy(Sbs[:, c, :], S32[1].all(), eng="act")
            if c > 0:
                state_update(1, c)
        P.memset(S32[0].all(), 0.0, eng="pool")
        G.nmt = 0

        def front(c):
            cs = slice(c * 128, (c + 1) * 128)
            pyd = PB[3] if c % 2 == 0 else PB[6]
            for di in range(2):
                U = G.cst[:, (C_UF if di == 0 else C_UB):(C_UF if di == 0 else C_UB) + 128]
                nm = G.cst[:, (C_NMF if di == 0 else C_NMB):(C_NMF if di == 0 else C_NMB) + 128]
                rh = rhsb[di]
                P.tt(rh.all(), U.v(lambda a: a.unsqueeze(1).to_broadcast([128, 8, 128])),
                     la[:, c, di * 8:di * 8 + 8].v(lambda a: a.unsqueeze(2).to_broadcast([128, 8, 128])), ALU.mult,
                     eng=("dve" if di == 0 else "pool"))
                for hh in range(2):
                    P.mm(PB[hh].all(), G.ones.all(), rh[:, hh * 4:(hh + 1) * 4, :].v(lambda a: a.rearrange("p h i -> p (h i)")))
                for h in range(8):
                    tt_ = Tt[h % 2]
                    P.stt(tt_.all(), PB[h // 4][:, (h % 4) * 128:(h % 4 + 1) * 128], A2[:, c, di * 8 + h:di * 8 + h + 1], nm, ALU.subtract, ALU.add)
                    P.act(E[di][:, h, :], tt_.all(), AF.Exp)
            for g in range(2):
                P.mm(PB[2][:, g * 128:(g + 1) * 128], BT[:, g, cs], CT[:, g, cs])
            P.tt(E[0].all(), E[0].all(), E[1].all(), ALU.add, eng="pool")
            for h in range(8):
                mt = MT[G.nmt % 3]
                G.nmt += 1
                P.tt(mt.all(), PB[2][:, (h // 4) * 128:(h // 4 + 1) * 128], E[0][:, h, :], ALU.mult)
                P.mm(pyd[:, h * 64:(h + 1) * 64], mt.all(), xtok[:, c, h * 64:(h + 1) * 64])

        def back(c):
            cs = slice(c * 128, (c + 1) * 128)
            pyd = PB[3] if c % 2 == 0 else PB[6]
            P.copy(S16[0].all(), S32[0].all(), eng="act")
            for g in range(2):
                P.mm(PB[4][:, g * 256:(g + 1) * 256], CT[:, g, cs], S16[0][:, g * 256:(g + 1) * 256])
                P.mm(PB[5][:, g * 256:(g + 1) * 256], CT[:, g, cs], Sbs[:, c, g * 256:(g + 1) * 256])
            if c < NT - 1:
                state_update(0, c)
            pz = PB[0]
            for k in range(NK):
                P.mm(pz.all(), X[:, k, cs].r(), Wz[:, k, :].r(), start=(k == 0), stop=(k == NK - 1))
            P.act(sz.all(), pz.all(), AF.Silu)
            P.tt(t1.all().v(hd), PB[4].all().v(hd), hb(ea, c, 0), ALU.mult)
            P.tt(t2.all().v(hd), PB[5].all().v(hd), hb(ea, c, 8), ALU.mult)
            P.tt(t1.all(), t1.all(), t2.all(), ALU.add, eng="pool")
            P.tt(t2.all().v(hd), xtok[:, c, :].v(hd), dsk.all().v(lambda a: a.unsqueeze(2).to_broadcast([128, 8, 64])), ALU.mult, eng="pool")
            P.tt(t1.all(), t1.all(), t2.all(), ALU.add, eng="pool")
            P.tt(t1.all(), pyd.all(), t1.all(), ALU.add)
            P.tt(t1.all(), t1.all(), sz.all(), ALU.mult)
            P.act(t2.all(), t1.all(), AF.Square, accum_out=ssq[:, 0:1])
            P.act(ssq[:, 1:2], ssq[:, 0:1], AF.Sqrt, bias=epsn.all(), scale=1.0 / 512)
            P.recip(ssq[:, 1:2], ssq[:, 1:2])
            P.stt(yk.all(), t1.all(), ssq[:, 1:2], nwb.all(), ALU.mult, ALU.mult)
            pv = f16v(PB[7])
            for c4 in range(4):
                P.transpose(Acc(pv.ap[:, c4 * 128:(c4 + 1) * 128], pv.cells), yk[:, c4 * 128:(c4 + 1) * 128], G.id16.all())
            P.copy(yT[:, 0:4, cs], Acc(pv.ap[:, 0:512].rearrange("p (h t) -> p h t", t=128), pv.cells), eng="act")

        front(0)
        for c in range(NT):
            if c + 1 < NT:
                front(c + 1)
            back(c)
        P.release(m1)
    else:
        for k in range(4):
            P.memset(yT[:, k, :], 0.0, eng="pool")
    out_proj_ln1(G, d["w_out_ab"][j], yT, d["ln1_g"][l], d["ln1_b"][l])
    P.release(m0)


DEPTH = 4
_DRAM_SPECS = [
    ("consts", [128, C_END], F32), ("x", [L, D], F32),
    ("w_in_ab", [2, D, 2320], F32R), ("w_dt", [2, D, 16], F32), ("ssm_conv_w", [2, 5, 1024], F32), ("ssm_conv_b", [2, 1024], F32),
    ("ssm_dt_bias", [2, 2, 8], F32), ("ssm_a_log", [2, 2, 8], F32), ("ssm_d", [2, 8], F32), ("ssm_norm_w", [2, 512], F32),
    ("attn_q_norm", [2, 64], F32), ("attn_k_norm", [2, 64], F32), ("w_out_ab", [2, D, D], F32),
    ("w_in_cd", [2, D, 3072], F32R), ("pool_w", [2, 4, 128, 128], F32R), ("pool_scale", [2, 512], F32),
    ("hgrn_lb_logits", [4, 512], F32), ("hgrn_norm_w", [2, 512], F32), ("w_out_cd", [2, D, D], F32),
    ("router_w", [4, D, NE], F32), ("moe_w1", [4, NE, D, FF], F32), ("moe_w3", [4, NE, D, FF], F32), ("moe_w2", [4, NE, FF, D], F32),
    ("ln1_g", [4, D], F32), ("ln1_b", [4, D], F32), ("ln2_g", [4, D], F32), ("ln2_b", [4, D], F32),
]


def build_program(layers=range(DEPTH)):
    nc = bass.Bass("TRN2", target_bir_lowering=False, dynamic_dma_scratch_size=8192)
    nc.dge_precook = False
    dram = {}
    for name, shape, dt in _DRAM_SPECS:
        dram[name] = nc.dram_tensor(name, list(shape), dt, kind="ExternalInput").ap()
    out = nc.dram_tensor("out", [L, D], F32, kind="ExternalOutput").ap()
    P = Prog(nc)
    G = setup(P, nc, dram)
    load_x(G, dram["x"])
    for l in layers:
        if l % 2 == 0:
            mixer_ab(G, l)
        else:
            mixer_cd(G, l)
        moe_phase(G, l)
    store_x(G, out)
    P.emit()
    P.close()
    return nc


def kernel(**inputs):
    x = np.ascontiguousarray(np.asarray(inputs["x"], dtype=np.float32))
    nb = x.shape[0]
    shared = {"consts": make_consts()}
    for name, shape, dt in _DRAM_SPECS:
        if name in ("consts", "x", "w_dt"):
            continue
        shared[name] = np.ascontiguousarray(np.asarray(inputs[name], dtype=np.float32))
    shared["w_dt"] = np.ascontiguousarray(shared["w_in_ab"][:, :, 1536:1552])
    nc = build_program()
    in_maps = []
    for b in range(nb):
        m = dict(shared)
        m["x"] = x[b]
        in_maps.append(m)
    res = run_bass_kernel_spmd(nc, in_maps, core_ids=list(range(nb)))
    return np.stack([np.asarray(r["out"], dtype=np.float32) for r in res.results], axis=0)
```
